# Optimizing a Trainium2 kernel written in Bass

```python
import jax, jax.numpy as jnp
from jax import lax
import numpy as np

D_MODEL = 1024
BATCH = 2
SEQ = 8192
DEPTH = 4
DEC_BATCH = 128
DEC_SEQ = 8
PAST_LEN = 2048
PAGE_SIZE = 128

D_MIX = D_MODEL
CONV_W = 4
HEAD_DIM = 64
ATT_WIDTH = 3 * D_MIX // 8
ATT_HEADS = ATT_WIDTH // HEAD_DIM
DILATED = ((128, 1), (512, 4), (2048, 16))
MAX_WINDOW = 2048
ATT_BLK = 128
ATT_SCALE = HEAD_DIM ** -0.5
SSD_WIDTH = 3 * D_MIX // 8
SSD_HEAD_DIM = 64
SSD_HEADS = SSD_WIDTH // SSD_HEAD_DIM
SSD_GROUPS = 2
SSD_STATE = 128
SSD_CHUNK = 128
SSD_CONV_CH = SSD_WIDTH + 2 * SSD_GROUPS * SSD_STATE
LRU_WIDTH = D_MIX - ATT_WIDTH - SSD_WIDTH
LRU_BLOCKS = 4
LRU_BLOCK = LRU_WIDTH // LRU_BLOCKS
LRU_C = 8.0
D_FF = -(-8 * D_MODEL // (3 * 256)) * 256
N_IN = 2 * LRU_WIDTH + 3 * ATT_WIDTH + SSD_WIDTH + SSD_CONV_CH + SSD_HEADS
EPS = 1e-6

kernel_name = "hymba_lru_dilatedswa_ssd_decode_step"


def _rmsnorm(x, g):
    xf = x.astype(jnp.float32)
    y = xf * lax.rsqrt(jnp.mean(xf * xf, axis=-1, keepdims=True) + EPS)
    return (y * g.astype(jnp.float32)).astype(x.dtype)


def _split_cols(p):
    sizes = (LRU_WIDTH, LRU_WIDTH, ATT_WIDTH, ATT_WIDTH, ATT_WIDTH, SSD_WIDTH, SSD_CONV_CH, SSD_HEADS)
    offs = np.cumsum(sizes)[:-1].tolist()
    return jnp.split(p, offs, axis=-1)


def _causal_conv(x, buf, w, b):
    T = x.shape[1]
    xp = jnp.concatenate([buf, x], axis=1)
    y = b + xp[:, 0:T] * w[0]
    for j in range(1, CONV_W):
        y = y + xp[:, j:j + T] * w[j]
    return y, xp[:, -(CONV_W - 1):]


def _rg_lru(x, h0, wa, ba, wx, bx, lam):
    b, T, _ = x.shape
    xb = x.reshape(b, T, LRU_BLOCKS, LRU_BLOCK)
    r = jax.nn.sigmoid((jnp.einsum('btki,kij->btkj', xb, wa).reshape(b, T, LRU_WIDTH) + ba).astype(jnp.float32))
    i = jax.nn.sigmoid((jnp.einsum('btki,kij->btkj', xb, wx).reshape(b, T, LRU_WIDTH) + bx).astype(jnp.float32))
    log_a = -LRU_C * r * jax.nn.softplus(-lam.astype(jnp.float32))
    a = jnp.exp(log_a)
    u = jnp.sqrt(-jnp.expm1(2.0 * log_a)) * (i * x.astype(jnp.float32))
    u = u.at[:, 0].add(a[:, 0] * h0.astype(jnp.float32))

    def comb(left, right):
        return (left[0] * right[0], right[0] * left[1] + right[1])

    _, h = lax.associative_scan(comb, (a, u), axis=1)
    return h.astype(x.dtype), h[:, -1].astype(x.dtype)


def _dilated_branch_prompt(q, k, v, window, dil):
    b, S, H, Dh = q.shape
    span = window // dil
    L = S // dil
    n = b * dil

    def to_res(t):
        return t.reshape(b, L, dil, H, Dh).transpose(0, 2, 1, 3, 4).reshape(n, L, H, Dh)

    nb = -(-L // ATT_BLK)
    Lp = nb * ATT_BLK
    pad = ((0, 0), (0, Lp - L), (0, 0), (0, 0))
    qb, kb, vb = [jnp.pad(to_res(t), pad).reshape(n, nb, ATT_BLK, H, Dh) for t in (q, k, v)]
    zblk = ((0, 0), (1, 0), (0, 0), (0, 0), (0, 0))
    kk = jnp.concatenate([jnp.pad(kb, zblk)[:, :-1], kb], axis=2)
    vv = jnp.concatenate([jnp.pad(vb, zblk)[:, :-1], vb], axis=2)
    s = jnp.einsum('nbqhd,nbkhd->nbhqk', qb, kk, preferred_element_type=jnp.float32) * ATT_SCALE
    qi = jnp.arange(ATT_BLK)[:, None]
    ki = jnp.arange(2 * ATT_BLK)[None, :]
    dist = ATT_BLK + qi - ki
    kpos = (jnp.arange(nb)[:, None, None] - 1) * ATT_BLK + ki[None]
    mask = ((dist >= 0) & (dist <= span))[None] & (kpos >= 0)
    s = jnp.where(mask[None, :, None], s, -jnp.inf)
    lse = jax.nn.logsumexp(s, axis=-1)
    p = jnp.exp(s - lse[..., None])
    o = jnp.einsum('nbhqk,nbkhd->nbqhd', p.astype(vv.dtype), vv, preferred_element_type=jnp.float32)
    o = o.reshape(n, Lp, H, Dh)[:, :L].reshape(b, dil, L, H, Dh).transpose(0, 2, 1, 3, 4).reshape(b, S, H, Dh)
    lse = lse.transpose(0, 1, 3, 2).reshape(n, Lp, H)[:, :L].reshape(b, dil, L, H).transpose(0, 2, 1, 3).reshape(b, S, H)
    return o, lse


def _dilated_branch_decode(q, kc, vc, window, dil):
    T = q.shape[1]
    Lb = kc.shape[1] - T
    idx = Lb + jnp.arange(T)[:, None] - dil * jnp.arange(window // dil + 1)[None, :]
    valid = idx >= 0
    idx = jnp.maximum(idx, 0)
    kg = kc[:, idx]
    vg = vc[:, idx]
    s = jnp.einsum('bthd,btkhd->bthk', q, kg, preferred_element_type=jnp.float32) * ATT_SCALE
    s = jnp.where(valid[None, :, None, :], s, -jnp.inf)
    lse = jax.nn.logsumexp(s, axis=-1)
    p = jnp.exp(s - lse[..., None])
    o = jnp.einsum('bthk,btkhd->bthd', p.astype(vg.dtype), vg, preferred_element_type=jnp.float32)
    return o, lse


def _combine_branches(branches):
    o = jnp.stack([br[0] for br in branches])
    w = jax.nn.softmax(jnp.stack([br[1] for br in branches]), axis=0)
    return jnp.sum(w[..., None] * o, axis=0)


def _ssd(x, dt, A, Bm, Cm, h0, chunk):
    b, T, h, p = x.shape
    c = T // chunk
    hpg = SSD_HEADS // SSD_GROUPS
    Br = jnp.repeat(Bm.astype(jnp.float32), hpg, axis=2).reshape(b, c, chunk, h, SSD_STATE)
    Cr = jnp.repeat(Cm.astype(jnp.float32), hpg, axis=2).reshape(b, c, chunk, h, SSD_STATE)
    xr = (x.astype(jnp.float32) * dt[..., None]).reshape(b, c, chunk, h, p)
    acum = jnp.cumsum((dt * A).reshape(b, c, chunk, h), axis=2)
    causal = jnp.tril(jnp.ones((chunk, chunk), dtype=bool))[None, None, :, :, None]
    diff = acum[:, :, :, None, :] - acum[:, :, None, :, :]
    Lmat = jnp.exp(jnp.where(causal, diff, -jnp.inf))
    G = jnp.einsum('bclhn,bcshn->bclsh', Cr, Br) * Lmat
    y_diag = jnp.einsum('bclsh,bcshp->bclhp', G, xr)
    decay = jnp.exp(acum[:, :, -1:, :] - acum)
    st = jnp.einsum('bclhn,bclh,bclhp->bchpn', Br, decay, xr)
    tot = jnp.exp(acum[:, :, -1, :])

    def step(hc, inp):
        st_c, tot_c = inp
        return hc * tot_c[:, :, None, None] + st_c, hc

    hT, prev = lax.scan(step, h0.astype(jnp.float32), (st.transpose(1, 0, 2, 3, 4), tot.transpose(1, 0, 2)))
    prev = prev.transpose(1, 0, 2, 3, 4)
    y_off = jnp.einsum('bclhn,bchpn,bclh->bclhp', Cr, prev, jnp.exp(acum))
    return (y_diag + y_off).reshape(b, T, h, p), hT


def _layer(x, prm, st, is_prompt):
    (g_mix_in, g_mix_out, w_in, conv_a_w, conv_a_b, lru_wa, lru_ba, lru_wx, lru_bx, lru_lambda,
     conv_c_w, conv_c_b, dt_bias, a_log, d_skip, ssm_norm, w_out, g_ffn_in, g_ffn_out, w_gate_up, w_down) = prm
    lru_h0, lru_buf, k_buf, v_buf, ssd_h0, ssd_buf = st
    b, T, _ = x.shape
    h = _rmsnorm(x, g_mix_in)
    gate_a, x_a, q, k, v, z_c, xbc, dt_raw = _split_cols(h @ w_in)

    x_a, lru_buf_new = _causal_conv(x_a, lru_buf, conv_a_w, conv_a_b)
    h_a, lru_h_new = _rg_lru(x_a, lru_h0, lru_wa, lru_ba, lru_wx, lru_bx, lru_lambda)
    out_a = h_a * jax.nn.gelu(gate_a)

    q = q.reshape(b, T, ATT_HEADS, HEAD_DIM)
    k = k.reshape(b, T, ATT_HEADS, HEAD_DIM)
    v = v.reshape(b, T, ATT_HEADS, HEAD_DIM)
    if is_prompt:
        out_b = _combine_branches([_dilated_branch_prompt(q, k, v, w, d) for (w, d) in DILATED])
        keep = min(MAX_WINDOW, T)
        k_new, v_new = k[:, T - keep:], v[:, T - keep:]
    else:
        kc = jnp.concatenate([k_buf, k], axis=1)
        vc = jnp.concatenate([v_buf, v], axis=1)
        out_b = _combine_branches([_dilated_branch_decode(q, kc, vc, w, d) for (w, d) in DILATED])
        k_new, v_new = k, v
    out_b = out_b.reshape(b, T, ATT_WIDTH).astype(x.dtype)

    xbc, ssd_buf_new = _causal_conv(xbc, ssd_buf, conv_c_w, conv_c_b)
    xbc = jax.nn.silu(xbc)
    xs = xbc[..., :SSD_WIDTH].reshape(b, T, SSD_HEADS, SSD_HEAD_DIM)
    Bm = xbc[..., SSD_WIDTH:SSD_WIDTH + SSD_GROUPS * SSD_STATE].reshape(b, T, SSD_GROUPS, SSD_STATE)
    Cm = xbc[..., SSD_WIDTH + SSD_GROUPS * SSD_STATE:].reshape(b, T, SSD_GROUPS, SSD_STATE)
    dt = jax.nn.softplus(dt_raw.astype(jnp.float32) + dt_bias.astype(jnp.float32))
    A = -jnp.exp(a_log.astype(jnp.float32))
    chunk = SSD_CHUNK if is_prompt else T
    y_c, ssd_h_new = _ssd(xs, dt, A, Bm, Cm, ssd_h0, chunk)
    y_c = y_c + d_skip.astype(jnp.float32)[:, None] * xs.astype(jnp.float32)
    y_c = y_c.reshape(b, T, SSD_WIDTH) * jax.nn.silu(z_c.astype(jnp.float32))
    out_c = _rmsnorm(y_c, ssm_norm).astype(x.dtype)

    mix = jnp.concatenate([out_a, out_b, out_c], axis=-1) @ w_out
    x = x + _rmsnorm(mix, g_mix_out)

    hf = _rmsnorm(x, g_ffn_in)
    g_f, u_f = jnp.split(hf @ w_gate_up, 2, axis=-1)
    x = x + _rmsnorm((jax.nn.silu(g_f) * u_f) @ w_down, g_ffn_out)
    new_state = (lru_h_new, lru_buf_new, k_new, v_new, ssd_h_new.astype(x.dtype), ssd_buf_new)
    return x, new_state


def setup_inputs(seed: int = 0) -> dict:
    key = jax.random.key(seed)
    ks = iter(jax.random.split(key, 40))
    f32 = jnp.float32

    def nrm(shape, scale):
        return scale * jax.random.normal(next(ks), shape, f32)

    def gain(shape):
        return 1.0 + nrm(shape, 0.02)

    LB = min(MAX_WINDOW, PAST_LEN)
    a0 = jax.random.uniform(next(ks), (DEPTH, LRU_WIDTH), f32, minval=0.9, maxval=0.999)
    s0 = a0 ** (1.0 / LRU_C)
    lru_lambda = jnp.log(s0) - jnp.log1p(-s0)
    dt0 = jnp.exp(jax.random.uniform(next(ks), (DEPTH, SSD_HEADS), f32, minval=float(np.log(1e-3)), maxval=float(np.log(1e-1))))
    dt_bias = dt0 + jnp.log(-jnp.expm1(-dt0))
    a_log = jnp.log(jax.random.uniform(next(ks), (DEPTH, SSD_HEADS), f32, minval=1.0, maxval=16.0))
    return {
        "x_prompt": nrm((BATCH, SEQ, D_MODEL), 1.0),
        "x_sample": nrm((DEC_BATCH, DEC_SEQ, D_MODEL), 1.0),
        "state_lru_h": nrm((DEPTH, DEC_BATCH, LRU_WIDTH), 0.5),
        "state_lru_conv": nrm((DEPTH, DEC_BATCH, CONV_W - 1, LRU_WIDTH), 1.0),
        "cache_swa_k": nrm((DEPTH, DEC_BATCH, LB, ATT_HEADS, HEAD_DIM), 1.0),
        "cache_swa_v": nrm((DEPTH, DEC_BATCH, LB, ATT_HEADS, HEAD_DIM), 1.0),
        "state_ssd": nrm((DEPTH, DEC_BATCH, SSD_HEADS, SSD_HEAD_DIM, SSD_STATE), 0.1),
        "state_ssd_conv": nrm((DEPTH, DEC_BATCH, CONV_W - 1, SSD_CONV_CH), 1.0),
        "norm_mix_in": gain((DEPTH, D_MODEL)),
        "norm_mix_out": gain((DEPTH, D_MODEL)),
        "w_in": nrm((DEPTH, D_MODEL, N_IN), D_MODEL ** -0.5),
        "conv_a_w": nrm((DEPTH, CONV_W, LRU_WIDTH), 0.5),
        "conv_a_b": nrm((DEPTH, LRU_WIDTH), 0.01),
        "lru_wa": nrm((DEPTH, LRU_BLOCKS, LRU_BLOCK, LRU_BLOCK), LRU_BLOCK ** -0.5),
        "lru_ba": nrm((DEPTH, LRU_WIDTH), 0.01),
        "lru_wx": nrm((DEPTH, LRU_BLOCKS, LRU_BLOCK, LRU_BLOCK), LRU_BLOCK ** -0.5),
        "lru_bx": nrm((DEPTH, LRU_WIDTH), 0.01),
        "lru_lambda": lru_lambda,
        "conv_c_w": nrm((DEPTH, CONV_W, SSD_CONV_CH), 0.5),
        "conv_c_b": nrm((DEPTH, SSD_CONV_CH), 0.01),
        "dt_bias": dt_bias,
        "a_log": a_log,
        "d_skip": 1.0 + nrm((DEPTH, SSD_HEADS), 0.1),
        "ssm_norm": gain((DEPTH, SSD_WIDTH)),
        "w_out": nrm((DEPTH, D_MIX, D_MODEL), D_MIX ** -0.5),
        "norm_ffn_in": gain((DEPTH, D_MODEL)),
        "norm_ffn_out": gain((DEPTH, D_MODEL)),
        "w_gate_up": nrm((DEPTH, D_MODEL, 2 * D_FF), D_MODEL ** -0.5),
        "w_down": nrm((DEPTH, D_FF, D_MODEL), D_FF ** -0.5),
    }


def reference(x_prompt, x_sample, state_lru_h, state_lru_conv, cache_swa_k, cache_swa_v, state_ssd, state_ssd_conv,
              norm_mix_in, norm_mix_out, w_in, conv_a_w, conv_a_b, lru_wa, lru_ba, lru_wx, lru_bx, lru_lambda,
              conv_c_w, conv_c_b, dt_bias, a_log, d_skip, ssm_norm, w_out, norm_ffn_in, norm_ffn_out,
              w_gate_up, w_down):
    dtype = x_prompt.dtype
    bp = x_prompt.shape[0]
    yp, ys = x_prompt, x_sample
    p_new = [[] for _ in range(6)]
    s_new = [[] for _ in range(6)]
    for l in range(DEPTH):
        prm = (norm_mix_in[l], norm_mix_out[l], w_in[l], conv_a_w[l], conv_a_b[l], lru_wa[l], lru_ba[l],
               lru_wx[l], lru_bx[l], lru_lambda[l], conv_c_w[l], conv_c_b[l], dt_bias[l], a_log[l], d_skip[l],
               ssm_norm[l], w_out[l], norm_ffn_in[l], norm_ffn_out[l], w_gate_up[l], w_down[l])
        st_p = (jnp.zeros((bp, LRU_WIDTH), dtype), jnp.zeros((bp, CONV_W - 1, LRU_WIDTH), dtype), None, None,
                jnp.zeros((bp, SSD_HEADS, SSD_HEAD_DIM, SSD_STATE), dtype),
                jnp.zeros((bp, CONV_W - 1, SSD_CONV_CH), dtype))
        yp, np_l = _layer(yp, prm, st_p, True)
        st_s = (state_lru_h[l], state_lru_conv[l], cache_swa_k[l], cache_swa_v[l], state_ssd[l], state_ssd_conv[l])
        ys, ns_l = _layer(ys, prm, st_s, False)
        for j in range(6):
            p_new[j].append(np_l[j])
            s_new[j].append(ns_l[j])
    p_lru_h, p_lru_conv, p_swa_k, p_swa_v, p_ssd, p_ssd_conv = [jnp.stack(a) for a in p_new]
    s_lru_h, s_lru_conv, s_swa_k, s_swa_v, s_ssd, s_ssd_conv = [jnp.stack(a) for a in s_new]
    return (yp, ys, p_lru_h, p_lru_conv, p_swa_k, p_swa_v, p_ssd, p_ssd_conv,
            s_lru_h, s_lru_conv, s_swa_k, s_swa_v, s_ssd, s_ssd_conv)
```

```python
import os
import numpy as np
import concourse.bass as bass
import concourse.mybir as mybir
from concourse.bass_utils import run_bass_kernel_spmd
from contextlib import ExitStack

F32 = mybir.dt.float32
BF16 = mybir.dt.bfloat16
AF = mybir.ActivationFunctionType
ALU = mybir.AluOpType

D_MODEL = 1024
N_IN = 2950
D_FF = 2816
EPS = 1e-6
TS = 8
NB = 16
NCORES = 8


class Buf:
    def __init__(self, name):
        self.name = name
        self.w = None
        self.r = {}
        self.dsem = None
        self.dcnt = 0
        self.excl = False


class Tile:
    def __init__(self, ap, name, buf=None):
        self.ap = ap
        self.b = buf if buf is not None else Buf(name)

    def __getitem__(self, k):
        return self.ap[k]


class Sched:
    def __init__(self, nc, es):
        self.nc = nc
        self.es = es
        self.names = ['pe', 'act', 'dve', 'pool', 'sp']
        self.sem = {e: es.enter_context(nc.semaphore('s_' + e)) for e in self.names}
        self.cnt = {e: 0 for e in self.names}
        self.seen = {e: {} for e in self.names}
        self.q = {e: [] for e in self.names}
        self.dtoks = {}
        self.nsem = 5
        self.ninst = 0
        self.gseq = 0
        self.limit = int(os.environ.get('KLIMIT', '0'))
        self.marks = []

    def sb(self, name, shape, dt):
        t = self.es.enter_context(self.nc.sbuf_tensor(name, list(shape), dt))
        return Tile(t[tuple(slice(None) for _ in shape)], name)

    def ps(self, name, shape, dt):
        t = self.es.enter_context(self.nc.psum_tensor(name, list(shape), dt))
        r = Tile(t[tuple(slice(None) for _ in shape)], name)
        r.b.excl = True
        return r

    @staticmethod
    def _bufs(xs):
        return [x.b if isinstance(x, Tile) else x for x in xs]

    def _deps(self, e, reads, writes, skip_sem=None):
        need = {}

        def add(tok):
            s, v = tok
            k = id(s)
            if k not in need or need[k][1] < v:
                need[k] = (s, v)
        for b in reads:
            if b.w:
                add(b.w)
        for b in writes:
            if b.w and not (skip_sem is not None and b.w[0] is skip_sem):
                add(b.w)
            for tok in b.r.values():
                add(tok)
        out = []
        for k, (s, v) in need.items():
            if e == 'pe' and s is self.sem['pe']:
                continue
            if self.seen[e].get(k, 0) < v:
                self.seen[e][k] = v
                out.append((s, v))
        return out

    @staticmethod
    def _mark(tok, reads, writes):
        s, v = tok
        for b in reads:
            b.r[id(s)] = tok
        for b in writes:
            b.w = tok
            b.r = {}

    def op(self, e, fn, reads=(), writes=()):
        reads = self._bufs(reads)
        writes = self._bufs(writes)
        if e != 'pe':
            writes = writes + [b for b in reads if b.excl and b not in writes]
            reads = [b for b in reads if not b.excl]
        waits = self._deps(e, reads, writes)
        self.cnt[e] += 1
        tok = (self.sem[e], self.cnt[e])
        self._mark(tok, reads, writes)
        self.gseq += 1
        self.q[e].append((waits, fn, (self.sem[e], 1), self.gseq))
        self.ninst += 1 + len(waits)
        return tok

    def dma(self, e, out_ap, in_ap, reads=(), writes=(), sembuf=None):
        reads = self._bufs(reads)
        writes = self._bufs(writes)
        sb = sembuf.b if isinstance(sembuf, Tile) else sembuf
        if sb.dsem is None:
            sb.dsem = self.es.enter_context(self.nc.semaphore('d%d_%s' % (self.nsem, sb.name)))
            self.nsem += 1
        waits = self._deps(e, reads, writes, skip_sem=sb.dsem)
        sb.dcnt += 16
        tok = (sb.dsem, sb.dcnt)
        self._mark(tok, reads, writes)
        self.dtoks[id(sb.dsem)] = tok
        self.gseq += 1
        self.q[e].append((waits, lambda eng: eng.dma_start(out=out_ap, in_=in_ap), (sb.dsem, 16), self.gseq))
        self.ninst += 1 + len(waits)
        return tok

    def mark(self, label):
        self.marks.append((label, self.gseq))

    def wait_tok(self, e, tok):
        s, v = tok
        if e == 'pe' and s is self.sem['pe']:
            return
        if self.seen[e].get(id(s), 0) < v:
            self.seen[e][id(s)] = v
            self.gseq += 1
            self.q[e].append(([(s, v)], None, None, self.gseq))
            self.ninst += 1

    def barrier(self, engines=None):
        engines = engines or self.names
        toks = [(self.sem[o], self.cnt[o]) for o in self.names if self.cnt[o] > 0]
        toks += list(self.dtoks.values())
        for e in engines:
            for tok in toks:
                if tok[0] is self.sem.get(e):
                    continue
                self.wait_tok(e, tok)

    def emit(self):
        nc = self.nc
        S = self
        eng_of = {'pe': 'tensor', 'act': 'scalar', 'dve': 'vector', 'pool': 'gpsimd', 'sp': 'sync'}
        with nc.Block() as block:
            def mk(name):
                def run(eng):
                    for waits, fn, inc, seq in S.q[name]:
                        if S.limit and seq > S.limit:
                            break
                        for (s, v) in waits:
                            eng.wait_ge(s, v)
                        if fn is not None:
                            fn(eng).then_inc(inc[0], inc[1])
                return run
            for name in S.names:
                getattr(block, eng_of[name])(mk(name))


class Arena:
    def __init__(self, S, name, nel):
        self.t = S.sb(name, [128, nel], BF16)
        self.nel = nel
        self.off = 0
        self.hi = 0
        self.bufs = {}

    def reset(self):
        self.off = 0

    def carve(self, name, shape, dt):
        n = int(np.prod(shape[1:]))
        nel = n * (2 if dt == F32 else 1)
        if self.off % 2:
            self.off += 1
        assert self.off + nel <= self.nel, ("arena overflow", name, self.off + nel, self.nel)
        ap = self.t.ap[0:shape[0], self.off:self.off + nel]
        if dt == F32:
            ap = ap.bitcast(F32)
        if len(shape) == 3:
            ap = ap.rearrange("p (a b) -> p a b", b=shape[2])
        elif len(shape) == 4:
            ap = ap.rearrange("p (a b c) -> p a b c", b=shape[2], c=shape[3])
        self.off += nel
        self.hi = max(self.hi, self.off)
        if name not in self.bufs:
            self.bufs[name] = Buf(name)
        return Tile(ap, name, self.bufs[name])


def bc(ap, shape):
    return ap.to_broadcast(list(shape))


class Builder:
    def __init__(self, NT, CB, DEPTH):
        self.NT, self.CB, self.DEPTH = NT, CB, DEPTH
        self.WT = min(16, NT)
        self.nc = bass.Bass("TRN2", target_bir_lowering=False)
        self.din = {}
        self.dout = {}

    def I(self, name, shape, dt=F32):
        self.din[name] = self.nc.dram_tensor(name, list(shape), dt, kind="ExternalInput").ap()
        return self.din[name]

    def O(self, name, shape, dt=F32):
        self.dout[name] = self.nc.dram_tensor(name, list(shape), dt, kind="ExternalOutput").ap()
        return self.dout[name]

    def mm(self, out, lhsT, rhs, start, stop, reads, writes):
        self.S.op('pe', lambda e: e.matmul(out, lhsT, rhs, start=start, stop=stop), reads, writes)

    def tr(self, out, in_, ident, reads, writes):
        self.S.op('pe', lambda e: e.transpose(out, in_, ident), reads, writes)

    def act(self, out, in_, func, reads, writes, **kw):
        self.S.op('act', lambda e: e.activation(out, in_, func, **kw), reads, writes)

    def tt(self, eng, out, a, b, op, reads, writes):
        self.S.op(eng, lambda e: e.tensor_tensor(out, a, b, op), reads, writes)

    def tsc(self, eng, out, a, s1, s2, op0, op1, reads, writes):
        if s2 is None:
            self.S.op(eng, lambda e: e.tensor_scalar(out, a, s1, None, op0), reads, writes)
        else:
            self.S.op(eng, lambda e: e.tensor_scalar(out, a, s1, s2, op0, op1), reads, writes)

    def stt(self, out, in0, scalar, in1, op0, op1, reads, writes):
        self.S.op('dve', lambda e: e.scalar_tensor_tensor(out, in0, scalar, in1, op0, op1), reads, writes)

    def cp(self, eng, out, in_, reads, writes):
        if eng == 'act':
            self.S.op('act', lambda e: e.copy(out, in_), reads, writes)
        else:
            self.S.op(eng, lambda e: e.tensor_copy(out, in_), reads, writes)

    def memset(self, eng, ap, val, writes):
        self.S.op(eng, lambda e: e.memset(ap, val), (), writes)

    def recip(self, out, in_, reads, writes):
        self.S.op('dve', lambda e: e.reciprocal(out, in_), reads, writes)

    def gps(self):
        t = self.gbanks[self.gi % len(self.gbanks)]
        self.gi += 1
        return t

    def tps(self):
        t = self.tbanks[self.ti_ % len(self.tbanks)]
        self.ti_ += 1
        return t

    def build(self):
        NT, CB, L, WT = self.NT, self.CB, self.DEPTH, self.WT
        LB = CB * 128
        I, O = self.I, self.O
        I("xp", [NT, 128, 1024]); I("xs", [128, 1024])
        I("w_in", [L, 128, 8, N_IN]); I("w_outA", [L, 128, 5, 1024]); I("w_outC", [L, 128, 3, 1024])
        I("w_gu", [L, 128, 8, 2 * D_FF]); I("w_dn", [L, 128, 22, 1024])
        I("g4", [L, 4, 1024])
        I("cva_w", [L, 128, 2, 4]); I("cva_b", [L, 128, 2])
        I("cvx_w", [L, 64, 6, 4]); I("cvx_b", [L, 64, 6])
        I("cvbc_w", [L, 128, 4, 4]); I("cvbc_b", [L, 128, 4])
        I("wa_bd", [L, 128, 2, 128]); I("wx_bd", [L, 128, 2, 128]); I("lru_vec", [L, 128, 2, 3])
        I("ssd_h", [L, 18]); I("ssm_g", [L, 64, 6])
        I("st_lru_h", [L, 128, 2, NB]); I("st_lru_cv", [L, 128, 2, NB, 3])
        I("st_cvx", [L, 64, 6, NB, 3]); I("st_cvbc", [L, 128, 4, NB, 3])
        I("st_ssd", [L, 128, NB, 384]); I("kT_c", [L, NB, 3, 128, LB]); I("v_c", [L, NB, LB, 384])
        I("ident", [128, 128]); I("maskP", [128, 17, 128]); I("maskS", [128, CB + 1, 8])
        I("ssdc_p", [128, 3, 128]); I("ssdc_s", [128, 3, 128]); I("segm_s", [128, NB])
        O("y_p", [NT, 128, 1024]); O("y_s", [128, 1024])
        O("o_lru_h_p", [L, 128, 2, 1]); O("o_lru_h_s", [L, 128, 2, NB])
        O("o_lru_cv_p", [L, 128, 2, 1, 3]); O("o_lru_cv_s", [L, 128, 2, NB, 3])
        O("o_k_p", [L, WT, 128, 384]); O("o_v_p", [L, WT, 128, 384])
        O("o_k_s", [L, 128, 384]); O("o_v_s", [L, 128, 384])
        O("o_ssd_p", [L, 128, 1, 384]); O("o_ssd_s", [L, 128, NB, 384])
        O("o_cvx_p", [L, 64, 6, 1, 3]); O("o_cvx_s", [L, 64, 6, NB, 3])
        O("o_cvbc_p", [L, 128, 4, 1, 3]); O("o_cvbc_s", [L, 128, 4, NB, 3])
        self.xscr = self.nc.dram_tensor("xscr", [NT + 1, 128, 1024], F32, kind="Internal").ap()
        self.xscr_b = [Buf("xscr%d" % i) for i in range(NT + 1)]
        self.out_toks = []

        with ExitStack() as es:
            S = self.S = Sched(self.nc, es)
            self.identf = S.sb("identf", [128, 128], F32)
            self.identb = S.sb("identb", [128, 128], BF16)
            self.onesf = S.sb("onesf", [128, 128], F32)
            self.segm = S.sb("segm_sb", [128, NB], F32)
            self.gA = S.sb("gA", [128, 1024], F32)
            self.gB = S.sb("gB", [128, 1024], F32)
            self.xb = [S.sb("xt0", [128, 1024], F32)] * 2
            self.hn = S.sb("hn", [128, 1024], BF16)
            self.hT = S.sb("hT", [128, 8, 128], BF16)
            self.junk = self.hn
            self.sm = S.sb("small", [128, 64], F32)
            self.sm_b = [Buf("sm%d" % i) for i in range(16)]
            self.arena = Arena(S, "arena", 80500)
            self.gbanks = [S.ps("pg%d" % i, [128, 512], F32) for i in range(4)]
            self.tbanksf = [S.ps("pt%d" % i, [128, 512], F32) for i in range(2)]
            self.tbanks = [Tile(t.ap.bitcast(BF16), "ptb%d" % i, t.b) for i, t in enumerate(self.tbanksf)]
            self.plong = [S.ps("plong%d" % i, [128, 512], F32) for i in range(2)]
            self.pacc = Tile(self.plong[0][:, 0:390].rearrange("p (h d) -> p h d", d=65), "pacc", self.plong[0].b)
            self.gi = 0
            self.ti_ = 0
            S.dma('sp', self.identf[:], self.din["ident"], writes=[self.identf], sembuf=self.identf)
            S.dma('pool', self.identb[:], self.din["ident"], writes=[self.identb], sembuf=self.identb)
            S.dma('sp', self.segm[:], self.din["segm_s"], writes=[self.segm], sembuf=self.segm)
            self.memset('pool', self.onesf[:], 1.0, [self.onesf])

            for l in range(L):
                self.mixer_phase(l)
                self.ffn_phase(l)
            for tok in self.out_toks:
                S.wait_tok('sp', tok)
            S.barrier(['sp'])
            self.ninst = S.ninst
            S.emit()
        return self.nc

    def x_src(self, l, phase, i):
        if l == 0 and phase == 0:
            return (self.din["xp"][i] if i < self.NT else self.din["xs"]), None
        return self.xscr[i], self.xscr_b[i]

    def x_dst(self, l, phase, i):
        if l == self.DEPTH - 1 and phase == 1:
            return (self.dout["y_p"][i] if i < self.NT else self.dout["y_s"]), None
        return self.xscr[i], self.xscr_b[i]

    def load_x(self, l, phase, i, xt):
        ap, db = self.x_src(l, phase, i)
        self.S.dma('sp', xt[:], ap, reads=([db] if db else []), writes=[xt], sembuf=xt)

    def store_x(self, l, phase, i, xt):
        ap, db = self.x_dst(l, phase, i)
        tok = self.S.dma('sp', ap, xt[:], reads=[xt], writes=([db] if db else []), sembuf=xt)
        if db is None:
            self.out_toks.append(tok)

    def norm_T(self, xt, g_bc):
        sm, hn, hT = self.sm, self.hn, self.hT
        b0 = self.sm_b[0]
        self.act(self.junk[:], xt[:], AF.Square, [xt], [self.junk, b0], accum_out=sm[:, 0:1])
        self.tsc('dve', sm[:, 0:1], sm[:, 0:1], 1.0 / D_MODEL, EPS, ALU.mult, ALU.add, [b0], [b0])
        self.act(sm[:, 0:1], sm[:, 0:1], AF.Sqrt, [b0], [b0])
        self.recip(sm[:, 0:1], sm[:, 0:1], [b0], [b0])
        self.stt(hn[:], xt[:], sm[:, 0:1], g_bc[:], ALU.mult, ALU.mult, [xt, b0, g_bc], [hn])
        pT = self.tps()
        pv = pT[:].rearrange("p (a b) -> p a b", b=128)
        for c in range(8):
            self.tr(pv[:, c, :], hn[:, c * 128:(c + 1) * 128], self.identb[:], [hn, self.identb], [pT])
        self.cp('act', hT[:], pv, [pT], [hT])

    def out_norm_residual(self, xt, pbanks, g_bc):
        sm = self.sm
        b1 = self.sm_b[1]
        self.act(self.junk[:, 0:512], pbanks[0][:], AF.Square, [pbanks[0]], [self.junk, b1], accum_out=sm[:, 1:2])
        self.act(self.junk[:, 512:1024], pbanks[1][:], AF.Square, [pbanks[1]], [self.junk, b1], accum_out=sm[:, 2:3])
        self.tt('dve', sm[:, 1:2], sm[:, 1:2], sm[:, 2:3], ALU.add, [b1], [b1])
        self.tsc('dve', sm[:, 1:2], sm[:, 1:2], 1.0 / D_MODEL, EPS, ALU.mult, ALU.add, [b1], [b1])
        self.act(sm[:, 1:2], sm[:, 1:2], AF.Sqrt, [b1], [b1])
        self.recip(sm[:, 1:2], sm[:, 1:2], [b1], [b1])
        tmp = self.otmp
        for hf in range(2):
            sl = slice(hf * 512, (hf + 1) * 512)
            self.stt(tmp[:, sl], pbanks[hf][:], sm[:, 1:2], g_bc[:, sl], ALU.mult, ALU.mult, [pbanks[hf], b1, g_bc], [tmp])
        self.tt('pool', xt[:], xt[:], tmp[:], ALU.add, [xt, tmp], [xt])

    def mixer_phase(self, l):
        S, NT, CB = self.S, self.NT, self.CB
        din = self.din
        S.barrier()
        A = self.arena
        A.reset()
        self.w_in = A.carve("w_in", [128, 8, N_IN], BF16)
        self.w_oA = A.carve("w_oA", [128, 5, 1024], BF16)
        self.w_oC = A.carve("w_oC", [128, 3, 1024], BF16)
        for c in range(8):
            S.dma('pool', self.w_in[:, c, :], din["w_in"][l, :, c, :], writes=[self.w_in], sembuf=self.w_in)
        S.dma('pool', self.w_oA[:], din["w_outA"][l], writes=[self.w_oA], sembuf=self.w_oA)
        S.dma('pool', self.w_oC[:], din["w_outC"][l], writes=[self.w_oC], sembuf=self.w_oC)
        S.dma('sp', self.gA[:], din["g4"][l, 0].partition_broadcast(128), writes=[self.gA], sembuf=self.gA)
        S.dma('sp', self.gB[:], din["g4"][l, 1].partition_broadcast(128), writes=[self.gB], sembuf=self.gB)
        p = self.prm = {}
        def ld(name, shape, src, dt=F32, q='sp'):
            t = A.carve(name, shape, dt)
            S.dma(q, t[tuple(slice(None) for _ in shape)], src, writes=[t], sembuf=t)
            p[name] = t
            return t
        ld("cva_w", [128, 2, 4], din["cva_w"][l]); ld("cva_b", [128, 2], din["cva_b"][l])
        ld("cvx_w", [64, 6, 4], din["cvx_w"][l]); ld("cvx_b", [64, 6], din["cvx_b"][l])
        ld("cvbc_w", [128, 4, 4], din["cvbc_w"][l]); ld("cvbc_b", [128, 4], din["cvbc_b"][l])
        ld("wa_bd", [128, 2, 128], din["wa_bd"][l], BF16, 'pool'); ld("wx_bd", [128, 2, 128], din["wx_bd"][l], BF16, 'pool')
        ld("lru_vec", [128, 2, 3], din["lru_vec"][l])
        ld("ssd_h", [128, 18], din["ssd_h"][l].partition_broadcast(128))
        ld("ssm_g", [64, 6], din["ssm_g"][l])
        self.maskP = ld("maskP", [128, 17, 128], din["maskP"], BF16, 'pool')
        self.maskS = ld("maskS", [128, CB + 1, 8], din["maskS"])
        self.ssdc = {True: ld("ssdc_p", [128, 3, 128], din["ssdc_p"]), False: ld("ssdc_s", [128, 3, 128], din["ssdc_s"])}
        cl = p["cl"] = A.carve("cl", [128, 2], F32)
        lam = p["lru_vec"][:, :, 2]
        self.act(cl[:], lam, AF.Exp, [p["lru_vec"]], [cl], scale=-1.0)
        self.act(cl[:], cl[:], AF.Ln, [cl], [cl], bias=1.0)
        self.tsc('dve', cl[:], cl[:], -8.0, None, ALU.mult, None, [cl], [cl])
        An = p["Aneg"] = A.carve("Aneg", [128, 6], F32)
        self.act(An[:], p["ssd_h"][:, 6:12], AF.Exp, [p["ssd_h"]], [An])
        self.tsc('dve', An[:], An[:], -1.0, None, ALU.mult, None, [An], [An])
        S.mark('mixer params')
        c = A.carve
        off0 = A.off
        self.KT = c("KT", [128, 3, 17, 128], BF16)
        self.KT_b = [Buf("KT%d" % i) for i in range(17)]
        self.V1 = c("V1", [128, 17, 6, 65], BF16)
        self.V1_b = [Buf("V1%d" % i) for i in range(17)]
        self.ebuf = [c("ebuf%d" % i, [128, 4, 128], BF16) for i in range(2)]
        self.pbuf = [c("pbuf%d" % i, [128, 4, 128], BF16) for i in range(2)]
        self.otok = c("otok", [128, 6, 64], BF16)
        self.st_p = c("st_p", [128, 384], F32)
        end_p = A.off
        A.off = off0
        self.KTc = c("KTc", [128, CB * 128], BF16)
        self.V1c = c("V1c", [128, CB + 1, 2, 65], BF16)
        self.es_ = c("es_", [128, CB + 1, 8], F32)
        self.ps_ = c("ps_", [128, CB + 1, 8], BF16)
        self.ob = c("ob", [8, 384], BF16)
        self.vnew = c("vnew", [8, 384], BF16)
        self.stb = [c("stb%d" % i, [128, 384], F32) for i in range(2)]
        self.xm = c("xm", [128, 384], BF16)
        self.KTn = c("KTn", [128, 3, 128], BF16)
        self.vtokb = c("vtokb", [128, 384], BF16)
        A.off = max(A.off, end_p)
        self.QT = c("QT", [128, 3, 128], BF16)
        self.kvst = [c("kvst%d" % i, [128, 384], F32) for i in range(2)]
        self.xpA = c("xpA", [128, 2, NB * (3 + TS)], F32)
        self.xpX = c("xpX", [64, 6, NB * (3 + TS)], F32)
        self.xpBC = c("xpBC", [128, 4, NB * (3 + TS)], F32)
        self.cA = c("cA", [128, 2, 128], F32)
        self.cX = c("cX", [64, 6, 128], F32)
        self.cBC = c("cBC", [128, 4, 128], F32)
        self.ctmp = c("ctmp", [128, 768], F32)
        self.ge = c("ge", [128, 2, 128], F32)
        self.xcb = c("xcb", [128, 2, 128], BF16)
        self.lr = c("lr", [128, 2, 128], F32)
        self.li = c("li", [128, 2, 128], F32)
        self.la = c("la", [128, 2, 128], F32)
        self.lu = c("lu", [128, 2, 128], F32)
        self.lh = c("lh", [128, 2, 128], F32)
        self.hst = {True: c("hst_p", [128, 2, 1], F32), False: c("hst_s", [128, 2, NB], F32)}
        self.lt = c("lt", [128, 2, NB], F32)
        self.mixT = c("mixT", [128, 5, 128], BF16)
        self.mixC = c("mixC", [128, 3, 128], BF16)
        self.rden = c("rden", [128, 6, 1], F32)
        self.sz = c("sz", [64, 6, 128], F32)
        self.xsTb = c("xsTb", [64, 6, 128], BF16)
        self.BCb = c("BCb", [128, 4, 128], BF16)
        self.dts = c("dts", [128, 4, 8], F32)
        self.xr = c("xr", [128, 6, 64], BF16)
        self.xrd = c("xrd", [128, 6, 64], BF16)
        self.Btok = c("Btok", [128, 256], BF16)
        RD = c("RD", [128, 12, 128], F32)
        self.R = Tile(RD[:, 0:6, :], "R", RD.b)
        self.CE = self.R
        self.Dm = Tile(RD[:, 6:12, :], "Dm", RD.b)
        self.otmp = Tile(RD[:, 0:8, :].rearrange("p a b -> p (a b)"), "otmp", RD.b)
        self.Eac = c("Eac", [128, 6, 128], F32)
        self.GT = c("GT", [128, 6, 128], BF16)
        self.R2 = c("R2", [128, NB, 6], F32)
        self.etot = c("etot", [128, NB, 6], F32)
        self.yy = c("yy", [64, 6, 128], F32)
        self.yt = Tile(self.ctmp[0:64, 0:768].rearrange("p (a b) -> p a b", b=128), "yt", self.ctmp.b)
        self.rs = c("rs", [64, 128], F32)
        self.memset('pool', self.st_p[:], 0.0, [self.st_p])
        self.memset('pool', self.hst[True][:], 0.0, [self.hst[True]])
        self.memset('pool', self.V1[:, :, :, 64:65], 1.0, self.V1_b)
        S.dma('sp', self.hst[False][:], din["st_lru_h"][l], writes=[self.hst[False]], sembuf=self.hst[False])

        for i in range(NT + 1):
            if i == NT:
                S.barrier()
                self.memset('pool', self.V1c[:, :, :, 64:65], 1.0, [self.V1c])
            self.load_x(l, 0, i, self.xb[0])
            self.mixer_tile(l, i)

    def mixer_tile(self, l, i):
        S, NT, CB = self.S, self.NT, self.CB
        din, dout, p = self.din, self.dout, self.prm
        isp = i < NT
        nseg, Ls = (1, 128) if isp else (NB, TS)
        W = 3 + Ls
        last = (i == NT - 1)
        want_kv = (not isp) or (i >= NT - self.WT)
        xt = self.xb[i % 2]
        S.mark('tile%d start' % i)
        self.norm_T(xt, self.gA)
        S.mark('tile%d normT' % i)
        hT, w_in = self.hT, self.w_in

        def proj_fm(ps_ap, ps_t, c0, M):
            for c in range(8):
                self.mm(ps_ap, w_in[:, c, c0:c0 + M], hT[:, c, :], c == 0, c == 7, [w_in, hT], [ps_t])

        def seg4(t, n):
            return t[:, :, 0:nseg * W].rearrange("p a (s w) -> p a s w", w=W)

        xpA4, xpX4, xpBC4 = seg4(self.xpA, 2), seg4(self.xpX, 6), seg4(self.xpBC, 4)
        if isp:
            if i == 0:
                for t, v in ((self.xpA, xpA4), (self.xpX, xpX4), (self.xpBC, xpBC4)):
                    self.memset('pool', v[:, :, :, 0:3], 0.0, [t])
            else:
                for t, v in ((self.xpA, xpA4), (self.xpX, xpX4), (self.xpBC, xpBC4)):
                    self.cp('pool', v[:, :, :, 0:3], v[:, :, :, Ls:Ls + 3], [t], [t])
        else:
            S.dma('sp', xpA4[:, :, :, 0:3], din["st_lru_cv"][l], writes=[self.xpA], sembuf=self.xpA)
            S.dma('sp', xpX4[:, :, :, 0:3], din["st_cvx"][l], writes=[self.xpX], sembuf=self.xpX)
            S.dma('sp', xpBC4[:, :, :, 0:3], din["st_cvbc"][l], writes=[self.xpBC], sembuf=self.xpBC)

        pg1 = self.gps()
        v1 = pg1[:].rearrange("p (a b) -> p a b", b=128)
        for t in range(4):
            proj_fm(v1[:, t, :], pg1, t * 128, 128)
        self.act(self.ge[:], v1[:, 0:2, :], AF.Gelu_apprx_tanh, [pg1], [self.ge])
        self.cp('dve', xpA4[:, :, :, 3:W], v1[:, 2:4, :].rearrange("p a (s w) -> p a s w", w=Ls), [pg1], [self.xpA])
        pq = self.gps()
        vq = pq[:].rearrange("p (a b) -> p a b", b=128)
        for t in range(3):
            proj_fm(vq[:, t, :], pq, 512 + t * 128, 128)
        self.cp('act', self.QT[:], vq[:, 0:3, :], [pq], [self.QT])
        pk = self.gps()
        vk = pk[:].rearrange("p (a b) -> p a b", b=128)
        for t in range(3):
            proj_fm(vk[:, t, :], pk, 896 + t * 128, 128)
        slot = i % 17
        if isp:
            self.cp('dve', self.KT[:, :, slot, :], vk[:, 0:3, :], [pk], [self.KT_b[slot]])
        else:
            self.cp('dve', self.KTn[:], vk[:, 0:3, :], [pk], [self.KTn])
        pv = self.gps()
        for c in range(8):
            self.mm(pv[:, 0:384], hT[:, c, :], w_in[:, c, 1280:1664], c == 0, c == 7, [hT, w_in], [pv])
        for c in range(8):
            self.mm(pv[:, 384:390], hT[:, c, :], w_in[:, c, 2944:2950], c == 0, c == 7, [hT, w_in], [pv])
        pv3 = pv[:, 0:384].rearrange("p (h d) -> p h d", d=64)
        if isp:
            self.cp('act', self.V1[:, slot, :, 0:64], pv3, [pv], [self.V1_b[slot]])
        else:
            self.cp('act', self.vtokb[:], pv[:, 0:384], [pv], [self.vtokb])
        self.cp('dve', self.dts[:, 0, 0:6], pv[:, 384:390], [pv], [self.dts])
        if want_kv:
            vs = self.kvst[0]
            self.cp('act', vs[:], pv[:, 0:384], [pv], [vs])
            dst = dout["o_v_p"][l, i - (NT - self.WT)] if isp else dout["o_v_s"][l]
            self.out_toks.append(S.dma('sp', dst, vs[:], reads=[vs], sembuf=vs))
            pk2 = self.gps()
            for c in range(8):
                self.mm(pk2[:, 0:384], hT[:, c, :], w_in[:, c, 896:1280], c == 0, c == 7, [hT, w_in], [pk2])
            ks = self.kvst[1]
            self.cp('act', ks[:], pk2[:, 0:384], [pk2], [ks])
            dst = dout["o_k_p"][l, i - (NT - self.WT)] if isp else dout["o_k_s"][l]
            self.out_toks.append(S.dma('sp', dst, ks[:], reads=[ks], sembuf=ks))
        for hb in range(2):
            pz = self.gps()
            vz = pz[0:64, 0:384].rearrange("p (a b) -> p a b", b=128)
            for t in range(3):
                proj_fm(vz[:, t, :], pz, 1664 + (hb * 3 + t) * 64, 64)
            self.act(self.sz[:, hb * 3:hb * 3 + 3, :], vz, AF.Silu, [pz], [self.sz])
        for hb in range(2):
            px = self.gps()
            vx = px[0:64, 0:384].rearrange("p (a b) -> p a b", b=128)
            for t in range(3):
                proj_fm(vx[:, t, :], px, 2048 + (hb * 3 + t) * 64, 64)
            self.cp('dve', xpX4[:, hb * 3:hb * 3 + 3, :, 3:W], vx.rearrange("p a (s w) -> p a s w", w=Ls), [px], [self.xpX])
        pbc = self.gps()
        vbc = pbc[:].rearrange("p (a b) -> p a b", b=128)
        for t in range(4):
            proj_fm(vbc[:, t, :], pbc, 2432 + t * 128, 128)
        self.cp('act', xpBC4[:, :, :, 3:W], vbc.rearrange("p a (s w) -> p a s w", w=Ls), [pbc], [self.xpBC])

        S.mark('tile%d proj' % i)
        def conv(P, n, xp_t, xp4, wt, bt, out_t, eng):
            o4 = out_t[:].rearrange("p a (s w) -> p a s w", w=Ls)
            tmp = self.ctmp[0:P, 0:n * 128].rearrange("p (a s w) -> p a s w", a=n, w=Ls)
            shp = [P, n, nseg, Ls]
            self.tt(eng, o4, xp4[:, :, :, 0:Ls], bc(wt[:, :, 0:1].unsqueeze(3), shp), ALU.mult, [xp_t, wt], [out_t])
            for j in range(1, 4):
                self.tt(eng, tmp, xp4[:, :, :, j:j + Ls], bc(wt[:, :, j:j + 1].unsqueeze(3), shp), ALU.mult, [xp_t, wt], [self.ctmp])
                self.tt(eng, o4, o4, tmp, ALU.add, [out_t, self.ctmp], [out_t])
            self.tt(eng, o4, o4, bc(bt[:].unsqueeze(2).unsqueeze(3), shp), ALU.add, [out_t, bt], [out_t])
        conv(128, 2, self.xpA, xpA4, p["cva_w"], p["cva_b"], self.cA, 'pool')
        conv(64, 6, self.xpX, xpX4, p["cvx_w"], p["cvx_b"], self.cX, 'dve')
        conv(128, 4, self.xpBC, xpBC4, p["cvbc_w"], p["cvbc_b"], self.cBC, 'pool')
        if last or not isp:
            sfx = "_p" if isp else "_s"
            for nm, t, v in (("o_lru_cv", self.xpA, xpA4), ("o_cvx", self.xpX, xpX4), ("o_cvbc", self.xpBC, xpBC4)):
                self.out_toks.append(S.dma('sp', dout[nm + sfx][l], v[:, :, :, Ls:Ls + 3], reads=[t], sembuf=t))
        self.act(self.cX[:], self.cX[:], AF.Silu, [self.cX], [self.cX])
        self.act(self.cBC[:], self.cBC[:], AF.Silu, [self.cBC], [self.cBC])
        self.cp('pool', self.xsTb[:], self.cX[:], [self.cX], [self.xsTb])
        self.cp('pool', self.BCb[:], self.cBC[:], [self.cBC], [self.BCb])

        S.mark('tile%d conv' % i)
        self.lru(l, i, isp, nseg, Ls, last)
        S.mark('tile%d lru' % i)
        if isp:
            self.attn_prompt(i)
        else:
            self.attn_sample(l)
        S.mark('tile%d attn' % i)
        self.ssd(l, i, isp, nseg, Ls, last)
        S.mark('tile%d ssd' % i)

        po = [self.gps(), self.gps()]
        for hf in range(2):
            sl = slice(hf * 512, (hf + 1) * 512)
            for cidx in range(5):
                self.mm(po[hf][:], self.mixT[:, cidx, :], self.w_oA[:, cidx, sl], cidx == 0, False, [self.mixT, self.w_oA], [po[hf]])
            for h in range(3):
                self.mm(po[hf][:], self.mixC[:, h, :], self.w_oC[:, h, sl], False, h == 2, [self.mixC, self.w_oC], [po[hf]])
        self.out_norm_residual(xt, po, self.gB)
        self.store_x(l, 0, i, xt)
        S.mark('tile%d done' % i)

    def lru(self, l, i, isp, nseg, Ls, last):
        S, p = self.S, self.prm
        cA, xcb, lr, li, la, lu, lh = self.cA, self.xcb, self.lr, self.li, self.la, self.lu, self.lh
        hst = self.hst[isp]
        self.cp('pool', xcb[:], cA[:], [cA], [xcb])
        pr = self.gps()
        v = pr[:].rearrange("p (a b) -> p a b", b=128)
        for t in range(2):
            self.mm(v[:, t, :], p["wa_bd"][:, t, :], xcb[:, t, :], True, True, [p["wa_bd"], xcb], [pr])
            self.mm(v[:, 2 + t, :], p["wx_bd"][:, t, :], xcb[:, t, :], True, True, [p["wx_bd"], xcb], [pr])
        for t in range(2):
            self.act(lr[:, t, :], v[:, t, :], AF.Sigmoid, [pr, p["lru_vec"]], [lr], bias=p["lru_vec"][:, t, 0:1])
            self.act(li[:, t, :], v[:, 2 + t, :], AF.Sigmoid, [pr, p["lru_vec"]], [li], bias=p["lru_vec"][:, t, 1:2])
        for t in range(2):
            self.act(la[:, t, :], lr[:, t, :], AF.Exp, [lr, p["cl"]], [la], scale=p["cl"][:, t:t + 1])
        self.tt('pool', lr[:], la[:], la[:], ALU.mult, [la], [lr])
        self.tsc('pool', lr[:], lr[:], -1.0, 1.0, ALU.mult, ALU.add, [lr], [lr])
        self.act(lr[:], lr[:], AF.Sqrt, [lr], [lr])
        self.tt('dve', li[:], li[:], cA[:], ALU.mult, [li, cA], [li])
        self.tt('dve', lu[:], lr[:], li[:], ALU.mult, [lr, li], [lu])
        a0 = la[:].rearrange("p a (s w) -> p a s w", w=Ls)[:, :, :, 0]
        u0 = lu[:].rearrange("p a (s w) -> p a s w", w=Ls)[:, :, :, 0]
        lt = self.lt[:, :, 0:nseg]
        self.tt('dve', lt, a0, hst[:], ALU.mult, [la, hst], [self.lt])
        self.tt('dve', u0, u0, lt, ALU.add, [lu, self.lt], [lu])
        self.memset('dve', a0, 0.0, [la])
        for t in range(2):
            S.op('dve', (lambda t: (lambda e: e.tensor_tensor_scan(lh[:, t, :], la[:, t, :], lu[:, t, :], 0.0, ALU.mult, ALU.add)))(t), [la, lu], [lh])
        hl = lh[:].rearrange("p a (s w) -> p a s w", w=Ls)[:, :, :, Ls - 1]
        self.cp('pool', hst[:], hl, [lh], [hst])
        if last or not isp:
            dst = self.dout["o_lru_h_p" if isp else "o_lru_h_s"][l]
            self.out_toks.append(S.dma('sp', dst, hst[:], reads=[hst], sembuf=hst))
        self.tt('dve', self.mixT[:, 0:2, :], lh[:], self.ge[:], ALU.mult, [lh, self.ge], [self.mixT])

    def attn_prompt(self, i):
        S = self.S
        pacc = self.pacc
        nk = min(16, i) + 1
        k = 0
        for h in range(6):
            pr_, hh = h // 2, h % 2
            rows = slice(hh * 64, hh * 64 + 64)
            for o0 in range(0, nk, 4):
                nb = min(4, nk - o0)
                ps = self.gps()
                v = ps[:].rearrange("p (a b) -> p a b", b=128)
                for jj in range(nb):
                    sj = (i - (o0 + jj)) % 17
                    self.mm(v[:, jj, :], self.KT[rows, pr_, sj, :], self.QT[rows, pr_, :], True, True, [self.KT_b[sj], self.QT], [ps])
                eb, pb = self.ebuf[k % 2], self.pbuf[k % 2]
                k += 1
                self.act(eb[:, 0:nb, :], v[:, 0:nb, :], AF.Exp, [ps], [eb], scale=0.125)
                self.tt('dve' if k % 2 else 'pool', pb[:, 0:nb, :], eb[:, 0:nb, :], self.maskP[:, o0:o0 + nb, :], ALU.mult, [eb, self.maskP], [pb])
                for jj in range(nb):
                    o = o0 + jj
                    sj = (i - o) % 17
                    self.mm(pacc[:, h, :], pb[:, jj, :], self.V1[:, sj, h, :], o == 0, o == nk - 1, [pb, self.V1_b[sj]], [pacc])
        self.recip(self.rden[:], pacc[:, :, 64:65], [pacc], [self.rden])
        self.tt('dve', self.otok[:], pacc[:, :, 0:64], bc(self.rden[:], [128, 6, 64]), ALU.mult, [pacc, self.rden], [self.otok])
        pT = self.tps()
        pv = pT[:].rearrange("p (a b) -> p a b", b=128)
        of = self.otok[:].rearrange("p h d -> p (h d)")
        for c in range(3):
            self.tr(pv[:, c, :], of[:, c * 128:(c + 1) * 128], self.identb[:], [self.otok, self.identb], [pT])
        self.cp('act', self.mixT[:, 2:5, :], pv[:, 0:3, :], [pT], [self.mixT])

    def attn_sample(self, l):
        S, CB = self.S, self.CB
        din = self.din
        pacc = self.pacc
        KTc, V1c = self.KTc, self.V1c
        for b in range(NB):
            cs = slice(b * 8, b * 8 + 8)
            pn = self.gps()
            self.mm(pn[0:8, 0:384], self.identb[:, cs], self.vtokb[:], True, True, [self.identb, self.vtokb], [pn])
            self.cp('act', self.vnew[:], pn[0:8, 0:384], [pn], [self.vnew])
            for pr_ in range(3):
                S.dma('pool', KTc[:], din["kT_c"][l, b, pr_], writes=[KTc], sembuf=KTc)
                vsrc = din["v_c"][l, b].rearrange("(blk k) (h d) -> k blk h d", k=128, d=64)
                for hh in range(2):
                    S.dma('pool', V1c[:, 0:CB, hh, 0:64], vsrc[:, :, 2 * pr_ + hh, :], writes=[V1c], sembuf=V1c)
                self.cp('act', V1c[0:8, CB, :, 0:64], self.vnew[0:8, pr_ * 128:(pr_ + 1) * 128].rearrange("p (h d) -> p h d", d=64), [self.vnew], [V1c])
                for hh in range(2):
                    h = pr_ * 2 + hh
                    rows = slice(hh * 64, hh * 64 + 64)
                    ps = self.gps()
                    v = ps[:, 0:(CB + 1) * 8].rearrange("p (a b) -> p a b", b=8)
                    for blk in range(CB):
                        self.mm(v[:, blk, :], KTc[rows, blk * 128:(blk + 1) * 128], self.QT[rows, pr_, cs], True, True, [KTc, self.QT], [ps])
                    self.mm(v[0:8, CB, :], self.KTn[rows, pr_, cs], self.QT[rows, pr_, cs], True, True, [self.KTn, self.QT], [ps])
                    self.act(self.es_[:], v, AF.Exp, [ps], [self.es_], scale=0.125)
                    self.tt('dve', self.ps_[:], self.es_[:], self.maskS[:], ALU.mult, [self.es_, self.maskS], [self.ps_])
                    for blk in range(CB):
                        self.mm(pacc[0:8, h, :], self.ps_[:, blk, :], V1c[:, blk, hh, :], blk == 0, False, [self.ps_, V1c], [pacc])
                    self.mm(pacc[0:8, h, :], self.ps_[0:8, CB, :], V1c[0:8, CB, hh, :], False, True, [self.ps_, V1c], [pacc])
            self.recip(self.rden[0:8], pacc[0:8, :, 64:65], [pacc], [self.rden])
            self.tt('dve', self.ob[:].rearrange("p (h d) -> p h d", d=64), pacc[0:8, :, 0:64], bc(self.rden[0:8], [8, 6, 64]), ALU.mult, [pacc, self.rden], [self.ob])
            pT = self.tps()
            pv = pT[:].rearrange("p (a b) -> p a b", b=128)
            for c in range(3):
                self.tr(pv[:, c, 0:8], self.ob[0:8, c * 128:(c + 1) * 128], self.identb[0:8, 0:8], [self.ob, self.identb], [pT])
            self.cp('act', self.mixT[:, 2:5, cs], pv[:, 0:3, 0:8], [pT], [self.mixT])

    def ssd(self, l, i, isp, nseg, Ls, last):
        S, p = self.S, self.prm
        dts = self.dts
        sc = self.ssdc[isp]
        tri, neg, same = sc[:, 0, :], sc[:, 1, :], sc[:, 2, :]
        hp = p["ssd_h"]
        self.tt('dve', dts[:, 0, 0:6], dts[:, 0, 0:6], hp[:, 0:6], ALU.add, [dts, hp], [dts])
        self.act(dts[:, 0, 0:6], dts[:, 0, 0:6], AF.Exp, [dts], [dts])
        self.act(dts[:, 0, 0:6], dts[:, 0, 0:6], AF.Ln, [dts], [dts], bias=1.0)
        self.tt('dve', dts[:, 1, 0:6], dts[:, 0, 0:6], p["Aneg"][:], ALU.mult, [dts, p["Aneg"]], [dts])
        dt, dtA = dts[:, 0, 0:6], dts[:, 1, 0:6]
        pT = self.tps()
        for h in range(6):
            self.tr(pT[:, h * 64:(h + 1) * 64], self.xsTb[:, h, :], self.identb[0:64, 0:64], [self.xsTb, self.identb], [pT])
        self.tt('dve', self.xr[:], pT[:, 0:384].rearrange("p (h d) -> p h d", d=64), bc(dt.unsqueeze(2), [128, 6, 64]), ALU.mult, [pT, dts], [self.xr])
        pT2 = self.tps()
        for t in range(2):
            self.tr(pT2[:, t * 128:(t + 1) * 128], self.BCb[:, t, :], self.identb[:], [self.BCb, self.identb], [pT2])
        self.cp('act', self.Btok[:], pT2[:, 0:256], [pT2], [self.Btok])
        self.tt('pool', self.R[:], bc(tri.unsqueeze(1), [128, 6, 128]), bc(dtA.unsqueeze(2), [128, 6, 128]), ALU.mult, [sc, dts], [self.R])
        pa = [self.gps(), self.gps()]
        for hb in range(2):
            self.mm(pa[hb][:, 0:384], self.onesf[:], self.R[:, hb * 3:hb * 3 + 3, :].rearrange("p a b -> p (a b)"), True, True, [self.onesf, self.R], [pa[hb]])
        pb = self.gps()
        self.mm(pb[:, 0:6], tri, dtA, True, True, [sc, dts], [pb])
        self.mm(pb[:, 8:14], same, dtA, True, True, [sc, dts], [pb])
        self.cp('act', dts[:, 2, 0:6], pb[:, 0:6], [pb], [dts])
        self.cp('act', dts[:, 3, 0:6], pb[:, 8:14], [pb], [dts])
        acs, tot = dts[:, 2, 0:6], dts[:, 3, 0:6]
        for hb in range(2):
            pav = pa[hb][:, 0:384].rearrange("p (a b) -> p a b", b=128)
            self.tt('dve', self.Dm[:, hb * 3:hb * 3 + 3, :], pav, bc(dts[:, 2, hb * 3:hb * 3 + 3].unsqueeze(2), [128, 3, 128]), ALU.subtract, [pa[hb], dts], [self.Dm])
            self.act(self.Eac[:, hb * 3:hb * 3 + 3, :], pav, AF.Exp, [pa[hb]], [self.Eac])
        self.tt('pool', self.Dm[:], self.Dm[:], bc(neg.unsqueeze(1), [128, 6, 128]), ALU.add, [self.Dm, sc], [self.Dm])
        self.act(self.Dm[:], self.Dm[:], AF.Exp, [self.Dm], [self.Dm])
        pc = self.gps()
        pcv = pc[:, 0:256].rearrange("p (a b) -> p a b", b=128)
        for g in range(2):
            self.mm(pcv[:, g, :], self.BCb[:, g, :], self.BCb[:, 2 + g, :], True, True, [self.BCb], [pc])
        for g in range(2):
            self.tt('dve', self.GT[:, g * 3:g * 3 + 3, :], self.Dm[:, g * 3:g * 3 + 3, :], bc(pcv[:, g:g + 1, :], [128, 3, 128]), ALU.mult, [self.Dm, pc], [self.GT])
            self.tt('pool', self.CE[:, g * 3:g * 3 + 3, :], self.Eac[:, g * 3:g * 3 + 3, :], bc(self.cBC[:, 2 + g:3 + g, :], [128, 3, 128]), ALU.mult, [self.Eac, self.cBC], [self.CE])
        self.tt('dve', dts[:, 3, 0:6], tot, acs, ALU.subtract, [dts], [dts])
        self.act(dts[:, 3, 0:6], dts[:, 3, 0:6], AF.Exp, [dts], [dts])
        self.tt('dve', self.xrd[:], self.xr[:], bc(dts[:, 3, 0:6].unsqueeze(2), [128, 6, 64]), ALU.mult, [self.xr, dts], [self.xrd])
        R2 = self.R2[:, 0:nseg, :]
        if nseg > 1:
            self.tt('dve', R2, bc(dtA.unsqueeze(1), [128, nseg, 6]), bc(self.segm[:, 0:nseg].unsqueeze(2), [128, nseg, 6]), ALU.mult, [dts, self.segm], [self.R2])
        else:
            self.cp('dve', R2, dtA.unsqueeze(1), [dts], [self.R2])
        pe_ = self.gps()
        self.mm(pe_[:, 0:nseg * 6], self.onesf[:], R2.rearrange("p a b -> p (a b)"), True, True, [self.onesf, self.R2], [pe_])
        et = self.etot[:, 0:nseg, :]
        self.act(et, pe_[:, 0:nseg * 6].rearrange("p (a b) -> p a b", b=6), AF.Exp, [pe_], [self.etot])
        xrdf = self.xrd[:].rearrange("p h d -> p (h d)")
        py = self.plong
        for h in range(6):
            o = py[h // 3][0:64, (h % 3) * 128:(h % 3 + 1) * 128]
            self.mm(o, self.xr[:, h, :], self.GT[:, h, :], True, True, [self.xr, self.GT], [py[h // 3]])
        pyo = self.tbanksf
        for b in range(nseg):
            if isp:
                st = self.st_p
            else:
                st = self.stb[b % 2]
                S.dma('sp', st[:], self.din["st_ssd"][l, :, b, :], writes=[st], sembuf=st)
            for h in range(6):
                o = pyo[h // 3][0:64, (h % 3) * 128:(h % 3 + 1) * 128]
                self.mm(o[:, b * Ls:(b + 1) * Ls], st[:, h * 64:(h + 1) * 64], self.CE[:, h, b * Ls:(b + 1) * Ls], True, True, [st, self.CE], [pyo[h // 3]])
            if nseg > 1:
                self.tsc('dve', self.xm[:], xrdf, self.segm[:, b:b + 1], None, ALU.mult, None, [self.xrd, self.segm], [self.xm])
                xm, xm_t = self.xm[:], self.xm
            else:
                xm, xm_t = xrdf, self.xrd
            pst = self.gps()
            for g in range(2):
                self.mm(pst[:, g * 192:(g + 1) * 192], self.Btok[:, g * 128:(g + 1) * 128], xm[:, g * 192:(g + 1) * 192], True, True, [self.Btok, xm_t], [pst])
            sb3 = st[:].rearrange("p (h d) -> p h d", d=64)
            self.tt('pool', sb3, sb3, bc(self.etot[:, b, :].unsqueeze(2), [128, 6, 64]), ALU.mult, [st, self.etot], [st])
            self.tt('dve', st[:], st[:], pst[:, 0:384], ALU.add, [st, pst], [st])
            if not isp:
                self.out_toks.append(S.dma('sp', self.dout["o_ssd_s"][l, :, b, :], st[:], reads=[st], sembuf=st))
        if isp and last:
            self.out_toks.append(S.dma('sp', self.dout["o_ssd_p"][l, :, 0, :], self.st_p[:], reads=[self.st_p], sembuf=self.st_p))
        yy, yt = self.yy, self.yt
        self.tt('pool', yt[:], self.cX[:], bc(hp[0:64, 12:18].unsqueeze(2), [64, 6, 128]), ALU.mult, [self.cX, hp], [yt])
        for hb in range(2):
            self.tt('dve', yy[:, hb * 3:hb * 3 + 3, :], py[hb][0:64, 0:384].rearrange("p (a b) -> p a b", b=128), yt[:, hb * 3:hb * 3 + 3, :], ALU.add, [py[hb], yt], [yy])
            self.tt('dve', yy[:, hb * 3:hb * 3 + 3, :], pyo[hb][0:64, 0:384].rearrange("p (a b) -> p a b", b=128), yy[:, hb * 3:hb * 3 + 3, :], ALU.add, [pyo[hb], yy], [yy])
        self.tt('dve', yy[:], yy[:], self.sz[:], ALU.mult, [yy, self.sz], [yy])
        self.tt('pool', yt[:], yy[:], yy[:], ALU.mult, [yy], [yt])
        pss = self.gps()
        for h in range(6):
            self.mm(pss[0:64, 0:128], self.onesf[0:64, 0:64], yt[:, h, :], h == 0, h == 5, [self.onesf, yt], [pss])
        rs = self.rs
        self.tsc('dve', rs[:], pss[0:64, 0:128], 1.0 / 384.0, EPS, ALU.mult, ALU.add, [pss], [rs])
        self.act(rs[:], rs[:], AF.Sqrt, [rs], [rs])
        self.recip(rs[:], rs[:], [rs], [rs])
        self.tt('dve', yy[:], yy[:], bc(rs[:].unsqueeze(1), [64, 6, 128]), ALU.mult, [yy, rs], [yy])
        yy2 = yy[:].rearrange("p (c two) l -> p c two l", two=2)
        sg2 = p["ssm_g"][:].rearrange("p (c two) -> p c two", two=2)
        for hh in range(2):
            self.tt('dve', self.mixC[hh * 64:(hh + 1) * 64, :, :], yy2[:, :, hh, :], bc(sg2[:, :, hh].unsqueeze(2), [64, 3, 128]), ALU.mult, [yy, p["ssm_g"]], [self.mixC])

    def ffn_phase(self, l):
        S, NT = self.S, self.NT
        din = self.din
        S.barrier()
        A = self.arena
        A.reset()
        w_gu = A.carve("w_gu", [128, 8, 2 * D_FF], BF16)
        w_dn = A.carve("w_dn", [128, 22, 1024], BF16)
        for c in range(8):
            S.dma('pool', w_gu[:, c, :], din["w_gu"][l, :, c, :], writes=[w_gu], sembuf=w_gu)
        for c0 in range(0, 22, 6):
            c1 = min(22, c0 + 6)
            S.dma('pool', w_dn[:, c0:c1, :], din["w_dn"][l, :, c0:c1, :], writes=[w_dn], sembuf=w_dn)
        S.dma('sp', self.gA[:], din["g4"][l, 2].partition_broadcast(128), writes=[self.gA], sembuf=self.gA)
        S.dma('sp', self.gB[:], din["g4"][l, 3].partition_broadcast(128), writes=[self.gB], sembuf=self.gB)
        actb = A.carve("actb", [128, D_FF], BF16)
        self.otmp = Tile(actb[:, 0:2048].bitcast(F32), "otmp", actb.b)
        actT = A.carve("actT", [128, 22, 128], BF16)
        sg = [A.carve("sg%d" % i, [128, 512], F32) for i in range(2)]
        widths = [512] * 5 + [256]
        S.mark('ffn start')
        for i in range(NT + 1):
            xt = self.xb[0]
            self.load_x(l, 1, i, xt)
            self.norm_T(xt, self.gA)
            hT = self.hT
            off = 0
            for j, w in enumerate(widths):
                pg, pu = self.gps(), self.gps()
                for c in range(8):
                    self.mm(pg[:, 0:w], hT[:, c, :], w_gu[:, c, off:off + w], c == 0, c == 7, [hT, w_gu], [pg])
                for c in range(8):
                    self.mm(pu[:, 0:w], hT[:, c, :], w_gu[:, c, D_FF + off:D_FF + off + w], c == 0, c == 7, [hT, w_gu], [pu])
                s = sg[j % 2]
                self.act(s[:, 0:w], pg[:, 0:w], AF.Silu, [pg], [s])
                self.tt('dve', actb[:, off:off + w], s[:, 0:w], pu[:, 0:w], ALU.mult, [s, pu], [actb])
                off += w
            for k0 in range(0, 22, 8):
                k1 = min(22, k0 + 8)
                pT = self.tps()
                pv = pT[:].rearrange("p (a b) -> p a b", b=128)
                for k in range(k0, k1):
                    self.tr(pv[:, k - k0, :], actb[:, k * 128:(k + 1) * 128], self.identb[:], [actb, self.identb], [pT])
                self.cp('act' if (k0 // 8) % 2 == 0 else 'dve', actT[:, k0:k1, :], pv[:, 0:k1 - k0, :], [pT], [actT])
            po = [self.gps(), self.gps()]
            for hf in range(2):
                for k in range(22):
                    self.mm(po[hf][:], actT[:, k, :], w_dn[:, k, hf * 512:(hf + 1) * 512], k == 0, k == 21, [actT, w_dn], [po[hf]])
            self.out_norm_residual(xt, po, self.gB)
            self.store_x(l, 1, i, xt)


def _mult(dist):
    dist = np.asarray(dist)
    m = ((dist >= 0) & (dist <= 128)).astype(np.float32)
    m += ((dist >= 0) & (dist <= 512) & (dist % 4 == 0))
    m += ((dist >= 0) & (dist <= 2048) & (dist % 16 == 0))
    return m.astype(np.float32)


def _consts(CB):
    LB = CB * 128
    kl = np.arange(128)[:, None, None]
    o = np.arange(17)[None, :, None]
    ql = np.arange(128)[None, None, :]
    maskP = _mult(ql + 128 * o - kl)
    blk = np.arange(CB + 1)[None, :, None]
    t = np.arange(8)[None, None, :]
    r = blk * 128 + kl
    maskS = _mult(LB + t - r)
    maskS[8:, CB, :] = 0.0
    maskS = maskS.astype(np.float32)

    def ssdc(Ls):
        k = np.arange(128)
        seg = k // Ls
        same = (seg[:, None] == seg[None, :])
        tri = same & (k[:, None] <= k[None, :])
        neg = np.where(tri, 0.0, -30000.0)
        return np.stack([tri.astype(np.float32), neg.astype(np.float32), same.astype(np.float32)], 1)
    segm = (np.arange(128)[:, None] // TS == np.arange(NB)[None, :]).astype(np.float32)
    return dict(ident=np.eye(128, dtype=np.float32), maskP=np.ascontiguousarray(maskP), maskS=maskS,
                ssdc_p=np.ascontiguousarray(ssdc(128)), ssdc_s=np.ascontiguousarray(ssdc(TS)), segm_s=segm)


def _ct(a, P):
    sh = a.shape
    n = sh[-1] // P
    return np.moveaxis(a.reshape(sh[:-1] + (n, P)), -1, -2)


_CACHE = {}


def _RUN(nc, in_maps, core_ids):
    return run_bass_kernel_spmd(nc, in_maps, core_ids=core_ids)


def kernel(x_prompt, x_sample, state_lru_h, state_lru_conv, cache_swa_k, cache_swa_v, state_ssd, state_ssd_conv,
           norm_mix_in, norm_mix_out, w_in, conv_a_w, conv_a_b, lru_wa, lru_ba, lru_wx, lru_bx, lru_lambda,
           conv_c_w, conv_c_b, dt_bias, a_log, d_skip, ssm_norm, w_out, norm_ffn_in, norm_ffn_out,
           w_gate_up, w_down):
    f = lambda a: np.ascontiguousarray(np.asarray(a, dtype=np.float32))
    x_prompt, x_sample = f(x_prompt), f(x_sample)
    BATCH, SEQ, _ = x_prompt.shape
    L = w_in.shape[0]
    DB = x_sample.shape[0]
    LB = cache_swa_k.shape[2]
    NT, CB = SEQ // 128, LB // 128
    assert DB == NB * NCORES and x_sample.shape[1] == TS and BATCH * 4 == NCORES
    key = (NT, CB, L)
    if key not in _CACHE:
        bld = Builder(NT, CB, L)
        bld.build()
        _CACHE[key] = bld
    bld = _CACHE[key]
    WT = bld.WT

    sh = {}
    sh["w_in"] = f(np.asarray(w_in).reshape(L, 8, 128, N_IN).transpose(0, 2, 1, 3))
    wo = np.asarray(w_out)
    sh["w_outA"] = f(wo[:, 0:640].reshape(L, 5, 128, 1024).transpose(0, 2, 1, 3))
    sh["w_outC"] = f(wo[:, 640:1024].reshape(L, 3, 128, 1024).transpose(0, 2, 1, 3))
    sh["w_gu"] = f(np.asarray(w_gate_up).reshape(L, 8, 128, 2 * D_FF).transpose(0, 2, 1, 3))
    sh["w_dn"] = f(np.asarray(w_down).reshape(L, 22, 128, 1024).transpose(0, 2, 1, 3))
    sh["g4"] = f(np.stack([norm_mix_in, norm_mix_out, norm_ffn_in, norm_ffn_out], 1))
    sh["cva_w"] = f(_ct(np.asarray(conv_a_w), 128).transpose(0, 2, 3, 1))
    sh["cva_b"] = f(_ct(np.asarray(conv_a_b), 128))
    ccw, ccb = np.asarray(conv_c_w), np.asarray(conv_c_b)
    sh["cvx_w"] = f(_ct(ccw[:, :, 0:384], 64).transpose(0, 2, 3, 1))
    sh["cvx_b"] = f(_ct(ccb[:, 0:384], 64))
    sh["cvbc_w"] = f(_ct(ccw[:, :, 384:896], 128).transpose(0, 2, 3, 1))
    sh["cvbc_b"] = f(_ct(ccb[:, 384:896], 128))
    def bd(w):
        w = np.asarray(w)
        o = np.zeros((L, 128, 2, 128), np.float32)
        for t in range(2):
            for q in range(2):
                o[:, q * 64:(q + 1) * 64, t, q * 64:(q + 1) * 64] = w[:, t * 2 + q]
        return o
    sh["wa_bd"], sh["wx_bd"] = bd(lru_wa), bd(lru_wx)
    sh["lru_vec"] = f(np.stack([_ct(np.asarray(lru_ba), 128), _ct(np.asarray(lru_bx), 128), _ct(np.asarray(lru_lambda), 128)], -1))
    sh["ssd_h"] = f(np.concatenate([dt_bias, a_log, d_skip], 1))
    sh["ssm_g"] = f(_ct(np.asarray(ssm_norm), 64))
    sh.update(_consts(CB))

    slh, slc = np.asarray(state_lru_h), np.asarray(state_lru_conv)
    ssc, sss = np.asarray(state_ssd_conv), np.asarray(state_ssd)
    ck, cv = np.asarray(cache_swa_k), np.asarray(cache_swa_v)
    in_maps = []
    for c in range(NCORES):
        bs = slice(c * NB, (c + 1) * NB)
        m = dict(sh)
        m["xp"] = x_prompt[c // 4].reshape(NT, 128, 1024)
        m["xs"] = x_sample[bs].reshape(128, 1024)
        m["st_lru_h"] = f(_ct(slh[:, bs], 128).transpose(0, 2, 3, 1))
        m["st_lru_cv"] = f(_ct(slc[:, bs], 128).transpose(0, 3, 4, 1, 2))
        m["st_cvx"] = f(_ct(ssc[:, bs, :, 0:384], 64).transpose(0, 3, 4, 1, 2))
        m["st_cvbc"] = f(_ct(ssc[:, bs, :, 384:896], 128).transpose(0, 3, 4, 1, 2))
        m["st_ssd"] = f(sss[:, bs].transpose(0, 4, 1, 2, 3).reshape(L, 128, NB, 384))
        m["kT_c"] = f(ck[:, bs].reshape(L, NB, LB, 3, 128).transpose(0, 1, 3, 4, 2))
        m["v_c"] = f(cv[:, bs].reshape(L, NB, LB, 384))
        in_maps.append(m)

    res = _RUN(bld.nc, in_maps, core_ids=list(range(NCORES)))
    R = res.results
    g = lambda c, n: np.asarray(R[min(c, len(R) - 1)][n], dtype=np.float32)

    def ct_inv(a, caxis, taxis):
        a = np.moveaxis(a, (taxis, caxis), (-2, -1))
        return a.reshape(a.shape[:-2] + (-1,))
    pc = [0, 4]
    y_p = np.stack([g(c, "y_p").reshape(SEQ, 1024) for c in pc], 0)
    y_s = np.concatenate([g(c, "y_s").reshape(NB, TS, 1024) for c in range(NCORES)], 0)

    def lru_h(n, cores):
        return np.concatenate([ct_inv(g(c, n), 1, 2) for c in cores], 1)

    def cv_out(n, cores):
        return np.concatenate([ct_inv(g(c, n), 1, 2) for c in cores], 1)
    p_lru_h = lru_h("o_lru_h_p", pc)
    s_lru_h = lru_h("o_lru_h_s", range(NCORES))
    p_lru_conv = cv_out("o_lru_cv_p", pc)
    s_lru_conv = cv_out("o_lru_cv_s", range(NCORES))
    keep = WT * 128
    p_k = np.stack([g(c, "o_k_p").reshape(L, keep, 6, 64) for c in pc], 1)
    p_v = np.stack([g(c, "o_v_p").reshape(L, keep, 6, 64) for c in pc], 1)
    s_k = np.concatenate([g(c, "o_k_s").reshape(L, NB, TS, 6, 64) for c in range(NCORES)], 1)
    s_v = np.concatenate([g(c, "o_v_s").reshape(L, NB, TS, 6, 64) for c in range(NCORES)], 1)

    def ssd_out(n, cores):
        return np.concatenate([g(c, n).reshape(L, 128, -1, 6, 64).transpose(0, 2, 3, 4, 1) for c in cores], 1)
    p_ssd = ssd_out("o_ssd_p", pc)
    s_ssd = ssd_out("o_ssd_s", range(NCORES))

    def scv(sfx, cores):
        return np.concatenate([np.concatenate([ct_inv(g(c, "o_cvx" + sfx), 1, 2), ct_inv(g(c, "o_cvbc" + sfx), 1, 2)], -1) for c in cores], 1)
    p_ssd_conv = scv("_p", pc)
    s_ssd_conv = scv("_s", range(NCORES))
    outs = (y_p, y_s, p_lru_h, p_lru_conv, p_k, p_v, p_ssd, p_ssd_conv,
            s_lru_h, s_lru_conv, s_k, s_v, s_ssd, s_ssd_conv)
    return tuple(np.ascontiguousarray(o, dtype=np.float32) for o in outs)
```

```python
import os
import numpy as np
import concourse.bass as bass
import concourse.mybir as mybir
from concourse.bass_utils import run_bass_kernel_spmd
from contextlib import ExitStack

F32 = mybir.dt.float32
BF16 = mybir.dt.bfloat16
AF = mybir.ActivationFunctionType
ALU = mybir.AluOpType

D_MODEL = 1024
N_IN = 2950
D_FF = 2816
EPS = 1e-6
TS = 8
NB = 16
NCORES = 8


class Buf:
    def __init__(self, name):
        self.name = name
        self.w = None
        self.r = {}
        self.dsem = None
        self.dcnt = 0
        self.excl = False


class Tile:
    def __init__(self, ap, name, buf=None):
        self.ap = ap
        self.b = buf if buf is not None else Buf(name)

    def __getitem__(self, k):
        return self.ap[k]


class Sched:
    def __init__(self, nc, es):
        self.nc = nc
        self.es = es
        self.names = ['pe', 'act', 'dve', 'pool', 'sp']
        self.sem = {e: es.enter_context(nc.semaphore('s_' + e)) for e in self.names}
        self.cnt = {e: 0 for e in self.names}
        self.seen = {e: {} for e in self.names}
        self.q = {e: [] for e in self.names}
        self.dtoks = {}
        self.nsem = 5
        self.ninst = 0
        self.gseq = 0
        self.limit = int(os.environ.get('KLIMIT', '0'))
        self.marks = []

    def sb(self, name, shape, dt):
        t = self.es.enter_context(self.nc.sbuf_tensor(name, list(shape), dt))
        return Tile(t[tuple(slice(None) for _ in shape)], name)

    def ps(self, name, shape, dt):
        t = self.es.enter_context(self.nc.psum_tensor(name, list(shape), dt))
        r = Tile(t[tuple(slice(None) for _ in shape)], name)
        r.b.excl = True
        return r

    @staticmethod
    def _bufs(xs):
        return [x.b if isinstance(x, Tile) else x for x in xs]

    def _deps(self, e, reads, writes, skip_sem=None):
        need = {}

        def add(tok):
            s, v = tok
            k = id(s)
            if k not in need or need[k][1] < v:
                need[k] = (s, v)
        for b in reads:
            if b.w:
                add(b.w)
        for b in writes:
            if b.w and not (skip_sem is not None and b.w[0] is skip_sem):
                add(b.w)
            for tok in b.r.values():
                add(tok)
        out = []
        for k, (s, v) in need.items():
            if e == 'pe' and s is self.sem['pe']:
                continue
            if self.seen[e].get(k, 0) < v:
                self.seen[e][k] = v
                out.append((s, v))
        return out

    @staticmethod
    def _mark(tok, reads, writes):
        s, v = tok
        for b in reads:
            b.r[id(s)] = tok
        for b in writes:
            b.w = tok
            b.r = {}

    def op(self, e, fn, reads=(), writes=()):
        reads = self._bufs(reads)
        writes = self._bufs(writes)
        if e != 'pe':
            writes = writes + [b for b in reads if b.excl and b not in writes]
            reads = [b for b in reads if not b.excl]
        waits = self._deps(e, reads, writes)
        self.cnt[e] += 1
        tok = (self.sem[e], self.cnt[e])
        self._mark(tok, reads, writes)
        self.gseq += 1
        self.q[e].append((waits, fn, (self.sem[e], 1), self.gseq))
        self.ninst += 1 + len(waits)
        return tok

    def dma(self, e, out_ap, in_ap, reads=(), writes=(), sembuf=None):
        reads = self._bufs(reads)
        writes = self._bufs(writes)
        sb = sembuf.b if isinstance(sembuf, Tile) else sembuf
        if sb.dsem is None:
            sb.dsem = self.es.enter_context(self.nc.semaphore('d%d_%s' % (self.nsem, sb.name)))
            self.nsem += 1
        waits = self._deps(e, reads, writes, skip_sem=sb.dsem)
        sb.dcnt += 16
        tok = (sb.dsem, sb.dcnt)
        self._mark(tok, reads, writes)
        self.dtoks[id(sb.dsem)] = tok
        self.gseq += 1
        self.q[e].append((waits, lambda eng: eng.dma_start(out=out_ap, in_=in_ap), (sb.dsem, 16), self.gseq))
        self.ninst += 1 + len(waits)
        return tok

    def mark(self, label):
        self.marks.append((label, self.gseq))

    def wait_tok(self, e, tok):
        s, v = tok
        if e == 'pe' and s is self.sem['pe']:
            return
        if self.seen[e].get(id(s), 0) < v:
            self.seen[e][id(s)] = v
            self.gseq += 1
            self.q[e].append(([(s, v)], None, None, self.gseq))
            self.ninst += 1

    def barrier(self, engines=None):
        engines = engines or self.names
        toks = [(self.sem[o], self.cnt[o]) for o in self.names if self.cnt[o] > 0]
        toks += list(self.dtoks.values())
        for e in engines:
            for tok in toks:
                if tok[0] is self.sem.get(e):
                    continue
                self.wait_tok(e, tok)

    def emit(self):
        nc = self.nc
        S = self
        eng_of = {'pe': 'tensor', 'act': 'scalar', 'dve': 'vector', 'pool': 'gpsimd', 'sp': 'sync'}
        with nc.Block() as block:
            def mk(name):
                def run(eng):
                    for waits, fn, inc, seq in S.q[name]:
                        if S.limit and seq > S.limit:
                            break
                        for (s, v) in waits:
                            eng.wait_ge(s, v)
                        if fn is not None:
                            fn(eng).then_inc(inc[0], inc[1])
                return run
            for name in S.names:
                getattr(block, eng_of[name])(mk(name))


class Arena:
    def __init__(self, S, name, nel):
        self.t = S.sb(name, [128, nel], BF16)
        self.nel = nel
        self.off = 0
        self.hi = 0
        self.bufs = {}

    def reset(self):
        self.off = 0

    def carve(self, name, shape, dt):
        n = int(np.prod(shape[1:]))
        nel = n * (2 if dt == F32 else 1)
        if self.off % 2:
            self.off += 1
        assert self.off + nel <= self.nel, ("arena overflow", name, self.off + nel, self.nel)
        ap = self.t.ap[0:shape[0], self.off:self.off + nel]
        if dt == F32:
            ap = ap.bitcast(F32)
        if len(shape) == 3:
            ap = ap.rearrange("p (a b) -> p a b", b=shape[2])
        elif len(shape) == 4:
            ap = ap.rearrange("p (a b c) -> p a b c", b=shape[2], c=shape[3])
        self.off += nel
        self.hi = max(self.hi, self.off)
        if name not in self.bufs:
            self.bufs[name] = Buf(name)
        return Tile(ap, name, self.bufs[name])


def bc(ap, shape):
    return ap.to_broadcast(list(shape))


class Builder:
    def __init__(self, NT, CB, DEPTH):
        self.NT, self.CB, self.DEPTH = NT, CB, DEPTH
        self.WT = min(16, NT)
        self.nc = bass.Bass("TRN2", target_bir_lowering=False)
        self.din = {}
        self.dout = {}

    def I(self, name, shape, dt=F32):
        self.din[name] = self.nc.dram_tensor(name, list(shape), dt, kind="ExternalInput").ap()
        return self.din[name]

    def O(self, name, shape, dt=F32):
        self.dout[name] = self.nc.dram_tensor(name, list(shape), dt, kind="ExternalOutput").ap()
        return self.dout[name]

    def mm(self, out, lhsT, rhs, start, stop, reads, writes):
        self.S.op('pe', lambda e: e.matmul(out, lhsT, rhs, start=start, stop=stop), reads, writes)

    def tr(self, out, in_, ident, reads, writes):
        self.S.op('pe', lambda e: e.transpose(out, in_, ident), reads, writes)

    def act(self, out, in_, func, reads, writes, **kw):
        self.S.op('act', lambda e: e.activation(out, in_, func, **kw), reads, writes)

    def tt(self, eng, out, a, b, op, reads, writes):
        self.S.op(eng, lambda e: e.tensor_tensor(out, a, b, op), reads, writes)

    def tsc(self, eng, out, a, s1, s2, op0, op1, reads, writes):
        if s2 is None:
            self.S.op(eng, lambda e: e.tensor_scalar(out, a, s1, None, op0), reads, writes)
        else:
            self.S.op(eng, lambda e: e.tensor_scalar(out, a, s1, s2, op0, op1), reads, writes)

    def stt(self, out, in0, scalar, in1, op0, op1, reads, writes):
        self.S.op('dve', lambda e: e.scalar_tensor_tensor(out, in0, scalar, in1, op0, op1), reads, writes)

    def cp(self, eng, out, in_, reads, writes):
        if eng == 'act':
            self.S.op('act', lambda e: e.copy(out, in_), reads, writes)
        else:
            self.S.op(eng, lambda e: e.tensor_copy(out, in_), reads, writes)

    def memset(self, eng, ap, val, writes):
        self.S.op(eng, lambda e: e.memset(ap, val), (), writes)

    def recip(self, out, in_, reads, writes):
        self.S.op('dve', lambda e: e.reciprocal(out, in_), reads, writes)

    def gps(self):
        t = self.gbanks[self.gi % len(self.gbanks)]
        self.gi += 1
        return t

    def tps(self):
        t = self.tbanks[self.ti_ % len(self.tbanks)]
        self.ti_ += 1
        return t

    def build(self):
        NT, CB, L, WT = self.NT, self.CB, self.DEPTH, self.WT
        LB = CB * 128
        I, O = self.I, self.O
        I("xp", [NT, 128, 1024]); I("xs", [128, 1024])
        I("w_in", [L, 128, 8, N_IN]); I("w_outA", [L, 128, 5, 1024]); I("w_outC", [L, 128, 3, 1024])
        I("w_gu", [L, 128, 8, 2 * D_FF]); I("w_dn", [L, 128, 22, 1024])
        I("g4", [L, 4, 1024])
        I("cva_w", [L, 128, 2, 4]); I("cva_b", [L, 128, 2])
        I("cvx_w", [L, 64, 6, 4]); I("cvx_b", [L, 64, 6])
        I("cvbc_w", [L, 128, 4, 4]); I("cvbc_b", [L, 128, 4])
        I("wa_bd", [L, 128, 2, 128]); I("wx_bd", [L, 128, 2, 128]); I("lru_vec", [L, 128, 2, 3])
        I("ssd_h", [L, 18]); I("ssm_g", [L, 64, 6])
        I("st_lru_h", [L, 128, 2, NB]); I("st_lru_cv", [L, 128, 2, NB, 3])
        I("st_cvx", [L, 64, 6, NB, 3]); I("st_cvbc", [L, 128, 4, NB, 3])
        I("st_ssd", [L, 128, NB, 384]); I("kT_c", [L, NB, 3, 128, LB]); I("v_c", [L, NB, LB, 384])
        I("ident", [128, 128]); I("maskP", [128, 17, 128]); I("maskS", [128, CB + 1, 8])
        I("ssdc_p", [128, 3, 128]); I("ssdc_s", [128, 3, 128]); I("segm_s", [128, NB])
        O("y_p", [NT, 128, 1024]); O("y_s", [128, 1024])
        O("o_lru_h_p", [L, 128, 2, 1]); O("o_lru_h_s", [L, 128, 2, NB])
        O("o_lru_cv_p", [L, 128, 2, 1, 3]); O("o_lru_cv_s", [L, 128, 2, NB, 3])
        O("o_k_p", [L, WT, 128, 384]); O("o_v_p", [L, WT, 128, 384])
        O("o_k_s", [L, 128, 384]); O("o_v_s", [L, 128, 384])
        O("o_ssd_p", [L, 128, 1, 384]); O("o_ssd_s", [L, 128, NB, 384])
        O("o_cvx_p", [L, 64, 6, 1, 3]); O("o_cvx_s", [L, 64, 6, NB, 3])
        O("o_cvbc_p", [L, 128, 4, 1, 3]); O("o_cvbc_s", [L, 128, 4, NB, 3])
        self.xscr = self.nc.dram_tensor("xscr", [NT + 1, 128, 1024], F32, kind="Internal").ap()
        self.xscr_b = [Buf("xscr%d" % i) for i in range(NT + 1)]
        self.out_toks = []

        with ExitStack() as es:
            S = self.S = Sched(self.nc, es)
            self.identf = S.sb("identf", [128, 128], F32)
            self.identb = S.sb("identb", [128, 128], BF16)
            self.onesf = S.sb("onesf", [128, 128], F32)
            self.segm = S.sb("segm_sb", [128, NB], F32)
            self.gA = S.sb("gA", [128, 1024], F32)
            self.gB = S.sb("gB", [128, 1024], F32)
            self.xb = [S.sb("xt0", [128, 1024], F32)] * 2
            self.hn = S.sb("hn", [128, 1024], BF16)
            self.hT = S.sb("hT", [128, 8, 128], BF16)
            self.junk = self.hn
            self.sm = S.sb("small", [128, 64], F32)
            self.sm_b = [Buf("sm%d" % i) for i in range(16)]
            self.arena = Arena(S, "arena", 80500)
            self.gbanks = [S.ps("pg%d" % i, [128, 512], F32) for i in range(4)]
            self.tbanksf = [S.ps("pt%d" % i, [128, 512], F32) for i in range(2)]
            self.tbanks = [Tile(t.ap.bitcast(BF16), "ptb%d" % i, t.b) for i, t in enumerate(self.tbanksf)]
            self.plong = [S.ps("plong%d" % i, [128, 512], F32) for i in range(2)]
            self.pacc = Tile(self.plong[0][:, 0:390].rearrange("p (h d) -> p h d", d=65), "pacc", self.plong[0].b)
            self.gi = 0
            self.ti_ = 0
            S.dma('sp', self.identf[:], self.din["ident"], writes=[self.identf], sembuf=self.identf)
            S.dma('pool', self.identb[:], self.din["ident"], writes=[self.identb], sembuf=self.identb)
            S.dma('sp', self.segm[:], self.din["segm_s"], writes=[self.segm], sembuf=self.segm)
            self.memset('pool', self.onesf[:], 1.0, [self.onesf])

            for l in range(L):
                self.mixer_phase(l)
                self.ffn_phase(l)
            for tok in self.out_toks:
                S.wait_tok('sp', tok)
            S.barrier(['sp'])
            self.ninst = S.ninst
            S.emit()
        return self.nc

    def x_src(self, l, phase, i):
        if l == 0 and phase == 0:
            return (self.din["xp"][i] if i < self.NT else self.din["xs"]), None
        return self.xscr[i], self.xscr_b[i]

    def x_dst(self, l, phase, i):
        if l == self.DEPTH - 1 and phase == 1:
            return (self.dout["y_p"][i] if i < self.NT else self.dout["y_s"]), None
        return self.xscr[i], self.xscr_b[i]

    def load_x(self, l, phase, i, xt):
        ap, db = self.x_src(l, phase, i)
        self.S.dma('sp', xt[:], ap, reads=([db] if db else []), writes=[xt], sembuf=xt)

    def store_x(self, l, phase, i, xt):
        ap, db = self.x_dst(l, phase, i)
        tok = self.S.dma('sp', ap, xt[:], reads=[xt], writes=([db] if db else []), sembuf=xt)
        if db is None:
            self.out_toks.append(tok)

    def norm_T(self, xt, g_bc):
        sm, hn, hT = self.sm, self.hn, self.hT
        b0 = self.sm_b[0]
        self.act(self.junk[:], xt[:], AF.Square, [xt], [self.junk, b0], accum_out=sm[:, 0:1])
        self.tsc('dve', sm[:, 0:1], sm[:, 0:1], 1.0 / D_MODEL, EPS, ALU.mult, ALU.add, [b0], [b0])
        self.act(sm[:, 0:1], sm[:, 0:1], AF.Sqrt, [b0], [b0])
        self.recip(sm[:, 0:1], sm[:, 0:1], [b0], [b0])
        self.stt(hn[:], xt[:], sm[:, 0:1], g_bc[:], ALU.mult, ALU.mult, [xt, b0, g_bc], [hn])
        pT = self.tps()
        pv = pT[:].rearrange("p (a b) -> p a b", b=128)
        for c in range(8):
            self.tr(pv[:, c, :], hn[:, c * 128:(c + 1) * 128], self.identb[:], [hn, self.identb], [pT])
        self.cp('act', hT[:], pv, [pT], [hT])

    def out_norm_residual(self, xt, pbanks, g_bc):
        sm = self.sm
        b1 = self.sm_b[1]
        self.act(self.junk[:, 0:512], pbanks[0][:], AF.Square, [pbanks[0]], [self.junk, b1], accum_out=sm[:, 1:2])
        self.act(self.junk[:, 512:1024], pbanks[1][:], AF.Square, [pbanks[1]], [self.junk, b1], accum_out=sm[:, 2:3])
        self.tt('dve', sm[:, 1:2], sm[:, 1:2], sm[:, 2:3], ALU.add, [b1], [b1])
        self.tsc('dve', sm[:, 1:2], sm[:, 1:2], 1.0 / D_MODEL, EPS, ALU.mult, ALU.add, [b1], [b1])
        self.act(sm[:, 1:2], sm[:, 1:2], AF.Sqrt, [b1], [b1])
        self.recip(sm[:, 1:2], sm[:, 1:2], [b1], [b1])
        tmp = self.otmp
        for hf in range(2):
            sl = slice(hf * 512, (hf + 1) * 512)
            self.stt(tmp[:, sl], pbanks[hf][:], sm[:, 1:2], g_bc[:, sl], ALU.mult, ALU.mult, [pbanks[hf], b1, g_bc], [tmp])
        self.tt('pool', xt[:], xt[:], tmp[:], ALU.add, [xt, tmp], [xt])

    def mixer_phase(self, l):
        S, NT, CB = self.S, self.NT, self.CB
        din = self.din
        S.barrier()
        A = self.arena
        A.reset()
        self.w_in = A.carve("w_in", [128, 8, N_IN], BF16)
        self.w_oA = A.carve("w_oA", [128, 5, 1024], BF16)
        self.w_oC = A.carve("w_oC", [128, 3, 1024], BF16)
        for c in range(8):
            S.dma('pool', self.w_in[:, c, :], din["w_in"][l, :, c, :], writes=[self.w_in], sembuf=self.w_in)
        S.dma('pool', self.w_oA[:], din["w_outA"][l], writes=[self.w_oA], sembuf=self.w_oA)
        S.dma('pool', self.w_oC[:], din["w_outC"][l], writes=[self.w_oC], sembuf=self.w_oC)
        S.dma('sp', self.gA[:], din["g4"][l, 0].partition_broadcast(128), writes=[self.gA], sembuf=self.gA)
        S.dma('sp', self.gB[:], din["g4"][l, 1].partition_broadcast(128), writes=[self.gB], sembuf=self.gB)
        p = self.prm = {}
        def ld(name, shape, src, dt=F32, q='sp'):
            t = A.carve(name, shape, dt)
            S.dma(q, t[tuple(slice(None) for _ in shape)], src, writes=[t], sembuf=t)
            p[name] = t
            return t
        ld("cva_w", [128, 2, 4], din["cva_w"][l]); ld("cva_b", [128, 2], din["cva_b"][l])
        ld("cvx_w", [64, 6, 4], din["cvx_w"][l]); ld("cvx_b", [64, 6], din["cvx_b"][l])
        ld("cvbc_w", [128, 4, 4], din["cvbc_w"][l]); ld("cvbc_b", [128, 4], din["cvbc_b"][l])
        ld("wa_bd", [128, 2, 128], din["wa_bd"][l], BF16, 'pool'); ld("wx_bd", [128, 2, 128], din["wx_bd"][l], BF16, 'pool')
        ld("lru_vec", [128, 2, 3], din["lru_vec"][l])
        ld("ssd_h", [128, 18], din["ssd_h"][l].partition_broadcast(128))
        ld("ssm_g", [64, 6], din["ssm_g"][l])
        self.maskP = ld("maskP", [128, 17, 128], din["maskP"], BF16, 'pool')
        self.maskS = ld("maskS", [128, CB + 1, 8], din["maskS"])
        self.ssdc = {True: ld("ssdc_p", [128, 3, 128], din["ssdc_p"]), False: ld("ssdc_s", [128, 3, 128], din["ssdc_s"])}
        cl = p["cl"] = A.carve("cl", [128, 2], F32)
        lam = p["lru_vec"][:, :, 2]
        self.act(cl[:], lam, AF.Exp, [p["lru_vec"]], [cl], scale=-1.0)
        self.act(cl[:], cl[:], AF.Ln, [cl], [cl], bias=1.0)
        self.tsc('dve', cl[:], cl[:], -8.0, None, ALU.mult, None, [cl], [cl])
        An = p["Aneg"] = A.carve("Aneg", [128, 6], F32)
        self.act(An[:], p["ssd_h"][:, 6:12], AF.Exp, [p["ssd_h"]], [An])
        self.tsc('dve', An[:], An[:], -1.0, None, ALU.mult, None, [An], [An])
        S.mark('mixer params')
        c = A.carve
        off0 = A.off
        self.KT = c("KT", [128, 3, 17, 128], BF16)
        self.KT_b = [Buf("KT%d" % i) for i in range(17)]
        self.V1 = c("V1", [128, 17, 6, 65], BF16)
        self.V1_b = [Buf("V1%d" % i) for i in range(17)]
        self.ebuf = [c("ebuf%d" % i, [128, 4, 128], BF16) for i in range(2)]
        self.pbuf = [c("pbuf%d" % i, [128, 4, 128], BF16) for i in range(2)]
        self.otok = c("otok", [128, 6, 64], BF16)
        self.st_p = c("st_p", [128, 384], F32)
        end_p = A.off
        A.off = off0
        self.KTc = c("KTc", [128, CB * 128], BF16)
        self.V1c = c("V1c", [128, CB + 1, 2, 65], BF16)
        self.es_ = c("es_", [128, CB + 1, 8], F32)
        self.ps_ = c("ps_", [128, CB + 1, 8], BF16)
        self.ob = c("ob", [8, 384], BF16)
        self.vnew = c("vnew", [8, 384], BF16)
        self.stb = [c("stb%d" % i, [128, 384], F32) for i in range(2)]
        self.xm = c("xm", [128, 384], BF16)
        self.KTn = c("KTn", [128, 3, 128], BF16)
        self.vtokb = c("vtokb", [128, 384], BF16)
        A.off = max(A.off, end_p)
        self.QT = c("QT", [128, 3, 128], BF16)
        self.kvst = [c("kvst%d" % i, [128, 384], F32) for i in range(2)]
        self.xpA = c("xpA", [128, 2, NB * (3 + TS)], F32)
        self.xpX = c("xpX", [64, 6, NB * (3 + TS)], F32)
        self.xpBC = c("xpBC", [128, 4, NB * (3 + TS)], F32)
        self.cA = c("cA", [128, 2, 128], F32)
        self.cX = c("cX", [64, 6, 128], F32)
        self.cBC = c("cBC", [128, 4, 128], F32)
        self.ctmp = c("ctmp", [128, 768], F32)
        self.ge = c("ge", [128, 2, 128], F32)
        self.xcb = c("xcb", [128, 2, 128], BF16)
        self.lr = c("lr", [128, 2, 128], F32)
        self.li = c("li", [128, 2, 128], F32)
        self.la = c("la", [128, 2, 128], F32)
        self.lu = c("lu", [128, 2, 128], F32)
        self.lh = c("lh", [128, 2, 128], F32)
        self.hst = {True: c("hst_p", [128, 2, 1], F32), False: c("hst_s", [128, 2, NB], F32)}
        self.lt = c("lt", [128, 2, NB], F32)
        self.mixT = c("mixT", [128, 5, 128], BF16)
        self.mixC = c("mixC", [128, 3, 128], BF16)
        self.rden = c("rden", [128, 6, 1], F32)
        self.sz = c("sz", [64, 6, 128], F32)
        self.xsTb = c("xsTb", [64, 6, 128], BF16)
        self.BCb = c("BCb", [128, 4, 128], BF16)
        self.dts = c("dts", [128, 4, 8], F32)
        self.xr = c("xr", [128, 6, 64], BF16)
        self.xrd = c("xrd", [128, 6, 64], BF16)
        self.Btok = c("Btok", [128, 256], BF16)
        RD = c("RD", [128, 12, 128], F32)
        self.R = Tile(RD[:, 0:6, :], "R", RD.b)
        self.CE = self.R
        self.Dm = Tile(RD[:, 6:12, :], "Dm", RD.b)
        self.otmp = Tile(RD[:, 0:8, :].rearrange("p a b -> p (a b)"), "otmp", RD.b)
        self.Eac = c("Eac", [128, 6, 128], F32)
        self.GT = c("GT", [128, 6, 128], BF16)
        self.R2 = c("R2", [128, NB, 6], F32)
        self.etot = c("etot", [128, NB, 6], F32)
        self.yy = c("yy", [64, 6, 128], F32)
        self.yt = Tile(self.ctmp[0:64, 0:768].rearrange("p (a b) -> p a b", b=128), "yt", self.ctmp.b)
        self.rs = c("rs", [64, 128], F32)
        self.memset('pool', self.st_p[:], 0.0, [self.st_p])
        self.memset('pool', self.hst[True][:], 0.0, [self.hst[True]])
        self.memset('pool', self.V1[:, :, :, 64:65], 1.0, self.V1_b)
        S.dma('sp', self.hst[False][:], din["st_lru_h"][l], writes=[self.hst[False]], sembuf=self.hst[False])

        for i in range(NT + 1):
            if i == NT:
                S.barrier()
                self.memset('pool', self.V1c[:, :, :, 64:65], 1.0, [self.V1c])
            self.load_x(l, 0, i, self.xb[0])
            self.mixer_tile(l, i)

    def mixer_tile(self, l, i):
        S, NT, CB = self.S, self.NT, self.CB
        din, dout, p = self.din, self.dout, self.prm
        isp = i < NT
        nseg, Ls = (1, 128) if isp else (NB, TS)
        W = 3 + Ls
        last = (i == NT - 1)
        want_kv = (not isp) or (i >= NT - self.WT)
        xt = self.xb[i % 2]
        S.mark('tile%d start' % i)
        self.norm_T(xt, self.gA)
        S.mark('tile%d normT' % i)
        hT, w_in = self.hT, self.w_in

        def proj_fm(ps_ap, ps_t, c0, M):
            for c in range(8):
                self.mm(ps_ap, w_in[:, c, c0:c0 + M], hT[:, c, :], c == 0, c == 7, [w_in, hT], [ps_t])

        def seg4(t, n):
            return t[:, :, 0:nseg * W].rearrange("p a (s w) -> p a s w", w=W)

        xpA4, xpX4, xpBC4 = seg4(self.xpA, 2), seg4(self.xpX, 6), seg4(self.xpBC, 4)
        if isp:
            if i == 0:
                for t, v in ((self.xpA, xpA4), (self.xpX, xpX4), (self.xpBC, xpBC4)):
                    self.memset('pool', v[:, :, :, 0:3], 0.0, [t])
            else:
                for t, v in ((self.xpA, xpA4), (self.xpX, xpX4), (self.xpBC, xpBC4)):
                    self.cp('pool', v[:, :, :, 0:3], v[:, :, :, Ls:Ls + 3], [t], [t])
        else:
            S.dma('sp', xpA4[:, :, :, 0:3], din["st_lru_cv"][l], writes=[self.xpA], sembuf=self.xpA)
            S.dma('sp', xpX4[:, :, :, 0:3], din["st_cvx"][l], writes=[self.xpX], sembuf=self.xpX)
            S.dma('sp', xpBC4[:, :, :, 0:3], din["st_cvbc"][l], writes=[self.xpBC], sembuf=self.xpBC)

        pg1 = self.gps()
        v1 = pg1[:].rearrange("p (a b) -> p a b", b=128)
        for t in range(4):
            proj_fm(v1[:, t, :], pg1, t * 128, 128)
        self.act(self.ge[:], v1[:, 0:2, :], AF.Gelu_apprx_tanh, [pg1], [self.ge])
        self.cp('dve', xpA4[:, :, :, 3:W], v1[:, 2:4, :].rearrange("p a (s w) -> p a s w", w=Ls), [pg1], [self.xpA])
        pq = self.gps()
        vq = pq[:].rearrange("p (a b) -> p a b", b=128)
        for t in range(3):
            proj_fm(vq[:, t, :], pq, 512 + t * 128, 128)
        self.cp('act', self.QT[:], vq[:, 0:3, :], [pq], [self.QT])
        pk = self.gps()
        vk = pk[:].rearrange("p (a b) -> p a b", b=128)
        for t in range(3):
            proj_fm(vk[:, t, :], pk, 896 + t * 128, 128)
        slot = i % 17
        if isp:
            self.cp('dve', self.KT[:, :, slot, :], vk[:, 0:3, :], [pk], [self.KT_b[slot]])
        else:
            self.cp('dve', self.KTn[:], vk[:, 0:3, :], [pk], [self.KTn])
        pv = self.gps()
        for c in range(8):
            self.mm(pv[:, 0:384], hT[:, c, :], w_in[:, c, 1280:1664], c == 0, c == 7, [hT, w_in], [pv])
        for c in range(8):
            self.mm(pv[:, 384:390], hT[:, c, :], w_in[:, c, 2944:2950], c == 0, c == 7, [hT, w_in], [pv])
        pv3 = pv[:, 0:384].rearrange("p (h d) -> p h d", d=64)
        if isp:
            self.cp('act', self.V1[:, slot, :, 0:64], pv3, [pv], [self.V1_b[slot]])
        else:
            self.cp('act', self.vtokb[:], pv[:, 0:384], [pv], [self.vtokb])
        self.cp('dve', self.dts[:, 0, 0:6], pv[:, 384:390], [pv], [self.dts])
        if want_kv:
            vs = self.kvst[0]
            self.cp('act', vs[:], pv[:, 0:384], [pv], [vs])
            dst = dout["o_v_p"][l, i - (NT - self.WT)] if isp else dout["o_v_s"][l]
            self.out_toks.append(S.dma('sp', dst, vs[:], reads=[vs], sembuf=vs))
            pk2 = self.gps()
            for c in range(8):
                self.mm(pk2[:, 0:384], hT[:, c, :], w_in[:, c, 896:1280], c == 0, c == 7, [hT, w_in], [pk2])
            ks = self.kvst[1]
            self.cp('act', ks[:], pk2[:, 0:384], [pk2], [ks])
            dst = dout["o_k_p"][l, i - (NT - self.WT)] if isp else dout["o_k_s"][l]
            self.out_toks.append(S.dma('sp', dst, ks[:], reads=[ks], sembuf=ks))
        for hb in range(2):
            pz = self.gps()
            vz = pz[0:64, 0:384].rearrange("p (a b) -> p a b", b=128)
            for t in range(3):
                proj_fm(vz[:, t, :], pz, 1664 + (hb * 3 + t) * 64, 64)
            self.act(self.sz[:, hb * 3:hb * 3 + 3, :], vz, AF.Silu, [pz], [self.sz])
        for hb in range(2):
            px = self.gps()
            vx = px[0:64, 0:384].rearrange("p (a b) -> p a b", b=128)
            for t in range(3):
                proj_fm(vx[:, t, :], px, 2048 + (hb * 3 + t) * 64, 64)
            self.cp('dve', xpX4[:, hb * 3:hb * 3 + 3, :, 3:W], vx.rearrange("p a (s w) -> p a s w", w=Ls), [px], [self.xpX])
        pbc = self.gps()
        vbc = pbc[:].rearrange("p (a b) -> p a b", b=128)
        for t in range(4):
            proj_fm(vbc[:, t, :], pbc, 2432 + t * 128, 128)
        self.cp('act', xpBC4[:, :, :, 3:W], vbc.rearrange("p a (s w) -> p a s w", w=Ls), [pbc], [self.xpBC])

        S.mark('tile%d proj' % i)
        def conv(P, n, xp_t, xp4, wt, bt, out_t, eng):
            o4 = out_t[:].rearrange("p a (s w) -> p a s w", w=Ls)
            tmp = self.ctmp[0:P, 0:n * 128].rearrange("p (a s w) -> p a s w", a=n, w=Ls)
            shp = [P, n, nseg, Ls]
            self.tt(eng, o4, xp4[:, :, :, 0:Ls], bc(wt[:, :, 0:1].unsqueeze(3), shp), ALU.mult, [xp_t, wt], [out_t])
            for j in range(1, 4):
                self.tt(eng, tmp, xp4[:, :, :, j:j + Ls], bc(wt[:, :, j:j + 1].unsqueeze(3), shp), ALU.mult, [xp_t, wt], [self.ctmp])
                self.tt(eng, o4, o4, tmp, ALU.add, [out_t, self.ctmp], [out_t])
            self.tt(eng, o4, o4, bc(bt[:].unsqueeze(2).unsqueeze(3), shp), ALU.add, [out_t, bt], [out_t])
        conv(128, 2, self.xpA, xpA4, p["cva_w"], p["cva_b"], self.cA, 'pool')
        conv(64, 6, self.xpX, xpX4, p["cvx_w"], p["cvx_b"], self.cX, 'dve')
        conv(128, 4, self.xpBC, xpBC4, p["cvbc_w"], p["cvbc_b"], self.cBC, 'pool')
        if last or not isp:
            sfx = "_p" if isp else "_s"
            for nm, t, v in (("o_lru_cv", self.xpA, xpA4), ("o_cvx", self.xpX, xpX4), ("o_cvbc", self.xpBC, xpBC4)):
                self.out_toks.append(S.dma('sp', dout[nm + sfx][l], v[:, :, :, Ls:Ls + 3], reads=[t], sembuf=t))
        self.act(self.cX[:], self.cX[:], AF.Silu, [self.cX], [self.cX])
        self.act(self.cBC[:], self.cBC[:], AF.Silu, [self.cBC], [self.cBC])
        self.cp('pool', self.xsTb[:], self.cX[:], [self.cX], [self.xsTb])
        self.cp('pool', self.BCb[:], self.cBC[:], [self.cBC], [self.BCb])

        S.mark('tile%d conv' % i)
        gl = self.lru(l, i, isp, nseg, Ls, last)
        gs = self.ssd(l, i, isp, nseg, Ls, last)
        if isp:
            self.kctr = 0
            next(gl)
            self.attn_head(i, 0)
            next(gs)
            self.attn_head(i, 1)
            next(gl)
            self.attn_head(i, 2)
            next(gs)
            self.attn_head(i, 3)
            for _ in gl:
                pass
            self.attn_head(i, 4)
            self.attn_head(i, 5)
            self.attn_finish()
            for _ in gs:
                pass
        else:
            for _ in gl:
                pass
            self.attn_sample(l)
            for _ in gs:
                pass
        S.mark('tile%d mix' % i)

        po = [self.gps(), self.gps()]
        for hf in range(2):
            sl = slice(hf * 512, (hf + 1) * 512)
            for cidx in range(5):
                self.mm(po[hf][:], self.mixT[:, cidx, :], self.w_oA[:, cidx, sl], cidx == 0, False, [self.mixT, self.w_oA], [po[hf]])
            for h in range(3):
                self.mm(po[hf][:], self.mixC[:, h, :], self.w_oC[:, h, sl], False, h == 2, [self.mixC, self.w_oC], [po[hf]])
        self.out_norm_residual(xt, po, self.gB)
        self.store_x(l, 0, i, xt)
        S.mark('tile%d done' % i)

    def lru(self, l, i, isp, nseg, Ls, last):
        S, p = self.S, self.prm
        cA, xcb, lr, li, la, lu, lh = self.cA, self.xcb, self.lr, self.li, self.la, self.lu, self.lh
        hst = self.hst[isp]
        self.cp('pool', xcb[:], cA[:], [cA], [xcb])
        pr = self.gps()
        v = pr[:].rearrange("p (a b) -> p a b", b=128)
        for t in range(2):
            self.mm(v[:, t, :], p["wa_bd"][:, t, :], xcb[:, t, :], True, True, [p["wa_bd"], xcb], [pr])
            self.mm(v[:, 2 + t, :], p["wx_bd"][:, t, :], xcb[:, t, :], True, True, [p["wx_bd"], xcb], [pr])
        for t in range(2):
            self.act(lr[:, t, :], v[:, t, :], AF.Sigmoid, [pr, p["lru_vec"]], [lr], bias=p["lru_vec"][:, t, 0:1])
            self.act(li[:, t, :], v[:, 2 + t, :], AF.Sigmoid, [pr, p["lru_vec"]], [li], bias=p["lru_vec"][:, t, 1:2])
        for t in range(2):
            self.act(la[:, t, :], lr[:, t, :], AF.Exp, [lr, p["cl"]], [la], scale=p["cl"][:, t:t + 1])
        yield
        self.tt('pool', lr[:], la[:], la[:], ALU.mult, [la], [lr])
        self.tsc('pool', lr[:], lr[:], -1.0, 1.0, ALU.mult, ALU.add, [lr], [lr])
        self.act(lr[:], lr[:], AF.Sqrt, [lr], [lr])
        self.tt('dve', li[:], li[:], cA[:], ALU.mult, [li, cA], [li])
        self.tt('dve', lu[:], lr[:], li[:], ALU.mult, [lr, li], [lu])
        a0 = la[:].rearrange("p a (s w) -> p a s w", w=Ls)[:, :, :, 0]
        u0 = lu[:].rearrange("p a (s w) -> p a s w", w=Ls)[:, :, :, 0]
        lt = self.lt[:, :, 0:nseg]
        self.tt('dve', lt, a0, hst[:], ALU.mult, [la, hst], [self.lt])
        self.tt('dve', u0, u0, lt, ALU.add, [lu, self.lt], [lu])
        self.memset('dve', a0, 0.0, [la])
        for t in range(2):
            S.op('dve', (lambda t: (lambda e: e.tensor_tensor_scan(lh[:, t, :], la[:, t, :], lu[:, t, :], 0.0, ALU.mult, ALU.add)))(t), [la, lu], [lh])
        yield
        hl = lh[:].rearrange("p a (s w) -> p a s w", w=Ls)[:, :, :, Ls - 1]
        self.cp('pool', hst[:], hl, [lh], [hst])
        if last or not isp:
            dst = self.dout["o_lru_h_p" if isp else "o_lru_h_s"][l]
            self.out_toks.append(S.dma('sp', dst, hst[:], reads=[hst], sembuf=hst))
        self.tt('dve', self.mixT[:, 0:2, :], lh[:], self.ge[:], ALU.mult, [lh, self.ge], [self.mixT])

    def attn_head(self, i, h):
        pacc = self.pacc
        nk = min(16, i) + 1
        pr_, hh = h // 2, h % 2
        rows = slice(hh * 64, hh * 64 + 64)
        batches = [(o0, min(4, nk - o0)) for o0 in range(0, nk, 4)]

        def s_stage(bi):
            o0, nb = batches[bi]
            ps = self.gps()
            v = ps[:].rearrange("p (a b) -> p a b", b=128)
            for jj in range(nb):
                sj = (i - (o0 + jj)) % 17
                self.mm(v[:, jj, :], self.KT[rows, pr_, sj, :], self.QT[rows, pr_, :], True, True, [self.KT_b[sj], self.QT], [ps])
            return ps, v

        def rest(bi, ps, v):
            o0, nb = batches[bi]
            k = self.kctr
            self.kctr += 1
            eb, pb = self.ebuf[k % 2], self.pbuf[k % 2]
            self.act(eb[:, 0:nb, :], v[:, 0:nb, :], AF.Exp, [ps], [eb], scale=0.125)
            self.tt('dve' if k % 2 else 'pool', pb[:, 0:nb, :], eb[:, 0:nb, :], self.maskP[:, o0:o0 + nb, :], ALU.mult, [eb, self.maskP], [pb])
            for jj in range(nb):
                o = o0 + jj
                sj = (i - o) % 17
                self.mm(pacc[:, h, :], pb[:, jj, :], self.V1[:, sj, h, :], o == 0, o == nk - 1, [pb, self.V1_b[sj]], [pacc])
        cur = s_stage(0)
        for bi in range(len(batches)):
            nxt = s_stage(bi + 1) if bi + 1 < len(batches) else None
            rest(bi, *cur)
            cur = nxt

    def attn_finish(self):
        pacc = self.pacc
        self.recip(self.rden[:], pacc[:, :, 64:65], [pacc], [self.rden])
        self.tt('dve', self.otok[:], pacc[:, :, 0:64], bc(self.rden[:], [128, 6, 64]), ALU.mult, [pacc, self.rden], [self.otok])
        pT = self.tps()
        pv = pT[:].rearrange("p (a b) -> p a b", b=128)
        of = self.otok[:].rearrange("p h d -> p (h d)")
        for c in range(3):
            self.tr(pv[:, c, :], of[:, c * 128:(c + 1) * 128], self.identb[:], [self.otok, self.identb], [pT])
        self.cp('act', self.mixT[:, 2:5, :], pv[:, 0:3, :], [pT], [self.mixT])

    def attn_sample(self, l):
        S, CB = self.S, self.CB
        din = self.din
        pacc = self.pacc
        KTc, V1c = self.KTc, self.V1c
        for b in range(NB):
            cs = slice(b * 8, b * 8 + 8)
            pn = self.gps()
            self.mm(pn[0:8, 0:384], self.identb[:, cs], self.vtokb[:], True, True, [self.identb, self.vtokb], [pn])
            self.cp('act', self.vnew[:], pn[0:8, 0:384], [pn], [self.vnew])
            for pr_ in range(3):
                S.dma('pool', KTc[:], din["kT_c"][l, b, pr_], writes=[KTc], sembuf=KTc)
                vsrc = din["v_c"][l, b].rearrange("(blk k) (h d) -> k blk h d", k=128, d=64)
                for hh in range(2):
                    S.dma('pool', V1c[:, 0:CB, hh, 0:64], vsrc[:, :, 2 * pr_ + hh, :], writes=[V1c], sembuf=V1c)
                self.cp('act', V1c[0:8, CB, :, 0:64], self.vnew[0:8, pr_ * 128:(pr_ + 1) * 128].rearrange("p (h d) -> p h d", d=64), [self.vnew], [V1c])
                for hh in range(2):
                    h = pr_ * 2 + hh
                    rows = slice(hh * 64, hh * 64 + 64)
                    ps = self.gps()
                    v = ps[:, 0:(CB + 1) * 8].rearrange("p (a b) -> p a b", b=8)
                    for blk in range(CB):
                        self.mm(v[:, blk, :], KTc[rows, blk * 128:(blk + 1) * 128], self.QT[rows, pr_, cs], True, True, [KTc, self.QT], [ps])
                    self.mm(v[0:8, CB, :], self.KTn[rows, pr_, cs], self.QT[rows, pr_, cs], True, True, [self.KTn, self.QT], [ps])
                    self.act(self.es_[:], v, AF.Exp, [ps], [self.es_], scale=0.125)
                    self.tt('dve', self.ps_[:], self.es_[:], self.maskS[:], ALU.mult, [self.es_, self.maskS], [self.ps_])
                    for blk in range(CB):
                        self.mm(pacc[0:8, h, :], self.ps_[:, blk, :], V1c[:, blk, hh, :], blk == 0, False, [self.ps_, V1c], [pacc])
                    self.mm(pacc[0:8, h, :], self.ps_[0:8, CB, :], V1c[0:8, CB, hh, :], False, True, [self.ps_, V1c], [pacc])
            self.recip(self.rden[0:8], pacc[0:8, :, 64:65], [pacc], [self.rden])
            self.tt('dve', self.ob[:].rearrange("p (h d) -> p h d", d=64), pacc[0:8, :, 0:64], bc(self.rden[0:8], [8, 6, 64]), ALU.mult, [pacc, self.rden], [self.ob])
            pT = self.tps()
            pv = pT[:].rearrange("p (a b) -> p a b", b=128)
            for c in range(3):
                self.tr(pv[:, c, 0:8], self.ob[0:8, c * 128:(c + 1) * 128], self.identb[0:8, 0:8], [self.ob, self.identb], [pT])
            self.cp('act', self.mixT[:, 2:5, cs], pv[:, 0:3, 0:8], [pT], [self.mixT])

    def ssd(self, l, i, isp, nseg, Ls, last):
        S, p = self.S, self.prm
        dts = self.dts
        sc = self.ssdc[isp]
        tri, neg, same = sc[:, 0, :], sc[:, 1, :], sc[:, 2, :]
        hp = p["ssd_h"]
        self.tt('dve', dts[:, 0, 0:6], dts[:, 0, 0:6], hp[:, 0:6], ALU.add, [dts, hp], [dts])
        self.act(dts[:, 0, 0:6], dts[:, 0, 0:6], AF.Exp, [dts], [dts])
        self.act(dts[:, 0, 0:6], dts[:, 0, 0:6], AF.Ln, [dts], [dts], bias=1.0)
        self.tt('dve', dts[:, 1, 0:6], dts[:, 0, 0:6], p["Aneg"][:], ALU.mult, [dts, p["Aneg"]], [dts])
        dt, dtA = dts[:, 0, 0:6], dts[:, 1, 0:6]
        pT = self.tps()
        for h in range(6):
            self.tr(pT[:, h * 64:(h + 1) * 64], self.xsTb[:, h, :], self.identb[0:64, 0:64], [self.xsTb, self.identb], [pT])
        self.tt('dve', self.xr[:], pT[:, 0:384].rearrange("p (h d) -> p h d", d=64), bc(dt.unsqueeze(2), [128, 6, 64]), ALU.mult, [pT, dts], [self.xr])
        pT2 = self.tps()
        for t in range(2):
            self.tr(pT2[:, t * 128:(t + 1) * 128], self.BCb[:, t, :], self.identb[:], [self.BCb, self.identb], [pT2])
        self.cp('act', self.Btok[:], pT2[:, 0:256], [pT2], [self.Btok])
        self.tt('pool', self.R[:], bc(tri.unsqueeze(1), [128, 6, 128]), bc(dtA.unsqueeze(2), [128, 6, 128]), ALU.mult, [sc, dts], [self.R])
        pa = [self.gps(), self.gps()]
        for hb in range(2):
            self.mm(pa[hb][:, 0:384], self.onesf[:], self.R[:, hb * 3:hb * 3 + 3, :].rearrange("p a b -> p (a b)"), True, True, [self.onesf, self.R], [pa[hb]])
        pb = self.gps()
        self.mm(pb[:, 0:6], tri, dtA, True, True, [sc, dts], [pb])
        self.mm(pb[:, 8:14], same, dtA, True, True, [sc, dts], [pb])
        self.cp('act', dts[:, 2, 0:6], pb[:, 0:6], [pb], [dts])
        self.cp('act', dts[:, 3, 0:6], pb[:, 8:14], [pb], [dts])
        acs, tot = dts[:, 2, 0:6], dts[:, 3, 0:6]
        for hb in range(2):
            pav = pa[hb][:, 0:384].rearrange("p (a b) -> p a b", b=128)
            self.tt('dve', self.Dm[:, hb * 3:hb * 3 + 3, :], pav, bc(dts[:, 2, hb * 3:hb * 3 + 3].unsqueeze(2), [128, 3, 128]), ALU.subtract, [pa[hb], dts], [self.Dm])
            self.act(self.Eac[:, hb * 3:hb * 3 + 3, :], pav, AF.Exp, [pa[hb]], [self.Eac])
        yield
        self.tt('pool', self.Dm[:], self.Dm[:], bc(neg.unsqueeze(1), [128, 6, 128]), ALU.add, [self.Dm, sc], [self.Dm])
        self.act(self.Dm[:], self.Dm[:], AF.Exp, [self.Dm], [self.Dm])
        pc = self.gps()
        pcv = pc[:, 0:256].rearrange("p (a b) -> p a b", b=128)
        for g in range(2):
            self.mm(pcv[:, g, :], self.BCb[:, g, :], self.BCb[:, 2 + g, :], True, True, [self.BCb], [pc])
        for g in range(2):
            self.tt('dve', self.GT[:, g * 3:g * 3 + 3, :], self.Dm[:, g * 3:g * 3 + 3, :], bc(pcv[:, g:g + 1, :], [128, 3, 128]), ALU.mult, [self.Dm, pc], [self.GT])
            self.tt('pool', self.CE[:, g * 3:g * 3 + 3, :], self.Eac[:, g * 3:g * 3 + 3, :], bc(self.cBC[:, 2 + g:3 + g, :], [128, 3, 128]), ALU.mult, [self.Eac, self.cBC], [self.CE])
        self.tt('dve', dts[:, 3, 0:6], tot, acs, ALU.subtract, [dts], [dts])
        self.act(dts[:, 3, 0:6], dts[:, 3, 0:6], AF.Exp, [dts], [dts])
        self.tt('dve', self.xrd[:], self.xr[:], bc(dts[:, 3, 0:6].unsqueeze(2), [128, 6, 64]), ALU.mult, [self.xr, dts], [self.xrd])
        R2 = self.R2[:, 0:nseg, :]
        if nseg > 1:
            self.tt('dve', R2, bc(dtA.unsqueeze(1), [128, nseg, 6]), bc(self.segm[:, 0:nseg].unsqueeze(2), [128, nseg, 6]), ALU.mult, [dts, self.segm], [self.R2])
        else:
            self.cp('dve', R2, dtA.unsqueeze(1), [dts], [self.R2])
        pe_ = self.gps()
        self.mm(pe_[:, 0:nseg * 6], self.onesf[:], R2.rearrange("p a b -> p (a b)"), True, True, [self.onesf, self.R2], [pe_])
        et = self.etot[:, 0:nseg, :]
        self.act(et, pe_[:, 0:nseg * 6].rearrange("p (a b) -> p a b", b=6), AF.Exp, [pe_], [self.etot])
        xrdf = self.xrd[:].rearrange("p h d -> p (h d)")
        yield
        py = self.plong
        for h in range(6):
            o = py[h // 3][0:64, (h % 3) * 128:(h % 3 + 1) * 128]
            self.mm(o, self.xr[:, h, :], self.GT[:, h, :], True, True, [self.xr, self.GT], [py[h // 3]])
        pyo = self.tbanksf
        for b in range(nseg):
            if isp:
                st = self.st_p
            else:
                st = self.stb[b % 2]
                S.dma('sp', st[:], self.din["st_ssd"][l, :, b, :], writes=[st], sembuf=st)
            for h in range(6):
                o = pyo[h // 3][0:64, (h % 3) * 128:(h % 3 + 1) * 128]
                self.mm(o[:, b * Ls:(b + 1) * Ls], st[:, h * 64:(h + 1) * 64], self.CE[:, h, b * Ls:(b + 1) * Ls], True, True, [st, self.CE], [pyo[h // 3]])
            if nseg > 1:
                self.tsc('dve', self.xm[:], xrdf, self.segm[:, b:b + 1], None, ALU.mult, None, [self.xrd, self.segm], [self.xm])
                xm, xm_t = self.xm[:], self.xm
            else:
                xm, xm_t = xrdf, self.xrd
            pst = self.gps()
            for g in range(2):
                self.mm(pst[:, g * 192:(g + 1) * 192], self.Btok[:, g * 128:(g + 1) * 128], xm[:, g * 192:(g + 1) * 192], True, True, [self.Btok, xm_t], [pst])
            sb3 = st[:].rearrange("p (h d) -> p h d", d=64)
            self.tt('pool', sb3, sb3, bc(self.etot[:, b, :].unsqueeze(2), [128, 6, 64]), ALU.mult, [st, self.etot], [st])
            self.tt('dve', st[:], st[:], pst[:, 0:384], ALU.add, [st, pst], [st])
            if not isp:
                self.out_toks.append(S.dma('sp', self.dout["o_ssd_s"][l, :, b, :], st[:], reads=[st], sembuf=st))
        if isp and last:
            self.out_toks.append(S.dma('sp', self.dout["o_ssd_p"][l, :, 0, :], self.st_p[:], reads=[self.st_p], sembuf=self.st_p))
        yy, yt = self.yy, self.yt
        self.tt('pool', yt[:], self.cX[:], bc(hp[0:64, 12:18].unsqueeze(2), [64, 6, 128]), ALU.mult, [self.cX, hp], [yt])
        for hb in range(2):
            self.tt('dve', yy[:, hb * 3:hb * 3 + 3, :], py[hb][0:64, 0:384].rearrange("p (a b) -> p a b", b=128), yt[:, hb * 3:hb * 3 + 3, :], ALU.add, [py[hb], yt], [yy])
            self.tt('dve', yy[:, hb * 3:hb * 3 + 3, :], pyo[hb][0:64, 0:384].rearrange("p (a b) -> p a b", b=128), yy[:, hb * 3:hb * 3 + 3, :], ALU.add, [pyo[hb], yy], [yy])
        self.tt('dve', yy[:], yy[:], self.sz[:], ALU.mult, [yy, self.sz], [yy])
        self.tt('pool', yt[:], yy[:], yy[:], ALU.mult, [yy], [yt])
        pss = self.gps()
        for h in range(6):
            self.mm(pss[0:64, 0:128], self.onesf[0:64, 0:64], yt[:, h, :], h == 0, h == 5, [self.onesf, yt], [pss])
        rs = self.rs
        self.tsc('dve', rs[:], pss[0:64, 0:128], 1.0 / 384.0, EPS, ALU.mult, ALU.add, [pss], [rs])
        self.act(rs[:], rs[:], AF.Sqrt, [rs], [rs])
        self.recip(rs[:], rs[:], [rs], [rs])
        self.tt('dve', yy[:], yy[:], bc(rs[:].unsqueeze(1), [64, 6, 128]), ALU.mult, [yy, rs], [yy])
        yy2 = yy[:].rearrange("p (c two) l -> p c two l", two=2)
        sg2 = p["ssm_g"][:].rearrange("p (c two) -> p c two", two=2)
        for hh in range(2):
            self.tt('dve', self.mixC[hh * 64:(hh + 1) * 64, :, :], yy2[:, :, hh, :], bc(sg2[:, :, hh].unsqueeze(2), [64, 3, 128]), ALU.mult, [yy, p["ssm_g"]], [self.mixC])

    def ffn_phase(self, l):
        S, NT = self.S, self.NT
        din = self.din
        S.barrier()
        A = self.arena
        A.reset()
        w_gu = A.carve("w_gu", [128, 8, 2 * D_FF], BF16)
        w_dn = A.carve("w_dn", [128, 22, 1024], BF16)
        for c in range(8):
            S.dma('pool', w_gu[:, c, :], din["w_gu"][l, :, c, :], writes=[w_gu], sembuf=w_gu)
        for c0 in range(0, 22, 6):
            c1 = min(22, c0 + 6)
            S.dma('pool', w_dn[:, c0:c1, :], din["w_dn"][l, :, c0:c1, :], writes=[w_dn], sembuf=w_dn)
        S.dma('sp', self.gA[:], din["g4"][l, 2].partition_broadcast(128), writes=[self.gA], sembuf=self.gA)
        S.dma('sp', self.gB[:], din["g4"][l, 3].partition_broadcast(128), writes=[self.gB], sembuf=self.gB)
        actb = A.carve("actb", [128, D_FF], BF16)
        self.otmp = Tile(actb[:, 0:2048].bitcast(F32), "otmp", actb.b)
        actT = A.carve("actT", [128, 22, 128], BF16)
        sg = [A.carve("sg%d" % i, [128, 512], F32) for i in range(2)]
        widths = [512] * 5 + [256]
        S.mark('ffn start')
        xbs = [self.xb[0], A.carve("xb2", [128, 1024], F32)]
        self.load_x(l, 1, 0, xbs[0])
        for i in range(NT + 1):
            xt = xbs[i % 2]
            if i + 1 <= NT:
                self.load_x(l, 1, i + 1, xbs[(i + 1) % 2])
            self.norm_T(xt, self.gA)
            hT = self.hT
            off = 0
            for j, w in enumerate(widths):
                pg, pu = self.gps(), self.gps()
                for c in range(8):
                    self.mm(pg[:, 0:w], hT[:, c, :], w_gu[:, c, off:off + w], c == 0, c == 7, [hT, w_gu], [pg])
                for c in range(8):
                    self.mm(pu[:, 0:w], hT[:, c, :], w_gu[:, c, D_FF + off:D_FF + off + w], c == 0, c == 7, [hT, w_gu], [pu])
                s = sg[j % 2]
                self.act(s[:, 0:w], pg[:, 0:w], AF.Silu, [pg], [s])
                self.tt('dve', actb[:, off:off + w], s[:, 0:w], pu[:, 0:w], ALU.mult, [s, pu], [actb])
                off += w
            for k0 in range(0, 22, 8):
                k1 = min(22, k0 + 8)
                pT = self.tps()
                pv = pT[:].rearrange("p (a b) -> p a b", b=128)
                for k in range(k0, k1):
                    self.tr(pv[:, k - k0, :], actb[:, k * 128:(k + 1) * 128], self.identb[:], [actb, self.identb], [pT])
                self.cp('act' if (k0 // 8) % 2 == 0 else 'dve', actT[:, k0:k1, :], pv[:, 0:k1 - k0, :], [pT], [actT])
            po = [self.gps(), self.gps()]
            for hf in range(2):
                for k in range(22):
                    self.mm(po[hf][:], actT[:, k, :], w_dn[:, k, hf * 512:(hf + 1) * 512], k == 0, k == 21, [actT, w_dn], [po[hf]])
            self.out_norm_residual(xt, po, self.gB)
            self.store_x(l, 1, i, xt)


def _mult(dist):
    dist = np.asarray(dist)
    m = ((dist >= 0) & (dist <= 128)).astype(np.float32)
    m += ((dist >= 0) & (dist <= 512) & (dist % 4 == 0))
    m += ((dist >= 0) & (dist <= 2048) & (dist % 16 == 0))
    return m.astype(np.float32)


def _consts(CB):
    LB = CB * 128
    kl = np.arange(128)[:, None, None]
    o = np.arange(17)[None, :, None]
    ql = np.arange(128)[None, None, :]
    maskP = _mult(ql + 128 * o - kl)
    blk = np.arange(CB + 1)[None, :, None]
    t = np.arange(8)[None, None, :]
    r = blk * 128 + kl
    maskS = _mult(LB + t - r)
    maskS[8:, CB, :] = 0.0
    maskS = maskS.astype(np.float32)

    def ssdc(Ls):
        k = np.arange(128)
        seg = k // Ls
        same = (seg[:, None] == seg[None, :])
        tri = same & (k[:, None] <= k[None, :])
        neg = np.where(tri, 0.0, -30000.0)
        return np.stack([tri.astype(np.float32), neg.astype(np.float32), same.astype(np.float32)], 1)
    segm = (np.arange(128)[:, None] // TS == np.arange(NB)[None, :]).astype(np.float32)
    return dict(ident=np.eye(128, dtype=np.float32), maskP=np.ascontiguousarray(maskP), maskS=maskS,
                ssdc_p=np.ascontiguousarray(ssdc(128)), ssdc_s=np.ascontiguousarray(ssdc(TS)), segm_s=segm)


def _ct(a, P):
    sh = a.shape
    n = sh[-1] // P
    return np.moveaxis(a.reshape(sh[:-1] + (n, P)), -1, -2)


_CACHE = {}


def _RUN(nc, in_maps, core_ids):
    return run_bass_kernel_spmd(nc, in_maps, core_ids=core_ids)


def kernel(x_prompt, x_sample, state_lru_h, state_lru_conv, cache_swa_k, cache_swa_v, state_ssd, state_ssd_conv,
           norm_mix_in, norm_mix_out, w_in, conv_a_w, conv_a_b, lru_wa, lru_ba, lru_wx, lru_bx, lru_lambda,
           conv_c_w, conv_c_b, dt_bias, a_log, d_skip, ssm_norm, w_out, norm_ffn_in, norm_ffn_out,
           w_gate_up, w_down):
    f = lambda a: np.ascontiguousarray(np.asarray(a, dtype=np.float32))
    x_prompt, x_sample = f(x_prompt), f(x_sample)
    BATCH, SEQ, _ = x_prompt.shape
    L = w_in.shape[0]
    DB = x_sample.shape[0]
    LB = cache_swa_k.shape[2]
    NT, CB = SEQ // 128, LB // 128
    assert DB == NB * NCORES and x_sample.shape[1] == TS and BATCH * 4 == NCORES
    key = (NT, CB, L)
    if key not in _CACHE:
        bld = Builder(NT, CB, L)
        bld.build()
        _CACHE[key] = bld
    bld = _CACHE[key]
    WT = bld.WT

    sh = {}
    sh["w_in"] = f(np.asarray(w_in).reshape(L, 8, 128, N_IN).transpose(0, 2, 1, 3))
    wo = np.asarray(w_out)
    sh["w_outA"] = f(wo[:, 0:640].reshape(L, 5, 128, 1024).transpose(0, 2, 1, 3))
    sh["w_outC"] = f(wo[:, 640:1024].reshape(L, 3, 128, 1024).transpose(0, 2, 1, 3))
    sh["w_gu"] = f(np.asarray(w_gate_up).reshape(L, 8, 128, 2 * D_FF).transpose(0, 2, 1, 3))
    sh["w_dn"] = f(np.asarray(w_down).reshape(L, 22, 128, 1024).transpose(0, 2, 1, 3))
    sh["g4"] = f(np.stack([norm_mix_in, norm_mix_out, norm_ffn_in, norm_ffn_out], 1))
    sh["cva_w"] = f(_ct(np.asarray(conv_a_w), 128).transpose(0, 2, 3, 1))
    sh["cva_b"] = f(_ct(np.asarray(conv_a_b), 128))
    ccw, ccb = np.asarray(conv_c_w), np.asarray(conv_c_b)
    sh["cvx_w"] = f(_ct(ccw[:, :, 0:384], 64).transpose(0, 2, 3, 1))
    sh["cvx_b"] = f(_ct(ccb[:, 0:384], 64))
    sh["cvbc_w"] = f(_ct(ccw[:, :, 384:896], 128).transpose(0, 2, 3, 1))
    sh["cvbc_b"] = f(_ct(ccb[:, 384:896], 128))
    def bd(w):
        w = np.asarray(w)
        o = np.zeros((L, 128, 2, 128), np.float32)
        for t in range(2):
            for q in range(2):
                o[:, q * 64:(q + 1) * 64, t, q * 64:(q + 1) * 64] = w[:, t * 2 + q]
        return o
    sh["wa_bd"], sh["wx_bd"] = bd(lru_wa), bd(lru_wx)
    sh["lru_vec"] = f(np.stack([_ct(np.asarray(lru_ba), 128), _ct(np.asarray(lru_bx), 128), _ct(np.asarray(lru_lambda), 128)], -1))
    sh["ssd_h"] = f(np.concatenate([dt_bias, a_log, d_skip], 1))
    sh["ssm_g"] = f(_ct(np.asarray(ssm_norm), 64))
    sh.update(_consts(CB))

    slh, slc = np.asarray(state_lru_h), np.asarray(state_lru_conv)
    ssc, sss = np.asarray(state_ssd_conv), np.asarray(state_ssd)
    ck, cv = np.asarray(cache_swa_k), np.asarray(cache_swa_v)
    in_maps = []
    for c in range(NCORES):
        bs = slice(c * NB, (c + 1) * NB)
        m = dict(sh)
        m["xp"] = x_prompt[c // 4].reshape(NT, 128, 1024)
        m["xs"] = x_sample[bs].reshape(128, 1024)
        m["st_lru_h"] = f(_ct(slh[:, bs], 128).transpose(0, 2, 3, 1))
        m["st_lru_cv"] = f(_ct(slc[:, bs], 128).transpose(0, 3, 4, 1, 2))
        m["st_cvx"] = f(_ct(ssc[:, bs, :, 0:384], 64).transpose(0, 3, 4, 1, 2))
        m["st_cvbc"] = f(_ct(ssc[:, bs, :, 384:896], 128).transpose(0, 3, 4, 1, 2))
        m["st_ssd"] = f(sss[:, bs].transpose(0, 4, 1, 2, 3).reshape(L, 128, NB, 384))
        m["kT_c"] = f(ck[:, bs].reshape(L, NB, LB, 3, 128).transpose(0, 1, 3, 4, 2))
        m["v_c"] = f(cv[:, bs].reshape(L, NB, LB, 384))
        in_maps.append(m)

    res = _RUN(bld.nc, in_maps, core_ids=list(range(NCORES)))
    R = res.results
    g = lambda c, n: np.asarray(R[min(c, len(R) - 1)][n], dtype=np.float32)

    def ct_inv(a, caxis, taxis):
        a = np.moveaxis(a, (taxis, caxis), (-2, -1))
        return a.reshape(a.shape[:-2] + (-1,))
    pc = [0, 4]
    y_p = np.stack([g(c, "y_p").reshape(SEQ, 1024) for c in pc], 0)
    y_s = np.concatenate([g(c, "y_s").reshape(NB, TS, 1024) for c in range(NCORES)], 0)

    def lru_h(n, cores):
        return np.concatenate([ct_inv(g(c, n), 1, 2) for c in cores], 1)

    def cv_out(n, cores):
        return np.concatenate([ct_inv(g(c, n), 1, 2) for c in cores], 1)
    p_lru_h = lru_h("o_lru_h_p", pc)
    s_lru_h = lru_h("o_lru_h_s", range(NCORES))
    p_lru_conv = cv_out("o_lru_cv_p", pc)
    s_lru_conv = cv_out("o_lru_cv_s", range(NCORES))
    keep = WT * 128
    p_k = np.stack([g(c, "o_k_p").reshape(L, keep, 6, 64) for c in pc], 1)
    p_v = np.stack([g(c, "o_v_p").reshape(L, keep, 6, 64) for c in pc], 1)
    s_k = np.concatenate([g(c, "o_k_s").reshape(L, NB, TS, 6, 64) for c in range(NCORES)], 1)
    s_v = np.concatenate([g(c, "o_v_s").reshape(L, NB, TS, 6, 64) for c in range(NCORES)], 1)

    def ssd_out(n, cores):
        return np.concatenate([g(c, n).reshape(L, 128, -1, 6, 64).transpose(0, 2, 3, 4, 1) for c in cores], 1)
    p_ssd = ssd_out("o_ssd_p", pc)
    s_ssd = ssd_out("o_ssd_s", range(NCORES))

    def scv(sfx, cores):
        return np.concatenate([np.concatenate([ct_inv(g(c, "o_cvx" + sfx), 1, 2), ct_inv(g(c, "o_cvbc" + sfx), 1, 2)], -1) for c in cores], 1)
    p_ssd_conv = scv("_p", pc)
    s_ssd_conv = scv("_s", range(NCORES))
    outs = (y_p, y_s, p_lru_h, p_lru_conv, p_k, p_v, p_ssd, p_ssd_conv,
            s_lru_h, s_lru_conv, s_k, s_v, s_ssd, s_ssd_conv)
    return tuple(np.ascontiguousarray(o, dtype=np.float32) for o in outs)
```

```python
import os
import numpy as np
import concourse.bass as bass
import concourse.mybir as mybir
from concourse.bass_utils import run_bass_kernel_spmd
from contextlib import ExitStack

F32 = mybir.dt.float32
BF16 = mybir.dt.bfloat16
AF = mybir.ActivationFunctionType
ALU = mybir.AluOpType

D_MODEL = 1024
N_IN = 2950
D_FF = 2816
EPS = 1e-6
TS = 8
NB = 16
NCORES = 8


class Buf:
    def __init__(self, name):
        self.name = name
        self.w = None
        self.r = {}
        self.dsem = None
        self.dcnt = 0
        self.excl = False


class Tile:
    def __init__(self, ap, name, buf=None):
        self.ap = ap
        self.b = buf if buf is not None else Buf(name)

    def __getitem__(self, k):
        return self.ap[k]


class Sched:
    def __init__(self, nc, es):
        self.nc = nc
        self.es = es
        self.names = ['pe', 'act', 'dve', 'pool', 'sp']
        self.sem = {e: es.enter_context(nc.semaphore('s_' + e)) for e in self.names}
        self.cnt = {e: 0 for e in self.names}
        self.seen = {e: {} for e in self.names}
        self.q = {e: [] for e in self.names}
        self.dtoks = {}
        self.nsem = 5
        self.ninst = 0
        self.gseq = 0
        self.limit = int(os.environ.get('KLIMIT', '0'))
        self.marks = []

    def sb(self, name, shape, dt):
        t = self.es.enter_context(self.nc.sbuf_tensor(name, list(shape), dt))
        return Tile(t[tuple(slice(None) for _ in shape)], name)

    def ps(self, name, shape, dt):
        t = self.es.enter_context(self.nc.psum_tensor(name, list(shape), dt))
        r = Tile(t[tuple(slice(None) for _ in shape)], name)
        r.b.excl = True
        return r

    @staticmethod
    def _bufs(xs):
        return [x.b if isinstance(x, Tile) else x for x in xs]

    def _deps(self, e, reads, writes, skip_sem=None):
        need = {}

        def add(tok):
            s, v = tok
            k = id(s)
            if k not in need or need[k][1] < v:
                need[k] = (s, v)
        for b in reads:
            if b.w:
                add(b.w)
        for b in writes:
            if b.w and not (skip_sem is not None and b.w[0] is skip_sem):
                add(b.w)
            for tok in b.r.values():
                add(tok)
        out = []
        for k, (s, v) in need.items():
            if e == 'pe' and s is self.sem['pe']:
                continue
            if self.seen[e].get(k, 0) < v:
                self.seen[e][k] = v
                out.append((s, v))
        return out

    @staticmethod
    def _mark(tok, reads, writes):
        s, v = tok
        for b in reads:
            b.r[id(s)] = tok
        for b in writes:
            b.w = tok
            b.r = {}

    def op(self, e, fn, reads=(), writes=()):
        reads = self._bufs(reads)
        writes = self._bufs(writes)
        if e != 'pe':
            writes = writes + [b for b in reads if b.excl and b not in writes]
            reads = [b for b in reads if not b.excl]
        waits = self._deps(e, reads, writes)
        self.cnt[e] += 1
        tok = (self.sem[e], self.cnt[e])
        self._mark(tok, reads, writes)
        self.gseq += 1
        self.q[e].append((waits, fn, (self.sem[e], 1), self.gseq))
        self.ninst += 1 + len(waits)
        return tok

    def dma(self, e, out_ap, in_ap, reads=(), writes=(), sembuf=None):
        reads = self._bufs(reads)
        writes = self._bufs(writes)
        sb = sembuf.b if isinstance(sembuf, Tile) else sembuf
        if sb.dsem is None:
            sb.dsem = self.es.enter_context(self.nc.semaphore('d%d_%s' % (self.nsem, sb.name)))
            self.nsem += 1
        waits = self._deps(e, reads, writes, skip_sem=sb.dsem)
        sb.dcnt += 16
        tok = (sb.dsem, sb.dcnt)
        self._mark(tok, reads, writes)
        self.dtoks[id(sb.dsem)] = tok
        self.gseq += 1
        self.q[e].append((waits, lambda eng: eng.dma_start(out=out_ap, in_=in_ap), (sb.dsem, 16), self.gseq))
        self.ninst += 1 + len(waits)
        return tok

    def mark(self, label):
        self.marks.append((label, self.gseq))

    def wait_tok(self, e, tok):
        s, v = tok
        if e == 'pe' and s is self.sem['pe']:
            return
        if self.seen[e].get(id(s), 0) < v:
            self.seen[e][id(s)] = v
            self.gseq += 1
            self.q[e].append(([(s, v)], None, None, self.gseq))
            self.ninst += 1

    def barrier(self, engines=None):
        engines = engines or self.names
        toks = [(self.sem[o], self.cnt[o]) for o in self.names if self.cnt[o] > 0]
        toks += list(self.dtoks.values())
        for e in engines:
            for tok in toks:
                if tok[0] is self.sem.get(e):
                    continue
                self.wait_tok(e, tok)

    def emit(self):
        nc = self.nc
        S = self
        eng_of = {'pe': 'tensor', 'act': 'scalar', 'dve': 'vector', 'pool': 'gpsimd', 'sp': 'sync'}
        with nc.Block() as block:
            def mk(name):
                def run(eng):
                    for waits, fn, inc, seq in S.q[name]:
                        if S.limit and seq > S.limit:
                            break
                        for (s, v) in waits:
                            eng.wait_ge(s, v)
                        if fn is not None:
                            fn(eng).then_inc(inc[0], inc[1])
                return run
            for name in S.names:
                getattr(block, eng_of[name])(mk(name))


class Arena:
    def __init__(self, S, name, nel):
        self.t = S.sb(name, [128, nel], BF16)
        self.nel = nel
        self.off = 0
        self.hi = 0
        self.bufs = {}

    def reset(self):
        self.off = 0

    def carve(self, name, shape, dt):
        n = int(np.prod(shape[1:]))
        nel = n * (2 if dt == F32 else 1)
        if self.off % 2:
            self.off += 1
        assert self.off + nel <= self.nel, ("arena overflow", name, self.off + nel, self.nel)
        ap = self.t.ap[0:shape[0], self.off:self.off + nel]
        if dt == F32:
            ap = ap.bitcast(F32)
        if len(shape) == 3:
            ap = ap.rearrange("p (a b) -> p a b", b=shape[2])
        elif len(shape) == 4:
            ap = ap.rearrange("p (a b c) -> p a b c", b=shape[2], c=shape[3])
        self.off += nel
        self.hi = max(self.hi, self.off)
        if name not in self.bufs:
            self.bufs[name] = Buf(name)
        return Tile(ap, name, self.bufs[name])


def bc(ap, shape):
    return ap.to_broadcast(list(shape))


class Builder:
    def __init__(self, NT, CB, DEPTH):
        self.NT, self.CB, self.DEPTH = NT, CB, DEPTH
        self.WT = min(16, NT)
        self.nc = bass.Bass("TRN2", target_bir_lowering=False)
        self.din = {}
        self.dout = {}

    def I(self, name, shape, dt=F32):
        self.din[name] = self.nc.dram_tensor(name, list(shape), dt, kind="ExternalInput").ap()
        return self.din[name]

    def O(self, name, shape, dt=F32):
        self.dout[name] = self.nc.dram_tensor(name, list(shape), dt, kind="ExternalOutput").ap()
        return self.dout[name]

    def mm(self, out, lhsT, rhs, start, stop, reads, writes):
        self.S.op('pe', lambda e: e.matmul(out, lhsT, rhs, start=start, stop=stop), reads, writes)

    def tr(self, out, in_, ident, reads, writes):
        self.S.op('pe', lambda e: e.transpose(out, in_, ident), reads, writes)

    def act(self, out, in_, func, reads, writes, **kw):
        self.S.op('act', lambda e: e.activation(out, in_, func, **kw), reads, writes)

    def tt(self, eng, out, a, b, op, reads, writes):
        self.S.op(eng, lambda e: e.tensor_tensor(out, a, b, op), reads, writes)

    def tsc(self, eng, out, a, s1, s2, op0, op1, reads, writes):
        if s2 is None:
            self.S.op(eng, lambda e: e.tensor_scalar(out, a, s1, None, op0), reads, writes)
        else:
            self.S.op(eng, lambda e: e.tensor_scalar(out, a, s1, s2, op0, op1), reads, writes)

    def stt(self, out, in0, scalar, in1, op0, op1, reads, writes):
        self.S.op('dve', lambda e: e.scalar_tensor_tensor(out, in0, scalar, in1, op0, op1), reads, writes)

    def cp(self, eng, out, in_, reads, writes):
        if eng == 'act':
            self.S.op('act', lambda e: e.copy(out, in_), reads, writes)
        else:
            self.S.op(eng, lambda e: e.tensor_copy(out, in_), reads, writes)

    def memset(self, eng, ap, val, writes):
        self.S.op(eng, lambda e: e.memset(ap, val), (), writes)

    def recip(self, out, in_, reads, writes):
        self.S.op('dve', lambda e: e.reciprocal(out, in_), reads, writes)

    def gps(self):
        t = self.gbanks[self.gi % len(self.gbanks)]
        self.gi += 1
        return t

    def tps(self):
        t = self.tbanks[self.ti_ % len(self.tbanks)]
        self.ti_ += 1
        return t

    def build(self):
        NT, CB, L, WT = self.NT, self.CB, self.DEPTH, self.WT
        LB = CB * 128
        I, O = self.I, self.O
        I("xp", [NT, 128, 1024]); I("xs", [128, 1024])
        I("w_in", [L, 128, 8, N_IN]); I("w_outA", [L, 128, 5, 1024]); I("w_outC", [L, 128, 3, 1024])
        I("w_gu", [L, 128, 8, 2 * D_FF]); I("w_dn", [L, 128, 22, 1024])
        I("g4", [L, 4, 1024])
        I("cva_w", [L, 128, 2, 4]); I("cva_b", [L, 128, 2])
        I("cvx_w", [L, 64, 6, 4]); I("cvx_b", [L, 64, 6])
        I("cvbc_w", [L, 128, 4, 4]); I("cvbc_b", [L, 128, 4])
        I("wa_bd", [L, 128, 2, 128]); I("wx_bd", [L, 128, 2, 128]); I("lru_vec", [L, 128, 2, 3])
        I("ssd_h", [L, 18]); I("ssm_g", [L, 64, 6])
        I("st_lru_h", [L, 128, 2, NB]); I("st_lru_cv", [L, 128, 2, NB, 3])
        I("st_cvx", [L, 64, 6, NB, 3]); I("st_cvbc", [L, 128, 4, NB, 3])
        I("st_ssd", [L, 128, NB, 384]); I("kT_c", [L, NB, 3, 128, LB]); I("v_c", [L, NB, LB, 384])
        I("ident", [128, 128]); I("maskP", [128, 17, 128]); I("maskS", [128, CB + 1, 8])
        I("ssdc_p", [128, 3, 128]); I("ssdc_s", [128, 3, 128]); I("segm_s", [128, NB])
        O("y_p", [NT, 128, 1024]); O("y_s", [128, 1024])
        O("o_lru_h_p", [L, 128, 2, 1]); O("o_lru_h_s", [L, 128, 2, NB])
        O("o_lru_cv_p", [L, 128, 2, 1, 3]); O("o_lru_cv_s", [L, 128, 2, NB, 3])
        O("o_k_p", [L, WT, 128, 384]); O("o_v_p", [L, WT, 128, 384])
        O("o_k_s", [L, 128, 384]); O("o_v_s", [L, 128, 384])
        O("o_ssd_p", [L, 128, 1, 384]); O("o_ssd_s", [L, 128, NB, 384])
        O("o_cvx_p", [L, 64, 6, 1, 3]); O("o_cvx_s", [L, 64, 6, NB, 3])
        O("o_cvbc_p", [L, 128, 4, 1, 3]); O("o_cvbc_s", [L, 128, 4, NB, 3])
        self.xscr = self.nc.dram_tensor("xscr", [NT + 1, 128, 1024], F32, kind="Internal").ap()
        self.xscr_b = [Buf("xscr%d" % i) for i in range(NT + 1)]
        self.out_toks = []

        with ExitStack() as es:
            S = self.S = Sched(self.nc, es)
            self.identf = S.sb("identf", [128, 128], F32)
            self.identb = S.sb("identb", [128, 128], BF16)
            self.onesf = S.sb("onesf", [128, 128], F32)
            self.segm = S.sb("segm_sb", [128, NB], F32)
            self.gA = S.sb("gA", [128, 1024], F32)
            self.gB = S.sb("gB", [128, 1024], F32)
            self.xb = [S.sb("xt0", [128, 1024], F32)] * 2
            self.hn = S.sb("hn", [128, 1024], BF16)
            self.hT = S.sb("hT", [128, 8, 128], BF16)
            self.junk = self.hn
            self.sm = S.sb("small", [128, 64], F32)
            self.sm_b = [Buf("sm%d" % i) for i in range(16)]
            self.arena = Arena(S, "arena", 80500)
            self.gbanks = [S.ps("pg%d" % i, [128, 512], F32) for i in range(4)]
            self.tbanksf = [S.ps("pt%d" % i, [128, 512], F32) for i in range(2)]
            self.tbanks = [Tile(t.ap.bitcast(BF16), "ptb%d" % i, t.b) for i, t in enumerate(self.tbanksf)]
            self.plong = [S.ps("plong%d" % i, [128, 512], F32) for i in range(2)]
            self.pacc = Tile(self.plong[0][:, 0:390].rearrange("p (h d) -> p h d", d=65), "pacc", self.plong[0].b)
            self.gi = 0
            self.ti_ = 0
            S.dma('sp', self.identf[:], self.din["ident"], writes=[self.identf], sembuf=self.identf)
            S.dma('pool', self.identb[:], self.din["ident"], writes=[self.identb], sembuf=self.identb)
            S.dma('sp', self.segm[:], self.din["segm_s"], writes=[self.segm], sembuf=self.segm)
            self.memset('pool', self.onesf[:], 1.0, [self.onesf])

            for l in range(L):
                self.mixer_phase(l)
                self.ffn_phase(l)
            for tok in self.out_toks:
                S.wait_tok('sp', tok)
            S.barrier(['sp'])
            self.ninst = S.ninst
            S.emit()
        return self.nc

    def x_src(self, l, phase, i):
        if l == 0 and phase == 0:
            return (self.din["xp"][i] if i < self.NT else self.din["xs"]), None
        return self.xscr[i], self.xscr_b[i]

    def x_dst(self, l, phase, i):
        if l == self.DEPTH - 1 and phase == 1:
            return (self.dout["y_p"][i] if i < self.NT else self.dout["y_s"]), None
        return self.xscr[i], self.xscr_b[i]

    def load_x(self, l, phase, i, xt):
        ap, db = self.x_src(l, phase, i)
        self.S.dma('sp', xt[:], ap, reads=([db] if db else []), writes=[xt], sembuf=xt)

    def store_x(self, l, phase, i, xt):
        ap, db = self.x_dst(l, phase, i)
        tok = self.S.dma('sp', ap, xt[:], reads=[xt], writes=([db] if db else []), sembuf=xt)
        if db is None:
            self.out_toks.append(tok)

    def norm_A(self, xt, g_bc, hn=None):
        sm = self.sm
        hn = hn or self.hn
        b0 = self.sm_b[0]
        self.act(hn[:], xt[:], AF.Square, [xt], [hn, b0], accum_out=sm[:, 0:1])
        self.tsc('dve', sm[:, 0:1], sm[:, 0:1], 1.0 / D_MODEL, EPS, ALU.mult, ALU.add, [b0], [b0])
        self.act(sm[:, 0:1], sm[:, 0:1], AF.Sqrt, [b0], [b0])
        self.recip(sm[:, 0:1], sm[:, 0:1], [b0], [b0])
        self.stt(hn[:], xt[:], sm[:, 0:1], g_bc[:], ALU.mult, ALU.mult, [xt, b0, g_bc], [hn])

    def norm_B(self, hn=None, hT=None):
        hn = hn or self.hn
        hT = hT or self.hT
        pT = self.tps()
        pv = pT[:].rearrange("p (a b) -> p a b", b=128)
        for c in range(8):
            self.tr(pv[:, c, :], hn[:, c * 128:(c + 1) * 128], self.identb[:], [hn, self.identb], [pT])
        self.cp('act', hT[:], pv, [pT], [hT])

    def norm_T(self, xt, g_bc):
        self.norm_A(xt, g_bc)
        self.norm_B()

    def out_norm_residual(self, xt, pbanks, g_bc):
        sm = self.sm
        b1 = self.sm_b[1]
        self.act(self.junk[:, 0:512], pbanks[0][:], AF.Square, [pbanks[0]], [self.junk, b1], accum_out=sm[:, 1:2])
        self.act(self.junk[:, 512:1024], pbanks[1][:], AF.Square, [pbanks[1]], [self.junk, b1], accum_out=sm[:, 2:3])
        self.tt('dve', sm[:, 1:2], sm[:, 1:2], sm[:, 2:3], ALU.add, [b1], [b1])
        self.tsc('dve', sm[:, 1:2], sm[:, 1:2], 1.0 / D_MODEL, EPS, ALU.mult, ALU.add, [b1], [b1])
        self.act(sm[:, 1:2], sm[:, 1:2], AF.Sqrt, [b1], [b1])
        self.recip(sm[:, 1:2], sm[:, 1:2], [b1], [b1])
        tmp = self.otmp
        for hf in range(2):
            sl = slice(hf * 512, (hf + 1) * 512)
            self.stt(tmp[:, sl], pbanks[hf][:], sm[:, 1:2], g_bc[:, sl], ALU.mult, ALU.mult, [pbanks[hf], b1, g_bc], [tmp])
        self.tt('pool', xt[:], xt[:], tmp[:], ALU.add, [xt, tmp], [xt])

    def mixer_phase(self, l):
        S, NT, CB = self.S, self.NT, self.CB
        din = self.din
        S.barrier()
        A = self.arena
        A.reset()
        self.w_in = A.carve("w_in", [128, 8, N_IN], BF16)
        self.w_oA = A.carve("w_oA", [128, 5, 1024], BF16)
        self.w_oC = A.carve("w_oC", [128, 3, 1024], BF16)
        for c in range(8):
            S.dma('pool', self.w_in[:, c, :], din["w_in"][l, :, c, :], writes=[self.w_in], sembuf=self.w_in)
        S.dma('pool', self.w_oA[:], din["w_outA"][l], writes=[self.w_oA], sembuf=self.w_oA)
        S.dma('pool', self.w_oC[:], din["w_outC"][l], writes=[self.w_oC], sembuf=self.w_oC)
        S.dma('sp', self.gA[:], din["g4"][l, 0].partition_broadcast(128), writes=[self.gA], sembuf=self.gA)
        S.dma('sp', self.gB[:], din["g4"][l, 1].partition_broadcast(128), writes=[self.gB], sembuf=self.gB)
        p = self.prm = {}
        def ld(name, shape, src, dt=F32, q='sp'):
            t = A.carve(name, shape, dt)
            S.dma(q, t[tuple(slice(None) for _ in shape)], src, writes=[t], sembuf=t)
            p[name] = t
            return t
        ld("cva_w", [128, 2, 4], din["cva_w"][l]); ld("cva_b", [128, 2], din["cva_b"][l])
        ld("cvx_w", [64, 6, 4], din["cvx_w"][l]); ld("cvx_b", [64, 6], din["cvx_b"][l])
        ld("cvbc_w", [128, 4, 4], din["cvbc_w"][l]); ld("cvbc_b", [128, 4], din["cvbc_b"][l])
        ld("wa_bd", [128, 2, 128], din["wa_bd"][l], BF16, 'pool'); ld("wx_bd", [128, 2, 128], din["wx_bd"][l], BF16, 'pool')
        ld("lru_vec", [128, 2, 3], din["lru_vec"][l])
        ld("ssd_h", [128, 18], din["ssd_h"][l].partition_broadcast(128))
        ld("ssm_g", [64, 6], din["ssm_g"][l])
        self.maskP = ld("maskP", [128, 17, 128], din["maskP"], BF16, 'pool')
        self.maskS = ld("maskS", [128, CB + 1, 8], din["maskS"])
        self.ssdc = {True: ld("ssdc_p", [128, 3, 128], din["ssdc_p"]), False: ld("ssdc_s", [128, 3, 128], din["ssdc_s"])}
        cl = p["cl"] = A.carve("cl", [128, 2], F32)
        lam = p["lru_vec"][:, :, 2]
        self.act(cl[:], lam, AF.Exp, [p["lru_vec"]], [cl], scale=-1.0)
        self.act(cl[:], cl[:], AF.Ln, [cl], [cl], bias=1.0)
        self.tsc('dve', cl[:], cl[:], -8.0, None, ALU.mult, None, [cl], [cl])
        An = p["Aneg"] = A.carve("Aneg", [128, 6], F32)
        self.act(An[:], p["ssd_h"][:, 6:12], AF.Exp, [p["ssd_h"]], [An])
        self.tsc('dve', An[:], An[:], -1.0, None, ALU.mult, None, [An], [An])
        S.mark('mixer params')
        c = A.carve
        off0 = A.off
        self.KT = c("KT", [128, 3, 17, 128], BF16)
        self.KT_b = [Buf("KT%d" % i) for i in range(17)]
        self.V1 = c("V1", [128, 17, 6, 65], BF16)
        self.V1_b = [Buf("V1%d" % i) for i in range(17)]
        self.ebuf = [c("ebuf%d" % i, [128, 4, 128], BF16) for i in range(2)]
        self.pbuf = [c("pbuf%d" % i, [128, 4, 128], BF16) for i in range(2)]
        self.otok = c("otok", [128, 6, 64], BF16)
        self.st_p = c("st_p", [128, 384], F32)
        end_p = A.off
        A.off = off0
        self.KTc = c("KTc", [128, CB * 128], BF16)
        self.V1c = c("V1c", [128, CB + 1, 2, 65], BF16)
        self.es_ = c("es_", [128, CB + 1, 8], F32)
        self.ps_ = c("ps_", [128, CB + 1, 8], BF16)
        self.ob = c("ob", [8, 384], BF16)
        self.vnew = c("vnew", [8, 384], BF16)
        self.stb = [c("stb%d" % i, [128, 384], F32) for i in range(2)]
        self.xm = c("xm", [128, 384], BF16)
        self.KTn = c("KTn", [128, 3, 128], BF16)
        self.vtokb = c("vtokb", [128, 384], BF16)
        A.off = max(A.off, end_p)
        self.QT = c("QT", [128, 3, 128], BF16)
        self.kvst = [c("kvst%d" % i, [128, 384], F32) for i in range(2)]
        self.xpA = c("xpA", [128, 2, NB * (3 + TS)], F32)
        self.xpX = c("xpX", [64, 6, NB * (3 + TS)], F32)
        self.xpBC = c("xpBC", [128, 4, NB * (3 + TS)], F32)
        self.cA = c("cA", [128, 2, 128], F32)
        self.cX = c("cX", [64, 6, 128], F32)
        self.cBC = c("cBC", [128, 4, 128], F32)
        self.ctmp = c("ctmp", [128, 768], F32)
        self.ge = c("ge", [128, 2, 128], F32)
        self.xcb = c("xcb", [128, 2, 128], BF16)
        self.lr = c("lr", [128, 2, 128], F32)
        self.li = c("li", [128, 2, 128], F32)
        self.la = c("la", [128, 2, 128], F32)
        self.lu = c("lu", [128, 2, 128], F32)
        self.lh = c("lh", [128, 2, 128], F32)
        self.hst = {True: c("hst_p", [128, 2, 1], F32), False: c("hst_s", [128, 2, NB], F32)}
        self.lt = c("lt", [128, 2, NB], F32)
        self.mixT = c("mixT", [128, 5, 128], BF16)
        self.mixC = c("mixC", [128, 3, 128], BF16)
        self.rden = c("rden", [128, 6, 1], F32)
        self.sz = c("sz", [64, 6, 128], F32)
        self.xsTb = c("xsTb", [64, 6, 128], BF16)
        self.BCb = c("BCb", [128, 4, 128], BF16)
        self.dts = c("dts", [128, 4, 8], F32)
        self.xr = c("xr", [128, 6, 64], BF16)
        self.xrd = c("xrd", [128, 6, 64], BF16)
        self.Btok = c("Btok", [128, 256], BF16)
        RD = c("RD", [128, 12, 128], F32)
        self.R = Tile(RD[:, 0:6, :], "R", RD.b)
        self.CE = self.R
        self.Dm = Tile(RD[:, 6:12, :], "Dm", RD.b)
        self.otmp = Tile(RD[:, 0:8, :].rearrange("p a b -> p (a b)"), "otmp", RD.b)
        self.Eac = c("Eac", [128, 6, 128], F32)
        self.GT = c("GT", [128, 6, 128], BF16)
        self.R2 = c("R2", [128, NB, 6], F32)
        self.etot = c("etot", [128, NB, 6], F32)
        self.yy = c("yy", [64, 6, 128], F32)
        self.yt = Tile(self.ctmp[0:64, 0:768].rearrange("p (a b) -> p a b", b=128), "yt", self.ctmp.b)
        self.rs = c("rs", [64, 128], F32)
        self.memset('pool', self.st_p[:], 0.0, [self.st_p])
        self.memset('pool', self.hst[True][:], 0.0, [self.hst[True]])
        self.memset('pool', self.V1[:, :, :, 64:65], 1.0, self.V1_b)
        S.dma('sp', self.hst[False][:], din["st_lru_h"][l], writes=[self.hst[False]], sembuf=self.hst[False])

        for i in range(NT + 1):
            if i == NT:
                S.barrier()
                self.memset('pool', self.V1c[:, :, :, 64:65], 1.0, [self.V1c])
            self.load_x(l, 0, i, self.xb[0])
            self.mixer_tile(l, i)

    def mixer_tile(self, l, i):
        S, NT, CB = self.S, self.NT, self.CB
        din, dout, p = self.din, self.dout, self.prm
        isp = i < NT
        nseg, Ls = (1, 128) if isp else (NB, TS)
        W = 3 + Ls
        last = (i == NT - 1)
        want_kv = (not isp) or (i >= NT - self.WT)
        xt = self.xb[i % 2]
        S.mark('tile%d start' % i)
        self.norm_T(xt, self.gA)
        S.mark('tile%d normT' % i)
        hT, w_in = self.hT, self.w_in

        def proj_fm(ps_ap, ps_t, c0, M):
            for c in range(8):
                self.mm(ps_ap, w_in[:, c, c0:c0 + M], hT[:, c, :], c == 0, c == 7, [w_in, hT], [ps_t])

        def seg4(t, n):
            return t[:, :, 0:nseg * W].rearrange("p a (s w) -> p a s w", w=W)

        xpA4, xpX4, xpBC4 = seg4(self.xpA, 2), seg4(self.xpX, 6), seg4(self.xpBC, 4)
        if isp:
            if i == 0:
                for t, v in ((self.xpA, xpA4), (self.xpX, xpX4), (self.xpBC, xpBC4)):
                    self.memset('pool', v[:, :, :, 0:3], 0.0, [t])
            else:
                for t, v in ((self.xpA, xpA4), (self.xpX, xpX4), (self.xpBC, xpBC4)):
                    self.cp('pool', v[:, :, :, 0:3], v[:, :, :, Ls:Ls + 3], [t], [t])
        else:
            S.dma('sp', xpA4[:, :, :, 0:3], din["st_lru_cv"][l], writes=[self.xpA], sembuf=self.xpA)
            S.dma('sp', xpX4[:, :, :, 0:3], din["st_cvx"][l], writes=[self.xpX], sembuf=self.xpX)
            S.dma('sp', xpBC4[:, :, :, 0:3], din["st_cvbc"][l], writes=[self.xpBC], sembuf=self.xpBC)

        pg1 = self.gps()
        v1 = pg1[:].rearrange("p (a b) -> p a b", b=128)
        for t in range(4):
            proj_fm(v1[:, t, :], pg1, t * 128, 128)
        self.act(self.ge[:], v1[:, 0:2, :], AF.Gelu_apprx_tanh, [pg1], [self.ge])
        self.cp('dve', xpA4[:, :, :, 3:W], v1[:, 2:4, :].rearrange("p a (s w) -> p a s w", w=Ls), [pg1], [self.xpA])
        pq = self.gps()
        vq = pq[:].rearrange("p (a b) -> p a b", b=128)
        for t in range(3):
            proj_fm(vq[:, t, :], pq, 512 + t * 128, 128)
        self.cp('act', self.QT[:], vq[:, 0:3, :], [pq], [self.QT])
        pk = self.gps()
        vk = pk[:].rearrange("p (a b) -> p a b", b=128)
        for t in range(3):
            proj_fm(vk[:, t, :], pk, 896 + t * 128, 128)
        slot = i % 17
        if isp:
            self.cp('dve', self.KT[:, :, slot, :], vk[:, 0:3, :], [pk], [self.KT_b[slot]])
        else:
            self.cp('dve', self.KTn[:], vk[:, 0:3, :], [pk], [self.KTn])
        pv = self.gps()
        for c in range(8):
            self.mm(pv[:, 0:384], hT[:, c, :], w_in[:, c, 1280:1664], c == 0, c == 7, [hT, w_in], [pv])
        for c in range(8):
            self.mm(pv[:, 384:390], hT[:, c, :], w_in[:, c, 2944:2950], c == 0, c == 7, [hT, w_in], [pv])
        pv3 = pv[:, 0:384].rearrange("p (h d) -> p h d", d=64)
        if isp:
            self.cp('act', self.V1[:, slot, :, 0:64], pv3, [pv], [self.V1_b[slot]])
        else:
            self.cp('act', self.vtokb[:], pv[:, 0:384], [pv], [self.vtokb])
        self.cp('dve', self.dts[:, 0, 0:6], pv[:, 384:390], [pv], [self.dts])
        if want_kv:
            vs = self.kvst[0]
            self.cp('act', vs[:], pv[:, 0:384], [pv], [vs])
            dst = dout["o_v_p"][l, i - (NT - self.WT)] if isp else dout["o_v_s"][l]
            self.out_toks.append(S.dma('sp', dst, vs[:], reads=[vs], sembuf=vs))
            pk2 = self.gps()
            for c in range(8):
                self.mm(pk2[:, 0:384], hT[:, c, :], w_in[:, c, 896:1280], c == 0, c == 7, [hT, w_in], [pk2])
            ks = self.kvst[1]
            self.cp('act', ks[:], pk2[:, 0:384], [pk2], [ks])
            dst = dout["o_k_p"][l, i - (NT - self.WT)] if isp else dout["o_k_s"][l]
            self.out_toks.append(S.dma('sp', dst, ks[:], reads=[ks], sembuf=ks))
        xpX5 = self.xpX[:, :, 0:nseg * W].rearrange("p (c two) (s w) -> p c two s w", two=2, w=W)
        sz4 = self.sz[:].rearrange("p (c two) l -> p c two l", two=2)

        def proj_z():
            pz = self.gps()
            vz = pz[:, 0:384].rearrange("p (a b) -> p a b", b=128)
            for t in range(3):
                proj_fm(vz[:, t, :], pz, 1664 + t * 128, 128)
            for hh in range(2):
                self.act(sz4[:, :, hh, :], vz[hh * 64:(hh + 1) * 64, :, :], AF.Silu, [pz], [self.sz])

        def proj_x_bc():
            px = self.gps()
            vx = px[:, 0:384].rearrange("p (a b) -> p a b", b=128)
            for t in range(3):
                proj_fm(vx[:, t, :], px, 2048 + t * 128, 128)
            for hh in range(2):
                self.cp('dve', xpX5[:, :, hh, :, 3:W], vx[hh * 64:(hh + 1) * 64, :, :].rearrange("p a (s w) -> p a s w", w=Ls), [px], [self.xpX])
            pbc = self.gps()
            vbc = pbc[:].rearrange("p (a b) -> p a b", b=128)
            for t in range(4):
                proj_fm(vbc[:, t, :], pbc, 2432 + t * 128, 128)
            self.cp('act', xpBC4[:, :, :, 3:W], vbc.rearrange("p a (s w) -> p a s w", w=Ls), [pbc], [self.xpBC])

        def conv(P, n, xp_t, xp4, wt, bt, out_t, eng):
            o4 = out_t[:].rearrange("p a (s w) -> p a s w", w=Ls)
            tmp = self.ctmp[0:P, 0:n * 128].rearrange("p (a s w) -> p a s w", a=n, w=Ls)
            shp = [P, n, nseg, Ls]
            self.tt(eng, o4, xp4[:, :, :, 0:Ls], bc(wt[:, :, 0:1].unsqueeze(3), shp), ALU.mult, [xp_t, wt], [out_t])
            for j in range(1, 4):
                self.tt(eng, tmp, xp4[:, :, :, j:j + Ls], bc(wt[:, :, j:j + 1].unsqueeze(3), shp), ALU.mult, [xp_t, wt], [self.ctmp])
                self.tt(eng, o4, o4, tmp, ALU.add, [out_t, self.ctmp], [out_t])
            self.tt(eng, o4, o4, bc(bt[:].unsqueeze(2).unsqueeze(3), shp), ALU.add, [out_t, bt], [out_t])

        def state_out(nm, t, v):
            if last or not isp:
                sfx = "_p" if isp else "_s"
                self.out_toks.append(S.dma('sp', dout[nm + sfx][l], v[:, :, :, Ls:Ls + 3], reads=[t], sembuf=t))

        def conv_a():
            conv(128, 2, self.xpA, xpA4, p["cva_w"], p["cva_b"], self.cA, 'pool')
            state_out("o_lru_cv", self.xpA, xpA4)

        def conv_xbc():
            conv(64, 6, self.xpX, xpX4, p["cvx_w"], p["cvx_b"], self.cX, 'dve')
            conv(128, 4, self.xpBC, xpBC4, p["cvbc_w"], p["cvbc_b"], self.cBC, 'pool')
            state_out("o_cvx", self.xpX, xpX4)
            state_out("o_cvbc", self.xpBC, xpBC4)
            self.act(self.cX[:], self.cX[:], AF.Silu, [self.cX], [self.cX])
            self.act(self.cBC[:], self.cBC[:], AF.Silu, [self.cBC], [self.cBC])
            self.cp('pool', self.xsTb[:], self.cX[:], [self.cX], [self.xsTb])
            self.cp('pool', self.BCb[:], self.cBC[:], [self.cBC], [self.BCb])

        gl = self.lru(l, i, isp, nseg, Ls, last)
        gs = self.ssd(l, i, isp, nseg, Ls, last)
        if isp:
            self.kctr = 0
            conv_a()
            next(gl)
            self.attn_head(i, 0)
            proj_z()
            self.attn_head(i, 1)
            proj_x_bc()
            conv_xbc()
            self.attn_head(i, 2)
            next(gs)
            self.attn_head(i, 3)
            next(gl)
            self.attn_head(i, 4)
            next(gs)
            self.attn_head(i, 5)
            for _ in gl:
                pass
            self.attn_finish()
            for _ in gs:
                pass
        else:
            proj_z()
            proj_x_bc()
            conv_a()
            conv_xbc()
            for _ in gl:
                pass
            self.attn_sample(l)
            for _ in gs:
                pass
        S.mark('tile%d mix' % i)

        po = [self.gps(), self.gps()]
        for hf in range(2):
            sl = slice(hf * 512, (hf + 1) * 512)
            for cidx in range(5):
                self.mm(po[hf][:], self.mixT[:, cidx, :], self.w_oA[:, cidx, sl], cidx == 0, False, [self.mixT, self.w_oA], [po[hf]])
            for h in range(3):
                self.mm(po[hf][:], self.mixC[:, h, :], self.w_oC[:, h, sl], False, h == 2, [self.mixC, self.w_oC], [po[hf]])
        self.out_norm_residual(xt, po, self.gB)
        self.store_x(l, 0, i, xt)
        S.mark('tile%d done' % i)

    def lru(self, l, i, isp, nseg, Ls, last):
        S, p = self.S, self.prm
        cA, xcb, lr, li, la, lu, lh = self.cA, self.xcb, self.lr, self.li, self.la, self.lu, self.lh
        hst = self.hst[isp]
        self.cp('pool', xcb[:], cA[:], [cA], [xcb])
        pr = self.gps()
        v = pr[:].rearrange("p (a b) -> p a b", b=128)
        for t in range(2):
            self.mm(v[:, t, :], p["wa_bd"][:, t, :], xcb[:, t, :], True, True, [p["wa_bd"], xcb], [pr])
            self.mm(v[:, 2 + t, :], p["wx_bd"][:, t, :], xcb[:, t, :], True, True, [p["wx_bd"], xcb], [pr])
        for t in range(2):
            self.act(lr[:, t, :], v[:, t, :], AF.Sigmoid, [pr, p["lru_vec"]], [lr], bias=p["lru_vec"][:, t, 0:1])
            self.act(li[:, t, :], v[:, 2 + t, :], AF.Sigmoid, [pr, p["lru_vec"]], [li], bias=p["lru_vec"][:, t, 1:2])
        for t in range(2):
            self.act(la[:, t, :], lr[:, t, :], AF.Exp, [lr, p["cl"]], [la], scale=p["cl"][:, t:t + 1])
        yield
        self.tt('pool', lr[:], la[:], la[:], ALU.mult, [la], [lr])
        self.tsc('pool', lr[:], lr[:], -1.0, 1.0, ALU.mult, ALU.add, [lr], [lr])
        self.act(lr[:], lr[:], AF.Sqrt, [lr], [lr])
        self.tt('dve', li[:], li[:], cA[:], ALU.mult, [li, cA], [li])
        self.tt('dve', lu[:], lr[:], li[:], ALU.mult, [lr, li], [lu])
        a0 = la[:].rearrange("p a (s w) -> p a s w", w=Ls)[:, :, :, 0]
        u0 = lu[:].rearrange("p a (s w) -> p a s w", w=Ls)[:, :, :, 0]
        lt = self.lt[:, :, 0:nseg]
        self.tt('dve', lt, a0, hst[:], ALU.mult, [la, hst], [self.lt])
        self.tt('dve', u0, u0, lt, ALU.add, [lu, self.lt], [lu])
        self.memset('dve', a0, 0.0, [la])
        for t in range(2):
            S.op('dve', (lambda t: (lambda e: e.tensor_tensor_scan(lh[:, t, :], la[:, t, :], lu[:, t, :], 0.0, ALU.mult, ALU.add)))(t), [la, lu], [lh])
        yield
        hl = lh[:].rearrange("p a (s w) -> p a s w", w=Ls)[:, :, :, Ls - 1]
        self.cp('pool', hst[:], hl, [lh], [hst])
        if last or not isp:
            dst = self.dout["o_lru_h_p" if isp else "o_lru_h_s"][l]
            self.out_toks.append(S.dma('sp', dst, hst[:], reads=[hst], sembuf=hst))
        self.tt('dve', self.mixT[:, 0:2, :], lh[:], self.ge[:], ALU.mult, [lh, self.ge], [self.mixT])

    def attn_head(self, i, h):
        pacc = self.pacc
        nk = min(16, i) + 1
        pr_, hh = h // 2, h % 2
        rows = slice(hh * 64, hh * 64 + 64)
        batches = [(o0, min(4, nk - o0)) for o0 in range(0, nk, 4)]

        def s_stage(bi):
            o0, nb = batches[bi]
            ps = self.gps()
            v = ps[:].rearrange("p (a b) -> p a b", b=128)
            for jj in range(nb):
                sj = (i - (o0 + jj)) % 17
                self.mm(v[:, jj, :], self.KT[rows, pr_, sj, :], self.QT[rows, pr_, :], True, True, [self.KT_b[sj], self.QT], [ps])
            return ps, v

        def rest(bi, ps, v):
            o0, nb = batches[bi]
            k = self.kctr
            self.kctr += 1
            eb, pb = self.ebuf[k % 2], self.pbuf[k % 2]
            self.act(eb[:, 0:nb, :], v[:, 0:nb, :], AF.Exp, [ps], [eb], scale=0.125)
            self.tt('dve' if k % 2 else 'pool', pb[:, 0:nb, :], eb[:, 0:nb, :], self.maskP[:, o0:o0 + nb, :], ALU.mult, [eb, self.maskP], [pb])
            for jj in range(nb):
                o = o0 + jj
                sj = (i - o) % 17
                self.mm(pacc[:, h, :], pb[:, jj, :], self.V1[:, sj, h, :], o == 0, o == nk - 1, [pb, self.V1_b[sj]], [pacc])
        cur = s_stage(0)
        for bi in range(len(batches)):
            nxt = s_stage(bi + 1) if bi + 1 < len(batches) else None
            rest(bi, *cur)
            cur = nxt

    def attn_finish(self):
        pacc = self.pacc
        self.recip(self.rden[:], pacc[:, :, 64:65], [pacc], [self.rden])
        self.tt('dve', self.otok[:], pacc[:, :, 0:64], bc(self.rden[:], [128, 6, 64]), ALU.mult, [pacc, self.rden], [self.otok])
        pT = self.tps()
        pv = pT[:].rearrange("p (a b) -> p a b", b=128)
        of = self.otok[:].rearrange("p h d -> p (h d)")
        for c in range(3):
            self.tr(pv[:, c, :], of[:, c * 128:(c + 1) * 128], self.identb[:], [self.otok, self.identb], [pT])
        self.cp('act', self.mixT[:, 2:5, :], pv[:, 0:3, :], [pT], [self.mixT])

    def attn_sample(self, l):
        S, CB = self.S, self.CB
        din = self.din
        pacc = self.pacc
        KTc, V1c = self.KTc, self.V1c
        for b in range(NB):
            cs = slice(b * 8, b * 8 + 8)
            pn = self.gps()
            self.mm(pn[0:8, 0:384], self.identb[:, cs], self.vtokb[:], True, True, [self.identb, self.vtokb], [pn])
            self.cp('act', self.vnew[:], pn[0:8, 0:384], [pn], [self.vnew])
            for pr_ in range(3):
                S.dma('pool', KTc[:], din["kT_c"][l, b, pr_], writes=[KTc], sembuf=KTc)
                vsrc = din["v_c"][l, b].rearrange("(blk k) (h d) -> k blk h d", k=128, d=64)
                for hh in range(2):
                    S.dma('pool', V1c[:, 0:CB, hh, 0:64], vsrc[:, :, 2 * pr_ + hh, :], writes=[V1c], sembuf=V1c)
                self.cp('act', V1c[0:8, CB, :, 0:64], self.vnew[0:8, pr_ * 128:(pr_ + 1) * 128].rearrange("p (h d) -> p h d", d=64), [self.vnew], [V1c])
                for hh in range(2):
                    h = pr_ * 2 + hh
                    rows = slice(hh * 64, hh * 64 + 64)
                    ps = self.gps()
                    v = ps[:, 0:(CB + 1) * 8].rearrange("p (a b) -> p a b", b=8)
                    for blk in range(CB):
                        self.mm(v[:, blk, :], KTc[rows, blk * 128:(blk + 1) * 128], self.QT[rows, pr_, cs], True, True, [KTc, self.QT], [ps])
                    self.mm(v[0:8, CB, :], self.KTn[rows, pr_, cs], self.QT[rows, pr_, cs], True, True, [self.KTn, self.QT], [ps])
                    self.act(self.es_[:], v, AF.Exp, [ps], [self.es_], scale=0.125)
                    self.tt('dve', self.ps_[:], self.es_[:], self.maskS[:], ALU.mult, [self.es_, self.maskS], [self.ps_])
                    for blk in range(CB):
                        self.mm(pacc[0:8, h, :], self.ps_[:, blk, :], V1c[:, blk, hh, :], blk == 0, False, [self.ps_, V1c], [pacc])
                    self.mm(pacc[0:8, h, :], self.ps_[0:8, CB, :], V1c[0:8, CB, hh, :], False, True, [self.ps_, V1c], [pacc])
            self.recip(self.rden[0:8], pacc[0:8, :, 64:65], [pacc], [self.rden])
            self.tt('dve', self.ob[:].rearrange("p (h d) -> p h d", d=64), pacc[0:8, :, 0:64], bc(self.rden[0:8], [8, 6, 64]), ALU.mult, [pacc, self.rden], [self.ob])
            pT = self.tps()
            pv = pT[:].rearrange("p (a b) -> p a b", b=128)
            for c in range(3):
                self.tr(pv[:, c, 0:8], self.ob[0:8, c * 128:(c + 1) * 128], self.identb[0:8, 0:8], [self.ob, self.identb], [pT])
            self.cp('act', self.mixT[:, 2:5, cs], pv[:, 0:3, 0:8], [pT], [self.mixT])

    def ssd(self, l, i, isp, nseg, Ls, last):
        S, p = self.S, self.prm
        dts = self.dts
        sc = self.ssdc[isp]
        tri, neg, same = sc[:, 0, :], sc[:, 1, :], sc[:, 2, :]
        hp = p["ssd_h"]
        self.tt('dve', dts[:, 0, 0:6], dts[:, 0, 0:6], hp[:, 0:6], ALU.add, [dts, hp], [dts])
        self.act(dts[:, 0, 0:6], dts[:, 0, 0:6], AF.Exp, [dts], [dts])
        self.act(dts[:, 0, 0:6], dts[:, 0, 0:6], AF.Ln, [dts], [dts], bias=1.0)
        self.tt('dve', dts[:, 1, 0:6], dts[:, 0, 0:6], p["Aneg"][:], ALU.mult, [dts, p["Aneg"]], [dts])
        dt, dtA = dts[:, 0, 0:6], dts[:, 1, 0:6]
        pT = self.tps()
        for h in range(6):
            self.tr(pT[:, h * 64:(h + 1) * 64], self.xsTb[:, h, :], self.identb[0:64, 0:64], [self.xsTb, self.identb], [pT])
        self.tt('dve', self.xr[:], pT[:, 0:384].rearrange("p (h d) -> p h d", d=64), bc(dt.unsqueeze(2), [128, 6, 64]), ALU.mult, [pT, dts], [self.xr])
        pT2 = self.tps()
        for t in range(2):
            self.tr(pT2[:, t * 128:(t + 1) * 128], self.BCb[:, t, :], self.identb[:], [self.BCb, self.identb], [pT2])
        self.cp('act', self.Btok[:], pT2[:, 0:256], [pT2], [self.Btok])
        self.tt('pool', self.R[:], bc(tri.unsqueeze(1), [128, 6, 128]), bc(dtA.unsqueeze(2), [128, 6, 128]), ALU.mult, [sc, dts], [self.R])
        pa = [self.gps(), self.gps()]
        for hb in range(2):
            self.mm(pa[hb][:, 0:384], self.onesf[:], self.R[:, hb * 3:hb * 3 + 3, :].rearrange("p a b -> p (a b)"), True, True, [self.onesf, self.R], [pa[hb]])
        pb = self.gps()
        self.mm(pb[:, 0:6], tri, dtA, True, True, [sc, dts], [pb])
        self.mm(pb[:, 8:14], same, dtA, True, True, [sc, dts], [pb])
        self.cp('act', dts[:, 2, 0:6], pb[:, 0:6], [pb], [dts])
        self.cp('act', dts[:, 3, 0:6], pb[:, 8:14], [pb], [dts])
        acs, tot = dts[:, 2, 0:6], dts[:, 3, 0:6]
        for hb in range(2):
            pav = pa[hb][:, 0:384].rearrange("p (a b) -> p a b", b=128)
            self.tt('dve', self.Dm[:, hb * 3:hb * 3 + 3, :], pav, bc(dts[:, 2, hb * 3:hb * 3 + 3].unsqueeze(2), [128, 3, 128]), ALU.subtract, [pa[hb], dts], [self.Dm])
            self.act(self.Eac[:, hb * 3:hb * 3 + 3, :], pav, AF.Exp, [pa[hb]], [self.Eac])
        yield
        self.tt('pool', self.Dm[:], self.Dm[:], bc(neg.unsqueeze(1), [128, 6, 128]), ALU.add, [self.Dm, sc], [self.Dm])
        self.act(self.Dm[:], self.Dm[:], AF.Exp, [self.Dm], [self.Dm])
        pc = self.gps()
        pcv = pc[:, 0:256].rearrange("p (a b) -> p a b", b=128)
        for g in range(2):
            self.mm(pcv[:, g, :], self.BCb[:, g, :], self.BCb[:, 2 + g, :], True, True, [self.BCb], [pc])
        for g in range(2):
            self.tt('dve', self.GT[:, g * 3:g * 3 + 3, :], self.Dm[:, g * 3:g * 3 + 3, :], bc(pcv[:, g:g + 1, :], [128, 3, 128]), ALU.mult, [self.Dm, pc], [self.GT])
            self.tt('pool', self.CE[:, g * 3:g * 3 + 3, :], self.Eac[:, g * 3:g * 3 + 3, :], bc(self.cBC[:, 2 + g:3 + g, :], [128, 3, 128]), ALU.mult, [self.Eac, self.cBC], [self.CE])
        self.tt('dve', dts[:, 3, 0:6], tot, acs, ALU.subtract, [dts], [dts])
        self.act(dts[:, 3, 0:6], dts[:, 3, 0:6], AF.Exp, [dts], [dts])
        self.tt('dve', self.xrd[:], self.xr[:], bc(dts[:, 3, 0:6].unsqueeze(2), [128, 6, 64]), ALU.mult, [self.xr, dts], [self.xrd])
        R2 = self.R2[:, 0:nseg, :]
        if nseg > 1:
            self.tt('dve', R2, bc(dtA.unsqueeze(1), [128, nseg, 6]), bc(self.segm[:, 0:nseg].unsqueeze(2), [128, nseg, 6]), ALU.mult, [dts, self.segm], [self.R2])
        else:
            self.cp('dve', R2, dtA.unsqueeze(1), [dts], [self.R2])
        pe_ = self.gps()
        self.mm(pe_[:, 0:nseg * 6], self.onesf[:], R2.rearrange("p a b -> p (a b)"), True, True, [self.onesf, self.R2], [pe_])
        et = self.etot[:, 0:nseg, :]
        self.act(et, pe_[:, 0:nseg * 6].rearrange("p (a b) -> p a b", b=6), AF.Exp, [pe_], [self.etot])
        xrdf = self.xrd[:].rearrange("p h d -> p (h d)")
        yield
        py = self.plong
        for h in range(6):
            o = py[h // 3][0:64, (h % 3) * 128:(h % 3 + 1) * 128]
            self.mm(o, self.xr[:, h, :], self.GT[:, h, :], True, True, [self.xr, self.GT], [py[h // 3]])
        pyo = self.tbanksf
        for b in range(nseg):
            if isp:
                st = self.st_p
            else:
                st = self.stb[b % 2]
                S.dma('sp', st[:], self.din["st_ssd"][l, :, b, :], writes=[st], sembuf=st)
            for h in range(6):
                o = pyo[h // 3][0:64, (h % 3) * 128:(h % 3 + 1) * 128]
                self.mm(o[:, b * Ls:(b + 1) * Ls], st[:, h * 64:(h + 1) * 64], self.CE[:, h, b * Ls:(b + 1) * Ls], True, True, [st, self.CE], [pyo[h // 3]])
            if nseg > 1:
                self.tsc('dve', self.xm[:], xrdf, self.segm[:, b:b + 1], None, ALU.mult, None, [self.xrd, self.segm], [self.xm])
                xm, xm_t = self.xm[:], self.xm
            else:
                xm, xm_t = xrdf, self.xrd
            pst = self.gps()
            for g in range(2):
                self.mm(pst[:, g * 192:(g + 1) * 192], self.Btok[:, g * 128:(g + 1) * 128], xm[:, g * 192:(g + 1) * 192], True, True, [self.Btok, xm_t], [pst])
            sb3 = st[:].rearrange("p (h d) -> p h d", d=64)
            self.tt('pool', sb3, sb3, bc(self.etot[:, b, :].unsqueeze(2), [128, 6, 64]), ALU.mult, [st, self.etot], [st])
            self.tt('dve', st[:], st[:], pst[:, 0:384], ALU.add, [st, pst], [st])
            if not isp:
                self.out_toks.append(S.dma('sp', self.dout["o_ssd_s"][l, :, b, :], st[:], reads=[st], sembuf=st))
        if isp and last:
            self.out_toks.append(S.dma('sp', self.dout["o_ssd_p"][l, :, 0, :], self.st_p[:], reads=[self.st_p], sembuf=self.st_p))
        yy, yt = self.yy, self.yt
        self.tt('pool', yt[:], self.cX[:], bc(hp[0:64, 12:18].unsqueeze(2), [64, 6, 128]), ALU.mult, [self.cX, hp], [yt])
        for hb in range(2):
            self.tt('dve', yy[:, hb * 3:hb * 3 + 3, :], py[hb][0:64, 0:384].rearrange("p (a b) -> p a b", b=128), yt[:, hb * 3:hb * 3 + 3, :], ALU.add, [py[hb], yt], [yy])
            self.tt('dve', yy[:, hb * 3:hb * 3 + 3, :], pyo[hb][0:64, 0:384].rearrange("p (a b) -> p a b", b=128), yy[:, hb * 3:hb * 3 + 3, :], ALU.add, [pyo[hb], yy], [yy])
        self.tt('dve', yy[:], yy[:], self.sz[:], ALU.mult, [yy, self.sz], [yy])
        self.tt('pool', yt[:], yy[:], yy[:], ALU.mult, [yy], [yt])
        pss = self.gps()
        for h in range(6):
            self.mm(pss[0:64, 0:128], self.onesf[0:64, 0:64], yt[:, h, :], h == 0, h == 5, [self.onesf, yt], [pss])
        rs = self.rs
        self.tsc('dve', rs[:], pss[0:64, 0:128], 1.0 / 384.0, EPS, ALU.mult, ALU.add, [pss], [rs])
        self.act(rs[:], rs[:], AF.Sqrt, [rs], [rs])
        self.recip(rs[:], rs[:], [rs], [rs])
        self.tt('dve', yy[:], yy[:], bc(rs[:].unsqueeze(1), [64, 6, 128]), ALU.mult, [yy, rs], [yy])
        yy2 = yy[:].rearrange("p (c two) l -> p c two l", two=2)
        sg2 = p["ssm_g"][:].rearrange("p (c two) -> p c two", two=2)
        for hh in range(2):
            self.tt('dve', self.mixC[hh * 64:(hh + 1) * 64, :, :], yy2[:, :, hh, :], bc(sg2[:, :, hh].unsqueeze(2), [64, 3, 128]), ALU.mult, [yy, p["ssm_g"]], [self.mixC])

    def ffn_phase(self, l):
        S, NT = self.S, self.NT
        din = self.din
        S.barrier()
        A = self.arena
        A.reset()
        w_gu = A.carve("w_gu", [128, 8, 2 * D_FF], BF16)
        w_dn = A.carve("w_dn", [128, 22, 1024], BF16)
        for c in range(8):
            S.dma('pool', w_gu[:, c, :], din["w_gu"][l, :, c, :], writes=[w_gu], sembuf=w_gu)
        for c0 in range(0, 22, 6):
            c1 = min(22, c0 + 6)
            S.dma('pool', w_dn[:, c0:c1, :], din["w_dn"][l, :, c0:c1, :], writes=[w_dn], sembuf=w_dn)
        S.dma('sp', self.gA[:], din["g4"][l, 2].partition_broadcast(128), writes=[self.gA], sembuf=self.gA)
        S.dma('sp', self.gB[:], din["g4"][l, 3].partition_broadcast(128), writes=[self.gB], sembuf=self.gB)
        actb = A.carve("actb", [128, D_FF], BF16)
        self.otmp = Tile(actb[:, 0:2048].bitcast(F32), "otmp", actb.b)
        actT = A.carve("actT", [128, 22, 128], BF16)
        sg = [A.carve("sg%d" % i, [128, 512], F32) for i in range(2)]
        widths = [512] * 5 + [256]
        S.mark('ffn start')
        xbs = [self.xb[0], A.carve("xb2", [128, 1024], F32)]
        hns = [self.hn, A.carve("hn2", [128, 1024], BF16)]
        hTs = [self.hT, A.carve("hT2", [128, 8, 128], BF16)]
        self.load_x(l, 1, 0, xbs[0])
        self.norm_A(xbs[0], self.gA, hns[0])
        self.norm_B(hns[0], hTs[0])
        for i in range(NT + 1):
            xt = xbs[i % 2]
            if i + 1 <= NT:
                self.load_x(l, 1, i + 1, xbs[(i + 1) % 2])
            hT = hTs[i % 2]
            off = 0
            for j, w in enumerate(widths):
                pg, pu = self.gps(), self.gps()
                for c in range(8):
                    self.mm(pg[:, 0:w], hT[:, c, :], w_gu[:, c, off:off + w], c == 0, c == 7, [hT, w_gu], [pg])
                for c in range(8):
                    self.mm(pu[:, 0:w], hT[:, c, :], w_gu[:, c, D_FF + off:D_FF + off + w], c == 0, c == 7, [hT, w_gu], [pu])
                s = sg[j % 2]
                self.act(s[:, 0:w], pg[:, 0:w], AF.Silu, [pg], [s])
                self.tt('dve', actb[:, off:off + w], s[:, 0:w], pu[:, 0:w], ALU.mult, [s, pu], [actb])
                off += w
            for k0 in range(0, 22, 8):
                k1 = min(22, k0 + 8)
                pT = self.tps()
                pv = pT[:].rearrange("p (a b) -> p a b", b=128)
                for k in range(k0, k1):
                    self.tr(pv[:, k - k0, :], actb[:, k * 128:(k + 1) * 128], self.identb[:], [actb, self.identb], [pT])
                self.cp('act' if (k0 // 8) % 2 == 0 else 'dve', actT[:, k0:k1, :], pv[:, 0:k1 - k0, :], [pT], [actT])
            if i + 1 <= NT:
                self.norm_A(xbs[(i + 1) % 2], self.gA, hns[(i + 1) % 2])
            po = [self.gps(), self.gps()]
            for hf in range(2):
                for k in range(22):
                    self.mm(po[hf][:], actT[:, k, :], w_dn[:, k, hf * 512:(hf + 1) * 512], k == 0, k == 21, [actT, w_dn], [po[hf]])
            if i + 1 <= NT:
                self.norm_B(hns[(i + 1) % 2], hTs[(i + 1) % 2])
            self.junk = hns[i % 2]
            self.out_norm_residual(xt, po, self.gB)
            self.junk = self.hn
            self.store_x(l, 1, i, xt)


def _mult(dist):
    dist = np.asarray(dist)
    m = ((dist >= 0) & (dist <= 128)).astype(np.float32)
    m += ((dist >= 0) & (dist <= 512) & (dist % 4 == 0))
    m += ((dist >= 0) & (dist <= 2048) & (dist % 16 == 0))
    return m.astype(np.float32)


def _consts(CB):
    LB = CB * 128
    kl = np.arange(128)[:, None, None]
    o = np.arange(17)[None, :, None]
    ql = np.arange(128)[None, None, :]
    maskP = _mult(ql + 128 * o - kl)
    blk = np.arange(CB + 1)[None, :, None]
    t = np.arange(8)[None, None, :]
    r = blk * 128 + kl
    maskS = _mult(LB + t - r)
    maskS[8:, CB, :] = 0.0
    maskS = maskS.astype(np.float32)

    def ssdc(Ls):
        k = np.arange(128)
        seg = k // Ls
        same = (seg[:, None] == seg[None, :])
        tri = same & (k[:, None] <= k[None, :])
        neg = np.where(tri, 0.0, -30000.0)
        return np.stack([tri.astype(np.float32), neg.astype(np.float32), same.astype(np.float32)], 1)
    segm = (np.arange(128)[:, None] // TS == np.arange(NB)[None, :]).astype(np.float32)
    return dict(ident=np.eye(128, dtype=np.float32), maskP=np.ascontiguousarray(maskP), maskS=maskS,
                ssdc_p=np.ascontiguousarray(ssdc(128)), ssdc_s=np.ascontiguousarray(ssdc(TS)), segm_s=segm)


def _ct(a, P):
    sh = a.shape
    n = sh[-1] // P
    return np.moveaxis(a.reshape(sh[:-1] + (n, P)), -1, -2)


_CACHE = {}


def _RUN(nc, in_maps, core_ids):
    return run_bass_kernel_spmd(nc, in_maps, core_ids=core_ids)


def kernel(x_prompt, x_sample, state_lru_h, state_lru_conv, cache_swa_k, cache_swa_v, state_ssd, state_ssd_conv,
           norm_mix_in, norm_mix_out, w_in, conv_a_w, conv_a_b, lru_wa, lru_ba, lru_wx, lru_bx, lru_lambda,
           conv_c_w, conv_c_b, dt_bias, a_log, d_skip, ssm_norm, w_out, norm_ffn_in, norm_ffn_out,
           w_gate_up, w_down):
    f = lambda a: np.ascontiguousarray(np.asarray(a, dtype=np.float32))
    x_prompt, x_sample = f(x_prompt), f(x_sample)
    BATCH, SEQ, _ = x_prompt.shape
    L = w_in.shape[0]
    DB = x_sample.shape[0]
    LB = cache_swa_k.shape[2]
    NT, CB = SEQ // 128, LB // 128
    assert DB == NB * NCORES and x_sample.shape[1] == TS and BATCH * 4 == NCORES
    key = (NT, CB, L)
    if key not in _CACHE:
        bld = Builder(NT, CB, L)
        bld.build()
        _CACHE[key] = bld
    bld = _CACHE[key]
    WT = bld.WT

    sh = {}
    sh["w_in"] = f(np.asarray(w_in).reshape(L, 8, 128, N_IN).transpose(0, 2, 1, 3))
    wo = np.asarray(w_out)
    sh["w_outA"] = f(wo[:, 0:640].reshape(L, 5, 128, 1024).transpose(0, 2, 1, 3))
    sh["w_outC"] = f(wo[:, 640:1024].reshape(L, 3, 128, 1024).transpose(0, 2, 1, 3))
    sh["w_gu"] = f(np.asarray(w_gate_up).reshape(L, 8, 128, 2 * D_FF).transpose(0, 2, 1, 3))
    sh["w_dn"] = f(np.asarray(w_down).reshape(L, 22, 128, 1024).transpose(0, 2, 1, 3))
    sh["g4"] = f(np.stack([norm_mix_in, norm_mix_out, norm_ffn_in, norm_ffn_out], 1))
    sh["cva_w"] = f(_ct(np.asarray(conv_a_w), 128).transpose(0, 2, 3, 1))
    sh["cva_b"] = f(_ct(np.asarray(conv_a_b), 128))
    ccw, ccb = np.asarray(conv_c_w), np.asarray(conv_c_b)
    sh["cvx_w"] = f(_ct(ccw[:, :, 0:384], 64).transpose(0, 2, 3, 1))
    sh["cvx_b"] = f(_ct(ccb[:, 0:384], 64))
    sh["cvbc_w"] = f(_ct(ccw[:, :, 384:896], 128).transpose(0, 2, 3, 1))
    sh["cvbc_b"] = f(_ct(ccb[:, 384:896], 128))
    def bd(w):
        w = np.asarray(w)
        o = np.zeros((L, 128, 2, 128), np.float32)
        for t in range(2):
            for q in range(2):
                o[:, q * 64:(q + 1) * 64, t, q * 64:(q + 1) * 64] = w[:, t * 2 + q]
        return o
    sh["wa_bd"], sh["wx_bd"] = bd(lru_wa), bd(lru_wx)
    sh["lru_vec"] = f(np.stack([_ct(np.asarray(lru_ba), 128), _ct(np.asarray(lru_bx), 128), _ct(np.asarray(lru_lambda), 128)], -1))
    sh["ssd_h"] = f(np.concatenate([dt_bias, a_log, d_skip], 1))
    sh["ssm_g"] = f(_ct(np.asarray(ssm_norm), 64))
    sh.update(_consts(CB))

    slh, slc = np.asarray(state_lru_h), np.asarray(state_lru_conv)
    ssc, sss = np.asarray(state_ssd_conv), np.asarray(state_ssd)
    ck, cv = np.asarray(cache_swa_k), np.asarray(cache_swa_v)
    in_maps = []
    for c in range(NCORES):
        bs = slice(c * NB, (c + 1) * NB)
        m = dict(sh)
        m["xp"] = x_prompt[c // 4].reshape(NT, 128, 1024)
        m["xs"] = x_sample[bs].reshape(128, 1024)
        m["st_lru_h"] = f(_ct(slh[:, bs], 128).transpose(0, 2, 3, 1))
        m["st_lru_cv"] = f(_ct(slc[:, bs], 128).transpose(0, 3, 4, 1, 2))
        m["st_cvx"] = f(_ct(ssc[:, bs, :, 0:384], 64).transpose(0, 3, 4, 1, 2))
        m["st_cvbc"] = f(_ct(ssc[:, bs, :, 384:896], 128).transpose(0, 3, 4, 1, 2))
        m["st_ssd"] = f(sss[:, bs].transpose(0, 4, 1, 2, 3).reshape(L, 128, NB, 384))
        m["kT_c"] = f(ck[:, bs].reshape(L, NB, LB, 3, 128).transpose(0, 1, 3, 4, 2))
        m["v_c"] = f(cv[:, bs].reshape(L, NB, LB, 384))
        in_maps.append(m)

    res = _RUN(bld.nc, in_maps, core_ids=list(range(NCORES)))
    R = res.results
    g = lambda c, n: np.asarray(R[min(c, len(R) - 1)][n], dtype=np.float32)

    def ct_inv(a, caxis, taxis):
        a = np.moveaxis(a, (taxis, caxis), (-2, -1))
        return a.reshape(a.shape[:-2] + (-1,))
    pc = [0, 4]
    y_p = np.stack([g(c, "y_p").reshape(SEQ, 1024) for c in pc], 0)
    y_s = np.concatenate([g(c, "y_s").reshape(NB, TS, 1024) for c in range(NCORES)], 0)

    def lru_h(n, cores):
        return np.concatenate([ct_inv(g(c, n), 1, 2) for c in cores], 1)

    def cv_out(n, cores):
        return np.concatenate([ct_inv(g(c, n), 1, 2) for c in cores], 1)
    p_lru_h = lru_h("o_lru_h_p", pc)
    s_lru_h = lru_h("o_lru_h_s", range(NCORES))
    p_lru_conv = cv_out("o_lru_cv_p", pc)
    s_lru_conv = cv_out("o_lru_cv_s", range(NCORES))
    keep = WT * 128
    p_k = np.stack([g(c, "o_k_p").reshape(L, keep, 6, 64) for c in pc], 1)
    p_v = np.stack([g(c, "o_v_p").reshape(L, keep, 6, 64) for c in pc], 1)
    s_k = np.concatenate([g(c, "o_k_s").reshape(L, NB, TS, 6, 64) for c in range(NCORES)], 1)
    s_v = np.concatenate([g(c, "o_v_s").reshape(L, NB, TS, 6, 64) for c in range(NCORES)], 1)

    def ssd_out(n, cores):
        return np.concatenate([g(c, n).reshape(L, 128, -1, 6, 64).transpose(0, 2, 3, 4, 1) for c in cores], 1)
    p_ssd = ssd_out("o_ssd_p", pc)
    s_ssd = ssd_out("o_ssd_s", range(NCORES))

    def scv(sfx, cores):
        return np.concatenate([np.concatenate([ct_inv(g(c, "o_cvx" + sfx), 1, 2), ct_inv(g(c, "o_cvbc" + sfx), 1, 2)], -1) for c in cores], 1)
    p_ssd_conv = scv("_p", pc)
    s_ssd_conv = scv("_s", range(NCORES))
    outs = (y_p, y_s, p_lru_h, p_lru_conv, p_k, p_v, p_ssd, p_ssd_conv,
            s_lru_h, s_lru_conv, s_k, s_v, s_ssd, s_ssd_conv)
    return tuple(np.ascontiguousarray(o, dtype=np.float32) for o in outs)
```

```python
import os
import numpy as np
import concourse.bass as bass
import concourse.mybir as mybir
from concourse.bass_utils import run_bass_kernel_spmd
from contextlib import ExitStack

F32 = mybir.dt.float32
BF16 = mybir.dt.bfloat16
AF = mybir.ActivationFunctionType
ALU = mybir.AluOpType

D_MODEL = 1024
N_IN = 2950
D_FF = 2816
EPS = 1e-6
TS = 8
NB = 16
NCORES = 8


class Buf:
    def __init__(self, name):
        self.name = name
        self.w = None
        self.r = {}
        self.dsem = None
        self.dcnt = 0
        self.excl = False


class Tile:
    def __init__(self, ap, name, buf=None):
        self.ap = ap
        self.b = buf if buf is not None else Buf(name)

    def __getitem__(self, k):
        return self.ap[k]


class Sched:
    def __init__(self, nc, es):
        self.nc = nc
        self.es = es
        self.names = ['pe', 'act', 'dve', 'pool', 'sp']
        self.sem = {e: es.enter_context(nc.semaphore('s_' + e)) for e in self.names}
        self.cnt = {e: 0 for e in self.names}
        self.seen = {e: {} for e in self.names}
        self.q = {e: [] for e in self.names}
        self.dtoks = {}
        self.nsem = 5
        self.ninst = 0
        self.gseq = 0
        self.limit = int(os.environ.get('KLIMIT', '0'))
        self.marks = []

    def sb(self, name, shape, dt):
        t = self.es.enter_context(self.nc.sbuf_tensor(name, list(shape), dt))
        return Tile(t[tuple(slice(None) for _ in shape)], name)

    def ps(self, name, shape, dt):
        t = self.es.enter_context(self.nc.psum_tensor(name, list(shape), dt))
        r = Tile(t[tuple(slice(None) for _ in shape)], name)
        r.b.excl = True
        return r

    @staticmethod
    def _bufs(xs):
        return [x.b if isinstance(x, Tile) else x for x in xs]

    def _deps(self, e, reads, writes, skip_sem=None):
        need = {}

        def add(tok):
            s, v = tok
            k = id(s)
            if k not in need or need[k][1] < v:
                need[k] = (s, v)
        for b in reads:
            if b.w:
                add(b.w)
        for b in writes:
            if b.w and not (skip_sem is not None and b.w[0] is skip_sem):
                add(b.w)
            for tok in b.r.values():
                add(tok)
        out = []
        for k, (s, v) in need.items():
            if e == 'pe' and s is self.sem['pe']:
                continue
            if self.seen[e].get(k, 0) < v:
                self.seen[e][k] = v
                out.append((s, v))
        return out

    @staticmethod
    def _mark(tok, reads, writes):
        s, v = tok
        for b in reads:
            b.r[id(s)] = tok
        for b in writes:
            b.w = tok
            b.r = {}

    def op(self, e, fn, reads=(), writes=()):
        reads = self._bufs(reads)
        writes = self._bufs(writes)
        if e != 'pe':
            writes = writes + [b for b in reads if b.excl and b not in writes]
            reads = [b for b in reads if not b.excl]
        waits = self._deps(e, reads, writes)
        self.cnt[e] += 1
        tok = (self.sem[e], self.cnt[e])
        self._mark(tok, reads, writes)
        self.gseq += 1
        self.q[e].append((waits, fn, (self.sem[e], 1), self.gseq))
        self.ninst += 1 + len(waits)
        return tok

    def dma(self, e, out_ap, in_ap, reads=(), writes=(), sembuf=None):
        reads = self._bufs(reads)
        writes = self._bufs(writes)
        sb = sembuf.b if isinstance(sembuf, Tile) else sembuf
        if sb.dsem is None:
            sb.dsem = self.es.enter_context(self.nc.semaphore('d%d_%s' % (self.nsem, sb.name)))
            self.nsem += 1
        waits = self._deps(e, reads, writes, skip_sem=sb.dsem)
        sb.dcnt += 16
        tok = (sb.dsem, sb.dcnt)
        self._mark(tok, reads, writes)
        self.dtoks[id(sb.dsem)] = tok
        self.gseq += 1
        self.q[e].append((waits, lambda eng: eng.dma_start(out=out_ap, in_=in_ap), (sb.dsem, 16), self.gseq))
        self.ninst += 1 + len(waits)
        return tok

    def mark(self, label):
        self.marks.append((label, self.gseq))

    def wait_tok(self, e, tok):
        s, v = tok
        if e == 'pe' and s is self.sem['pe']:
            return
        if self.seen[e].get(id(s), 0) < v:
            self.seen[e][id(s)] = v
            self.gseq += 1
            self.q[e].append(([(s, v)], None, None, self.gseq))
            self.ninst += 1

    def barrier(self, engines=None):
        engines = engines or self.names
        toks = [(self.sem[o], self.cnt[o]) for o in self.names if self.cnt[o] > 0]
        toks += list(self.dtoks.values())
        for e in engines:
            for tok in toks:
                if tok[0] is self.sem.get(e):
                    continue
                self.wait_tok(e, tok)

    def emit(self):
        nc = self.nc
        S = self
        eng_of = {'pe': 'tensor', 'act': 'scalar', 'dve': 'vector', 'pool': 'gpsimd', 'sp': 'sync'}
        with nc.Block() as block:
            def mk(name):
                def run(eng):
                    for waits, fn, inc, seq in S.q[name]:
                        if S.limit and seq > S.limit:
                            break
                        for (s, v) in waits:
                            eng.wait_ge(s, v)
                        if fn is not None:
                            fn(eng).then_inc(inc[0], inc[1])
                return run
            for name in S.names:
                getattr(block, eng_of[name])(mk(name))


class Arena:
    def __init__(self, S, name, nel):
        self.t = S.sb(name, [128, nel], BF16)
        self.nel = nel
        self.off = 0
        self.hi = 0
        self.bufs = {}

    def reset(self):
        self.off = 0

    def carve(self, name, shape, dt):
        n = int(np.prod(shape[1:]))
        nel = n * (2 if dt == F32 else 1)
        if self.off % 2:
            self.off += 1
        assert self.off + nel <= self.nel, ("arena overflow", name, self.off + nel, self.nel)
        ap = self.t.ap[0:shape[0], self.off:self.off + nel]
        if dt == F32:
            ap = ap.bitcast(F32)
        if len(shape) == 3:
            ap = ap.rearrange("p (a b) -> p a b", b=shape[2])
        elif len(shape) == 4:
            ap = ap.rearrange("p (a b c) -> p a b c", b=shape[2], c=shape[3])
        self.off += nel
        self.hi = max(self.hi, self.off)
        if name not in self.bufs:
            self.bufs[name] = Buf(name)
        return Tile(ap, name, self.bufs[name])


def bc(ap, shape):
    return ap.to_broadcast(list(shape))


class Builder:
    def __init__(self, NT, CB, DEPTH):
        self.NT, self.CB, self.DEPTH = NT, CB, DEPTH
        self.WT = min(16, NT)
        self.nc = bass.Bass("TRN2", target_bir_lowering=False)
        self.din = {}
        self.dout = {}

    def I(self, name, shape, dt=F32):
        self.din[name] = self.nc.dram_tensor(name, list(shape), dt, kind="ExternalInput").ap()
        return self.din[name]

    def O(self, name, shape, dt=F32):
        self.dout[name] = self.nc.dram_tensor(name, list(shape), dt, kind="ExternalOutput").ap()
        return self.dout[name]

    def mm(self, out, lhsT, rhs, start, stop, reads, writes):
        self.S.op('pe', lambda e: e.matmul(out, lhsT, rhs, start=start, stop=stop), reads, writes)

    def tr(self, out, in_, ident, reads, writes):
        self.S.op('pe', lambda e: e.transpose(out, in_, ident), reads, writes)

    def act(self, out, in_, func, reads, writes, **kw):
        self.S.op('act', lambda e: e.activation(out, in_, func, **kw), reads, writes)

    def tt(self, eng, out, a, b, op, reads, writes):
        self.S.op(eng, lambda e: e.tensor_tensor(out, a, b, op), reads, writes)

    def tsc(self, eng, out, a, s1, s2, op0, op1, reads, writes):
        if s2 is None:
            self.S.op(eng, lambda e: e.tensor_scalar(out, a, s1, None, op0), reads, writes)
        else:
            self.S.op(eng, lambda e: e.tensor_scalar(out, a, s1, s2, op0, op1), reads, writes)

    def stt(self, out, in0, scalar, in1, op0, op1, reads, writes):
        self.S.op('dve', lambda e: e.scalar_tensor_tensor(out, in0, scalar, in1, op0, op1), reads, writes)

    def cp(self, eng, out, in_, reads, writes):
        if eng == 'act':
            self.S.op('act', lambda e: e.copy(out, in_), reads, writes)
        else:
            self.S.op(eng, lambda e: e.tensor_copy(out, in_), reads, writes)

    def memset(self, eng, ap, val, writes):
        self.S.op(eng, lambda e: e.memset(ap, val), (), writes)

    def recip(self, out, in_, reads, writes):
        self.S.op('dve', lambda e: e.reciprocal(out, in_), reads, writes)

    def gps(self):
        t = self.gbanks[self.gi % len(self.gbanks)]
        self.gi += 1
        return t

    def tps(self):
        t = self.tbanks[self.ti_ % len(self.tbanks)]
        self.ti_ += 1
        return t

    def build(self):
        NT, CB, L, WT = self.NT, self.CB, self.DEPTH, self.WT
        LB = CB * 128
        I, O = self.I, self.O
        I("xp", [NT, 128, 1024]); I("xs", [128, 1024])
        I("w_in", [L, 128, 8, N_IN]); I("w_outA", [L, 128, 5, 1024]); I("w_outC", [L, 128, 3, 1024])
        I("w_gu", [L, 128, 8, 2 * D_FF]); I("w_dn", [L, 128, 22, 1024])
        I("g4", [L, 4, 1024])
        I("cva_w", [L, 128, 2, 4]); I("cva_b", [L, 128, 2])
        I("cvx_w", [L, 64, 6, 4]); I("cvx_b", [L, 64, 6])
        I("cvbc_w", [L, 128, 4, 4]); I("cvbc_b", [L, 128, 4])
        I("wa_bd", [L, 128, 2, 128]); I("wx_bd", [L, 128, 2, 128]); I("lru_vec", [L, 128, 2, 3])
        I("ssd_h", [L, 18]); I("ssm_g", [L, 64, 6])
        I("st_lru_h", [L, 128, 2, NB]); I("st_lru_cv", [L, 128, 2, NB, 3])
        I("st_cvx", [L, 64, 6, NB, 3]); I("st_cvbc", [L, 128, 4, NB, 3])
        I("st_ssd", [L, 128, NB, 384]); I("kT_c", [L, NB, 3, 128, LB]); I("v_c", [L, NB, LB, 384])
        I("ident", [128, 128]); I("maskP", [128, 17, 128]); I("maskS", [128, CB + 1, 8])
        I("ssdc_p", [128, 3, 128]); I("ssdc_s", [128, 3, 128]); I("segm_s", [128, NB])
        O("y_p", [NT, 128, 1024]); O("y_s", [128, 1024])
        O("o_lru_h_p", [L, 128, 2, 1]); O("o_lru_h_s", [L, 128, 2, NB])
        O("o_lru_cv_p", [L, 128, 2, 1, 3]); O("o_lru_cv_s", [L, 128, 2, NB, 3])
        O("o_k_p", [L, WT, 128, 384]); O("o_v_p", [L, WT, 128, 384])
        O("o_k_s", [L, 128, 384]); O("o_v_s", [L, 128, 384])
        O("o_ssd_p", [L, 128, 1, 384]); O("o_ssd_s", [L, 128, NB, 384])
        O("o_cvx_p", [L, 64, 6, 1, 3]); O("o_cvx_s", [L, 64, 6, NB, 3])
        O("o_cvbc_p", [L, 128, 4, 1, 3]); O("o_cvbc_s", [L, 128, 4, NB, 3])
        self.xscr = self.nc.dram_tensor("xscr", [NT + 1, 128, 1024], F32, kind="Internal").ap()
        self.xscr_b = [Buf("xscr%d" % i) for i in range(NT + 1)]
        self.out_toks = []

        with ExitStack() as es:
            S = self.S = Sched(self.nc, es)
            self.identf = S.sb("identf", [128, 128], F32)
            self.identb = S.sb("identb", [128, 128], BF16)
            self.onesf = S.sb("onesf", [128, 128], F32)
            self.segm = S.sb("segm_sb", [128, NB], F32)
            self.gA = S.sb("gA", [128, 1024], F32)
            self.gB = S.sb("gB", [128, 1024], F32)
            self.xb = [S.sb("xt0", [128, 1024], F32)] * 2
            self.hn = S.sb("hn", [128, 1024], BF16)
            self.hT = S.sb("hT", [128, 8, 128], BF16)
            self.junk = self.hn
            self.sm = S.sb("small", [128, 64], F32)
            self.sm_b = [Buf("sm%d" % i) for i in range(16)]
            self.arena = Arena(S, "arena", 80500)
            self.gbanks = [S.ps("pg%d" % i, [128, 512], F32) for i in range(4)]
            self.tbanksf = [S.ps("pt%d" % i, [128, 512], F32) for i in range(2)]
            self.tbanks = [Tile(t.ap.bitcast(BF16), "ptb%d" % i, t.b) for i, t in enumerate(self.tbanksf)]
            self.plong = [S.ps("plong%d" % i, [128, 512], F32) for i in range(2)]
            self.pacc = Tile(self.plong[0][:, 0:390].rearrange("p (h d) -> p h d", d=65), "pacc", self.plong[0].b)
            self.gi = 0
            self.ti_ = 0
            S.dma('sp', self.identf[:], self.din["ident"], writes=[self.identf], sembuf=self.identf)
            S.dma('pool', self.identb[:], self.din["ident"], writes=[self.identb], sembuf=self.identb)
            S.dma('sp', self.segm[:], self.din["segm_s"], writes=[self.segm], sembuf=self.segm)
            self.memset('pool', self.onesf[:], 1.0, [self.onesf])

            for l in range(L):
                self.mixer_phase(l)
                self.ffn_phase(l)
            for tok in self.out_toks:
                S.wait_tok('sp', tok)
            S.barrier(['sp'])
            self.ninst = S.ninst
            S.emit()
        return self.nc

    def x_src(self, l, phase, i):
        if l == 0 and phase == 0:
            return (self.din["xp"][i] if i < self.NT else self.din["xs"]), None
        return self.xscr[i], self.xscr_b[i]

    def x_dst(self, l, phase, i):
        if l == self.DEPTH - 1 and phase == 1:
            return (self.dout["y_p"][i] if i < self.NT else self.dout["y_s"]), None
        return self.xscr[i], self.xscr_b[i]

    def load_x(self, l, phase, i, xt):
        ap, db = self.x_src(l, phase, i)
        self.S.dma('sp', xt[:], ap, reads=([db] if db else []), writes=[xt], sembuf=xt)

    def store_x(self, l, phase, i, xt):
        ap, db = self.x_dst(l, phase, i)
        tok = self.S.dma('sp', ap, xt[:], reads=[xt], writes=([db] if db else []), sembuf=xt)
        if db is None:
            self.out_toks.append(tok)

    def norm_A(self, xt, g_bc, hn=None):
        sm = self.sm
        hn = hn or self.hn
        b0 = self.sm_b[0]
        self.act(hn[:], xt[:], AF.Square, [xt], [hn, b0], accum_out=sm[:, 0:1])
        self.tsc('dve', sm[:, 0:1], sm[:, 0:1], 1.0 / D_MODEL, EPS, ALU.mult, ALU.add, [b0], [b0])
        self.act(sm[:, 0:1], sm[:, 0:1], AF.Sqrt, [b0], [b0])
        self.recip(sm[:, 0:1], sm[:, 0:1], [b0], [b0])
        self.stt(hn[:], xt[:], sm[:, 0:1], g_bc[:], ALU.mult, ALU.mult, [xt, b0, g_bc], [hn])

    def norm_B(self, hn=None, hT=None):
        hn = hn or self.hn
        hT = hT or self.hT
        pT = self.tps()
        pv = pT[:].rearrange("p (a b) -> p a b", b=128)
        for c in range(8):
            self.tr(pv[:, c, :], hn[:, c * 128:(c + 1) * 128], self.identb[:], [hn, self.identb], [pT])
        self.cp('act', hT[:], pv, [pT], [hT])

    def norm_T(self, xt, g_bc):
        self.norm_A(xt, g_bc)
        self.norm_B()

    def out_norm_residual(self, xt, pbanks, g_bc):
        sm = self.sm
        b1 = self.sm_b[1]
        self.act(self.junk[:, 0:512], pbanks[0][:], AF.Square, [pbanks[0]], [self.junk, b1], accum_out=sm[:, 1:2])
        self.act(self.junk[:, 512:1024], pbanks[1][:], AF.Square, [pbanks[1]], [self.junk, b1], accum_out=sm[:, 2:3])
        self.tt('dve', sm[:, 1:2], sm[:, 1:2], sm[:, 2:3], ALU.add, [b1], [b1])
        self.tsc('dve', sm[:, 1:2], sm[:, 1:2], 1.0 / D_MODEL, EPS, ALU.mult, ALU.add, [b1], [b1])
        self.act(sm[:, 1:2], sm[:, 1:2], AF.Sqrt, [b1], [b1])
        self.recip(sm[:, 1:2], sm[:, 1:2], [b1], [b1])
        tmp = self.otmp
        for hf in range(2):
            sl = slice(hf * 512, (hf + 1) * 512)
            self.stt(tmp[:, sl], pbanks[hf][:], sm[:, 1:2], g_bc[:, sl], ALU.mult, ALU.mult, [pbanks[hf], b1, g_bc], [tmp])
        self.tt('pool', xt[:], xt[:], tmp[:], ALU.add, [xt, tmp], [xt])

    def mixer_phase(self, l):
        S, NT, CB = self.S, self.NT, self.CB
        din = self.din
        S.barrier()
        A = self.arena
        A.reset()
        self.w_in = A.carve("w_in", [128, 8, N_IN], BF16)
        self.w_oA = A.carve("w_oA", [128, 5, 1024], BF16)
        self.w_oC = A.carve("w_oC", [128, 3, 1024], BF16)
        for c in range(8):
            S.dma('pool', self.w_in[:, c, :], din["w_in"][l, :, c, :], writes=[self.w_in], sembuf=self.w_in)
        S.dma('pool', self.w_oA[:], din["w_outA"][l], writes=[self.w_oA], sembuf=self.w_oA)
        S.dma('pool', self.w_oC[:], din["w_outC"][l], writes=[self.w_oC], sembuf=self.w_oC)
        S.dma('sp', self.gA[:], din["g4"][l, 0].partition_broadcast(128), writes=[self.gA], sembuf=self.gA)
        S.dma('sp', self.gB[:], din["g4"][l, 1].partition_broadcast(128), writes=[self.gB], sembuf=self.gB)
        p = self.prm = {}
        def ld(name, shape, src, dt=F32, q='sp'):
            t = A.carve(name, shape, dt)
            S.dma(q, t[tuple(slice(None) for _ in shape)], src, writes=[t], sembuf=t)
            p[name] = t
            return t
        ld("cva_w", [128, 2, 4], din["cva_w"][l]); ld("cva_b", [128, 2], din["cva_b"][l])
        ld("cvx_w", [64, 6, 4], din["cvx_w"][l]); ld("cvx_b", [64, 6], din["cvx_b"][l])
        ld("cvbc_w", [128, 4, 4], din["cvbc_w"][l]); ld("cvbc_b", [128, 4], din["cvbc_b"][l])
        ld("wa_bd", [128, 2, 128], din["wa_bd"][l], BF16, 'pool'); ld("wx_bd", [128, 2, 128], din["wx_bd"][l], BF16, 'pool')
        ld("lru_vec", [128, 2, 3], din["lru_vec"][l])
        ld("ssd_h", [128, 18], din["ssd_h"][l].partition_broadcast(128))
        ld("ssm_g", [64, 6], din["ssm_g"][l])
        self.maskP = ld("maskP", [128, 17, 128], din["maskP"], BF16, 'pool')
        self.maskS = ld("maskS", [128, CB + 1, 8], din["maskS"])
        self.ssdc = {True: ld("ssdc_p", [128, 3, 128], din["ssdc_p"]), False: ld("ssdc_s", [128, 3, 128], din["ssdc_s"])}
        cl = p["cl"] = A.carve("cl", [128, 2], F32)
        lam = p["lru_vec"][:, :, 2]
        self.act(cl[:], lam, AF.Exp, [p["lru_vec"]], [cl], scale=-1.0)
        self.act(cl[:], cl[:], AF.Ln, [cl], [cl], bias=1.0)
        self.tsc('dve', cl[:], cl[:], -8.0, None, ALU.mult, None, [cl], [cl])
        An = p["Aneg"] = A.carve("Aneg", [128, 6], F32)
        self.act(An[:], p["ssd_h"][:, 6:12], AF.Exp, [p["ssd_h"]], [An])
        self.tsc('dve', An[:], An[:], -1.0, None, ALU.mult, None, [An], [An])
        S.mark('mixer params')
        c = A.carve
        off0 = A.off
        self.KT = c("KT", [128, 3, 17, 128], BF16)
        self.KT_b = [Buf("KT%d" % i) for i in range(17)]
        self.V1 = c("V1", [128, 17, 6, 65], BF16)
        self.V1_b = [Buf("V1%d" % i) for i in range(17)]
        self.ebuf = [c("ebuf%d" % i, [128, 4, 128], BF16) for i in range(2)]
        self.pbuf = [c("pbuf%d" % i, [128, 4, 128], BF16) for i in range(2)]
        self.otok = c("otok", [128, 6, 64], BF16)
        self.st_p = c("st_p", [128, 384], F32)
        end_p = A.off
        A.off = off0
        self.KTc = c("KTc", [128, CB * 128], BF16)
        self.V1c = c("V1c", [128, CB + 1, 2, 65], BF16)
        self.es_ = c("es_", [128, CB + 1, 8], F32)
        self.ps_ = c("ps_", [128, CB + 1, 8], BF16)
        self.ob = c("ob", [8, 384], BF16)
        self.vnew = c("vnew", [8, 384], BF16)
        self.stb = [c("stb%d" % i, [128, 384], F32) for i in range(2)]
        self.xm = c("xm", [128, 384], BF16)
        self.KTn = c("KTn", [128, 3, 128], BF16)
        self.vtokb = c("vtokb", [128, 384], BF16)
        A.off = max(A.off, end_p)
        self.QT = c("QT", [128, 3, 128], BF16)
        self.kvst = [c("kvst%d" % i, [128, 384], F32) for i in range(2)]
        self.xpA = c("xpA", [128, 2, NB * (3 + TS)], F32)
        self.xpX = c("xpX", [64, 6, NB * (3 + TS)], F32)
        self.xpBC = c("xpBC", [128, 4, NB * (3 + TS)], F32)
        self.cA = c("cA", [128, 2, 128], F32)
        self.cX = c("cX", [64, 6, 128], F32)
        self.cBC = c("cBC", [128, 4, 128], F32)
        self.ctmp = c("ctmp", [128, 768], F32)
        self.ge = c("ge", [128, 2, 128], F32)
        self.xcb = c("xcb", [128, 2, 128], BF16)
        self.lr = c("lr", [128, 2, 128], F32)
        self.li = c("li", [128, 2, 128], F32)
        self.la = c("la", [128, 2, 128], F32)
        self.lu = c("lu", [128, 2, 128], F32)
        self.lh = c("lh", [128, 2, 128], F32)
        self.hst = {True: c("hst_p", [128, 2, 1], F32), False: c("hst_s", [128, 2, NB], F32)}
        self.lt = c("lt", [128, 2, NB], F32)
        self.mixT = c("mixT", [128, 5, 128], BF16)
        self.mixC = c("mixC", [128, 3, 128], BF16)
        self.rden = c("rden", [128, 6, 1], F32)
        self.sz = c("sz", [64, 6, 128], F32)
        self.xsTb = c("xsTb", [64, 6, 128], BF16)
        self.BCb = c("BCb", [128, 4, 128], BF16)
        self.dts = c("dts", [128, 4, 8], F32)
        self.xr = c("xr", [128, 6, 64], BF16)
        self.xrd = c("xrd", [128, 6, 64], BF16)
        self.Btok = c("Btok", [128, 256], BF16)
        RD = c("RD", [128, 12, 128], F32)
        self.R = Tile(RD[:, 0:6, :], "R", RD.b)
        self.CE = self.R
        self.Dm = Tile(RD[:, 6:12, :], "Dm", RD.b)
        self.otmp = Tile(RD[:, 0:8, :].rearrange("p a b -> p (a b)"), "otmp", RD.b)
        self.Eac = c("Eac", [128, 6, 128], F32)
        self.GT = c("GT", [128, 6, 128], BF16)
        self.R2 = c("R2", [128, NB, 6], F32)
        self.etot = c("etot", [128, NB, 6], F32)
        self.yy = c("yy", [64, 6, 128], F32)
        self.yt = Tile(self.ctmp[0:64, 0:768].rearrange("p (a b) -> p a b", b=128), "yt", self.ctmp.b)
        self.rs = c("rs", [64, 128], F32)
        self.memset('pool', self.st_p[:], 0.0, [self.st_p])
        self.memset('pool', self.hst[True][:], 0.0, [self.hst[True]])
        self.memset('pool', self.V1[:, :, :, 64:65], 1.0, self.V1_b)
        S.dma('sp', self.hst[False][:], din["st_lru_h"][l], writes=[self.hst[False]], sembuf=self.hst[False])

        for i in range(NT + 1):
            if i == NT:
                S.barrier()
                self.memset('pool', self.V1c[:, :, :, 64:65], 1.0, [self.V1c])
            self.load_x(l, 0, i, self.xb[0])
            self.mixer_tile(l, i)

    def mixer_tile(self, l, i):
        S, NT, CB = self.S, self.NT, self.CB
        din, dout, p = self.din, self.dout, self.prm
        isp = i < NT
        nseg, Ls = (1, 128) if isp else (NB, TS)
        W = 3 + Ls
        last = (i == NT - 1)
        want_kv = (not isp) or (i >= NT - self.WT)
        xt = self.xb[i % 2]
        S.mark('tile%d start' % i)
        self.norm_T(xt, self.gA)
        S.mark('tile%d normT' % i)
        hT, w_in = self.hT, self.w_in

        def proj_fm(ps_ap, ps_t, c0, M):
            for c in range(8):
                self.mm(ps_ap, w_in[:, c, c0:c0 + M], hT[:, c, :], c == 0, c == 7, [w_in, hT], [ps_t])

        def seg4(t, n):
            return t[:, :, 0:nseg * W].rearrange("p a (s w) -> p a s w", w=W)

        xpA4, xpX4, xpBC4 = seg4(self.xpA, 2), seg4(self.xpX, 6), seg4(self.xpBC, 4)
        if isp:
            if i == 0:
                for t, v in ((self.xpA, xpA4), (self.xpX, xpX4), (self.xpBC, xpBC4)):
                    self.memset('pool', v[:, :, :, 0:3], 0.0, [t])
            else:
                for t, v in ((self.xpA, xpA4), (self.xpX, xpX4), (self.xpBC, xpBC4)):
                    self.cp('pool', v[:, :, :, 0:3], v[:, :, :, Ls:Ls + 3], [t], [t])
        else:
            S.dma('sp', xpA4[:, :, :, 0:3], din["st_lru_cv"][l], writes=[self.xpA], sembuf=self.xpA)
            S.dma('sp', xpX4[:, :, :, 0:3], din["st_cvx"][l], writes=[self.xpX], sembuf=self.xpX)
            S.dma('sp', xpBC4[:, :, :, 0:3], din["st_cvbc"][l], writes=[self.xpBC], sembuf=self.xpBC)

        pg1 = self.gps()
        v1 = pg1[:].rearrange("p (a b) -> p a b", b=128)
        for t in range(4):
            proj_fm(v1[:, t, :], pg1, t * 128, 128)
        self.act(self.ge[:], v1[:, 0:2, :], AF.Gelu_apprx_tanh, [pg1], [self.ge])
        self.cp('dve', xpA4[:, :, :, 3:W], v1[:, 2:4, :].rearrange("p a (s w) -> p a s w", w=Ls), [pg1], [self.xpA])
        pq = self.gps()
        vq = pq[:].rearrange("p (a b) -> p a b", b=128)
        for t in range(3):
            proj_fm(vq[:, t, :], pq, 512 + t * 128, 128)
        self.cp('act', self.QT[:], vq[:, 0:3, :], [pq], [self.QT])
        pk = self.gps()
        vk = pk[:].rearrange("p (a b) -> p a b", b=128)
        for t in range(3):
            proj_fm(vk[:, t, :], pk, 896 + t * 128, 128)
        slot = i % 17
        if isp:
            self.cp('dve', self.KT[:, :, slot, :], vk[:, 0:3, :], [pk], [self.KT_b[slot]])
        else:
            self.cp('dve', self.KTn[:], vk[:, 0:3, :], [pk], [self.KTn])
        pv = self.gps()
        for c in range(8):
            self.mm(pv[:, 0:384], hT[:, c, :], w_in[:, c, 1280:1664], c == 0, c == 7, [hT, w_in], [pv])
        for c in range(8):
            self.mm(pv[:, 384:390], hT[:, c, :], w_in[:, c, 2944:2950], c == 0, c == 7, [hT, w_in], [pv])
        pv3 = pv[:, 0:384].rearrange("p (h d) -> p h d", d=64)
        if isp:
            self.cp('act', self.V1[:, slot, :, 0:64], pv3, [pv], [self.V1_b[slot]])
        else:
            self.cp('act', self.vtokb[:], pv[:, 0:384], [pv], [self.vtokb])
        self.cp('dve', self.dts[:, 0, 0:6], pv[:, 384:390], [pv], [self.dts])
        if want_kv:
            vs = self.kvst[0]
            self.cp('act', vs[:], pv[:, 0:384], [pv], [vs])
            dst = dout["o_v_p"][l, i - (NT - self.WT)] if isp else dout["o_v_s"][l]
            self.out_toks.append(S.dma('sp', dst, vs[:], reads=[vs], sembuf=vs))
            pk2 = self.gps()
            for c in range(8):
                self.mm(pk2[:, 0:384], hT[:, c, :], w_in[:, c, 896:1280], c == 0, c == 7, [hT, w_in], [pk2])
            ks = self.kvst[1]
            self.cp('act', ks[:], pk2[:, 0:384], [pk2], [ks])
            dst = dout["o_k_p"][l, i - (NT - self.WT)] if isp else dout["o_k_s"][l]
            self.out_toks.append(S.dma('sp', dst, ks[:], reads=[ks], sembuf=ks))
        xpX5 = self.xpX[:, :, 0:nseg * W].rearrange("p (c two) (s w) -> p c two s w", two=2, w=W)
        sz4 = self.sz[:].rearrange("p (c two) l -> p c two l", two=2)

        def proj_z():
            pz = self.gps()
            vz = pz[:, 0:384].rearrange("p (a b) -> p a b", b=128)
            for t in range(3):
                proj_fm(vz[:, t, :], pz, 1664 + t * 128, 128)
            for hh in range(2):
                self.act(sz4[:, :, hh, :], vz[hh * 64:(hh + 1) * 64, :, :], AF.Silu, [pz], [self.sz])

        def proj_x_bc():
            px = self.gps()
            vx = px[:, 0:384].rearrange("p (a b) -> p a b", b=128)
            for t in range(3):
                proj_fm(vx[:, t, :], px, 2048 + t * 128, 128)
            for hh in range(2):
                self.cp('dve', xpX5[:, :, hh, :, 3:W], vx[hh * 64:(hh + 1) * 64, :, :].rearrange("p a (s w) -> p a s w", w=Ls), [px], [self.xpX])
            pbc = self.gps()
            vbc = pbc[:].rearrange("p (a b) -> p a b", b=128)
            for t in range(4):
                proj_fm(vbc[:, t, :], pbc, 2432 + t * 128, 128)
            self.cp('act', xpBC4[:, :, :, 3:W], vbc.rearrange("p a (s w) -> p a s w", w=Ls), [pbc], [self.xpBC])

        def conv(P, n, xp_t, xp4, wt, bt, out_t, eng):
            o4 = out_t[:].rearrange("p a (s w) -> p a s w", w=Ls)
            tmp = self.ctmp[0:P, 0:n * 128].rearrange("p (a s w) -> p a s w", a=n, w=Ls)
            shp = [P, n, nseg, Ls]
            self.tt(eng, o4, xp4[:, :, :, 0:Ls], bc(wt[:, :, 0:1].unsqueeze(3), shp), ALU.mult, [xp_t, wt], [out_t])
            for j in range(1, 4):
                self.tt(eng, tmp, xp4[:, :, :, j:j + Ls], bc(wt[:, :, j:j + 1].unsqueeze(3), shp), ALU.mult, [xp_t, wt], [self.ctmp])
                self.tt(eng, o4, o4, tmp, ALU.add, [out_t, self.ctmp], [out_t])
            self.tt(eng, o4, o4, bc(bt[:].unsqueeze(2).unsqueeze(3), shp), ALU.add, [out_t, bt], [out_t])

        def state_out(nm, t, v):
            if last or not isp:
                sfx = "_p" if isp else "_s"
                self.out_toks.append(S.dma('sp', dout[nm + sfx][l], v[:, :, :, Ls:Ls + 3], reads=[t], sembuf=t))

        def conv_a():
            conv(128, 2, self.xpA, xpA4, p["cva_w"], p["cva_b"], self.cA, 'pool')
            state_out("o_lru_cv", self.xpA, xpA4)

        def conv_xbc():
            conv(64, 6, self.xpX, xpX4, p["cvx_w"], p["cvx_b"], self.cX, 'dve')
            conv(128, 4, self.xpBC, xpBC4, p["cvbc_w"], p["cvbc_b"], self.cBC, 'pool')
            state_out("o_cvx", self.xpX, xpX4)
            state_out("o_cvbc", self.xpBC, xpBC4)
            self.act(self.cX[:], self.cX[:], AF.Silu, [self.cX], [self.cX])
            self.act(self.cBC[:], self.cBC[:], AF.Silu, [self.cBC], [self.cBC])
            self.cp('pool', self.xsTb[:], self.cX[:], [self.cX], [self.xsTb])
            self.cp('pool', self.BCb[:], self.cBC[:], [self.cBC], [self.BCb])

        gl = self.lru(l, i, isp, nseg, Ls, last)
        gs = self.ssd(l, i, isp, nseg, Ls, last)
        if isp:
            self.kctr = 0
            proj_z()
            proj_x_bc()
            conv_a()
            conv_xbc()
            next(gl)
            self.attn_head(i, 0)
            next(gs)
            self.attn_head(i, 1)
            next(gl)
            self.attn_head(i, 2)
            next(gs)
            self.attn_head(i, 3)
            for _ in gl:
                pass
            self.attn_head(i, 4)
            self.attn_head(i, 5)
            self.attn_finish()
            for _ in gs:
                pass
        else:
            proj_z()
            proj_x_bc()
            conv_a()
            conv_xbc()
            for _ in gl:
                pass
            self.attn_sample(l)
            for _ in gs:
                pass
        S.mark('tile%d mix' % i)

        po = [self.gps(), self.gps()]
        for hf in range(2):
            sl = slice(hf * 512, (hf + 1) * 512)
            for cidx in range(5):
                self.mm(po[hf][:], self.mixT[:, cidx, :], self.w_oA[:, cidx, sl], cidx == 0, False, [self.mixT, self.w_oA], [po[hf]])
            for h in range(3):
                self.mm(po[hf][:], self.mixC[:, h, :], self.w_oC[:, h, sl], False, h == 2, [self.mixC, self.w_oC], [po[hf]])
        self.out_norm_residual(xt, po, self.gB)
        self.store_x(l, 0, i, xt)
        S.mark('tile%d done' % i)

    def lru(self, l, i, isp, nseg, Ls, last):
        S, p = self.S, self.prm
        cA, xcb, lr, li, la, lu, lh = self.cA, self.xcb, self.lr, self.li, self.la, self.lu, self.lh
        hst = self.hst[isp]
        self.cp('pool', xcb[:], cA[:], [cA], [xcb])
        pr = self.gps()
        v = pr[:].rearrange("p (a b) -> p a b", b=128)
        for t in range(2):
            self.mm(v[:, t, :], p["wa_bd"][:, t, :], xcb[:, t, :], True, True, [p["wa_bd"], xcb], [pr])
            self.mm(v[:, 2 + t, :], p["wx_bd"][:, t, :], xcb[:, t, :], True, True, [p["wx_bd"], xcb], [pr])
        for t in range(2):
            self.act(lr[:, t, :], v[:, t, :], AF.Sigmoid, [pr, p["lru_vec"]], [lr], bias=p["lru_vec"][:, t, 0:1])
            self.act(li[:, t, :], v[:, 2 + t, :], AF.Sigmoid, [pr, p["lru_vec"]], [li], bias=p["lru_vec"][:, t, 1:2])
        for t in range(2):
            self.act(la[:, t, :], lr[:, t, :], AF.Exp, [lr, p["cl"]], [la], scale=p["cl"][:, t:t + 1])
        yield
        self.tt('pool', lr[:], la[:], la[:], ALU.mult, [la], [lr])
        self.tsc('pool', lr[:], lr[:], -1.0, 1.0, ALU.mult, ALU.add, [lr], [lr])
        self.act(lr[:], lr[:], AF.Sqrt, [lr], [lr])
        self.tt('dve', li[:], li[:], cA[:], ALU.mult, [li, cA], [li])
        self.tt('dve', lu[:], lr[:], li[:], ALU.mult, [lr, li], [lu])
        a0 = la[:].rearrange("p a (s w) -> p a s w", w=Ls)[:, :, :, 0]
        u0 = lu[:].rearrange("p a (s w) -> p a s w", w=Ls)[:, :, :, 0]
        lt = self.lt[:, :, 0:nseg]
        self.tt('dve', lt, a0, hst[:], ALU.mult, [la, hst], [self.lt])
        self.tt('dve', u0, u0, lt, ALU.add, [lu, self.lt], [lu])
        self.memset('dve', a0, 0.0, [la])
        for t in range(2):
            S.op('dve', (lambda t: (lambda e: e.tensor_tensor_scan(lh[:, t, :], la[:, t, :], lu[:, t, :], 0.0, ALU.mult, ALU.add)))(t), [la, lu], [lh])
        yield
        hl = lh[:].rearrange("p a (s w) -> p a s w", w=Ls)[:, :, :, Ls - 1]
        self.cp('pool', hst[:], hl, [lh], [hst])
        if last or not isp:
            dst = self.dout["o_lru_h_p" if isp else "o_lru_h_s"][l]
            self.out_toks.append(S.dma('sp', dst, hst[:], reads=[hst], sembuf=hst))
        self.tt('dve', self.mixT[:, 0:2, :], lh[:], self.ge[:], ALU.mult, [lh, self.ge], [self.mixT])

    def attn_head(self, i, h):
        pacc = self.pacc
        nk = min(16, i) + 1
        pr_, hh = h // 2, h % 2
        rows = slice(hh * 64, hh * 64 + 64)
        batches = [(o0, min(4, nk - o0)) for o0 in range(0, nk, 4)]

        def s_stage(bi):
            o0, nb = batches[bi]
            ps = self.gps()
            v = ps[:].rearrange("p (a b) -> p a b", b=128)
            for jj in range(nb):
                sj = (i - (o0 + jj)) % 17
                self.mm(v[:, jj, :], self.KT[rows, pr_, sj, :], self.QT[rows, pr_, :], True, True, [self.KT_b[sj], self.QT], [ps])
            return ps, v

        def rest(bi, ps, v):
            o0, nb = batches[bi]
            k = self.kctr
            self.kctr += 1
            eb, pb = self.ebuf[k % 2], self.pbuf[k % 2]
            self.act(eb[:, 0:nb, :], v[:, 0:nb, :], AF.Exp, [ps], [eb], scale=0.125)
            self.tt('dve' if k % 2 else 'pool', pb[:, 0:nb, :], eb[:, 0:nb, :], self.maskP[:, o0:o0 + nb, :], ALU.mult, [eb, self.maskP], [pb])
            for jj in range(nb):
                o = o0 + jj
                sj = (i - o) % 17
                self.mm(pacc[:, h, :], pb[:, jj, :], self.V1[:, sj, h, :], o == 0, o == nk - 1, [pb, self.V1_b[sj]], [pacc])
        cur = s_stage(0)
        for bi in range(len(batches)):
            nxt = s_stage(bi + 1) if bi + 1 < len(batches) else None
            rest(bi, *cur)
            cur = nxt

    def attn_finish(self):
        pacc = self.pacc
        self.recip(self.rden[:], pacc[:, :, 64:65], [pacc], [self.rden])
        self.tt('dve', self.otok[:], pacc[:, :, 0:64], bc(self.rden[:], [128, 6, 64]), ALU.mult, [pacc, self.rden], [self.otok])
        pT = self.tps()
        pv = pT[:].rearrange("p (a b) -> p a b", b=128)
        of = self.otok[:].rearrange("p h d -> p (h d)")
        for c in range(3):
            self.tr(pv[:, c, :], of[:, c * 128:(c + 1) * 128], self.identb[:], [self.otok, self.identb], [pT])
        self.cp('act', self.mixT[:, 2:5, :], pv[:, 0:3, :], [pT], [self.mixT])

    def attn_sample(self, l):
        S, CB = self.S, self.CB
        din = self.din
        pacc = self.pacc
        KTc, V1c = self.KTc, self.V1c
        for b in range(NB):
            cs = slice(b * 8, b * 8 + 8)
            pn = self.gps()
            self.mm(pn[0:8, 0:384], self.identb[:, cs], self.vtokb[:], True, True, [self.identb, self.vtokb], [pn])
            self.cp('act', self.vnew[:], pn[0:8, 0:384], [pn], [self.vnew])
            for pr_ in range(3):
                S.dma('pool', KTc[:], din["kT_c"][l, b, pr_], writes=[KTc], sembuf=KTc)
                vsrc = din["v_c"][l, b].rearrange("(blk k) (h d) -> k blk h d", k=128, d=64)
                for hh in range(2):
                    S.dma('pool', V1c[:, 0:CB, hh, 0:64], vsrc[:, :, 2 * pr_ + hh, :], writes=[V1c], sembuf=V1c)
                self.cp('act', V1c[0:8, CB, :, 0:64], self.vnew[0:8, pr_ * 128:(pr_ + 1) * 128].rearrange("p (h d) -> p h d", d=64), [self.vnew], [V1c])
                for hh in range(2):
                    h = pr_ * 2 + hh
                    rows = slice(hh * 64, hh * 64 + 64)
                    ps = self.gps()
                    v = ps[:, 0:(CB + 1) * 8].rearrange("p (a b) -> p a b", b=8)
                    for blk in range(CB):
                        self.mm(v[:, blk, :], KTc[rows, blk * 128:(blk + 1) * 128], self.QT[rows, pr_, cs], True, True, [KTc, self.QT], [ps])
                    self.mm(v[0:8, CB, :], self.KTn[rows, pr_, cs], self.QT[rows, pr_, cs], True, True, [self.KTn, self.QT], [ps])
                    self.act(self.es_[:], v, AF.Exp, [ps], [self.es_], scale=0.125)
                    self.tt('dve', self.ps_[:], self.es_[:], self.maskS[:], ALU.mult, [self.es_, self.maskS], [self.ps_])
                    for blk in range(CB):
                        self.mm(pacc[0:8, h, :], self.ps_[:, blk, :], V1c[:, blk, hh, :], blk == 0, False, [self.ps_, V1c], [pacc])
                    self.mm(pacc[0:8, h, :], self.ps_[0:8, CB, :], V1c[0:8, CB, hh, :], False, True, [self.ps_, V1c], [pacc])
            self.recip(self.rden[0:8], pacc[0:8, :, 64:65], [pacc], [self.rden])
            self.tt('dve', self.ob[:].rearrange("p (h d) -> p h d", d=64), pacc[0:8, :, 0:64], bc(self.rden[0:8], [8, 6, 64]), ALU.mult, [pacc, self.rden], [self.ob])
            pT = self.tps()
            pv = pT[:].rearrange("p (a b) -> p a b", b=128)
            for c in range(3):
                self.tr(pv[:, c, 0:8], self.ob[0:8, c * 128:(c + 1) * 128], self.identb[0:8, 0:8], [self.ob, self.identb], [pT])
            self.cp('act', self.mixT[:, 2:5, cs], pv[:, 0:3, 0:8], [pT], [self.mixT])

    def ssd(self, l, i, isp, nseg, Ls, last):
        S, p = self.S, self.prm
        dts = self.dts
        sc = self.ssdc[isp]
        tri, neg, same = sc[:, 0, :], sc[:, 1, :], sc[:, 2, :]
        hp = p["ssd_h"]
        self.tt('dve', dts[:, 0, 0:6], dts[:, 0, 0:6], hp[:, 0:6], ALU.add, [dts, hp], [dts])
        self.act(dts[:, 0, 0:6], dts[:, 0, 0:6], AF.Exp, [dts], [dts])
        self.act(dts[:, 0, 0:6], dts[:, 0, 0:6], AF.Ln, [dts], [dts], bias=1.0)
        self.tt('dve', dts[:, 1, 0:6], dts[:, 0, 0:6], p["Aneg"][:], ALU.mult, [dts, p["Aneg"]], [dts])
        dt, dtA = dts[:, 0, 0:6], dts[:, 1, 0:6]
        pT = self.tps()
        for h in range(6):
            self.tr(pT[:, h * 64:(h + 1) * 64], self.xsTb[:, h, :], self.identb[0:64, 0:64], [self.xsTb, self.identb], [pT])
        self.tt('dve', self.xr[:], pT[:, 0:384].rearrange("p (h d) -> p h d", d=64), bc(dt.unsqueeze(2), [128, 6, 64]), ALU.mult, [pT, dts], [self.xr])
        pT2 = self.tps()
        for t in range(2):
            self.tr(pT2[:, t * 128:(t + 1) * 128], self.BCb[:, t, :], self.identb[:], [self.BCb, self.identb], [pT2])
        self.cp('act', self.Btok[:], pT2[:, 0:256], [pT2], [self.Btok])
        self.tt('pool', self.R[:], bc(tri.unsqueeze(1), [128, 6, 128]), bc(dtA.unsqueeze(2), [128, 6, 128]), ALU.mult, [sc, dts], [self.R])
        pa = [self.gps(), self.gps()]
        for hb in range(2):
            self.mm(pa[hb][:, 0:384], self.onesf[:], self.R[:, hb * 3:hb * 3 + 3, :].rearrange("p a b -> p (a b)"), True, True, [self.onesf, self.R], [pa[hb]])
        pb = self.gps()
        self.mm(pb[:, 0:6], tri, dtA, True, True, [sc, dts], [pb])
        self.mm(pb[:, 8:14], same, dtA, True, True, [sc, dts], [pb])
        self.cp('act', dts[:, 2, 0:6], pb[:, 0:6], [pb], [dts])
        self.cp('act', dts[:, 3, 0:6], pb[:, 8:14], [pb], [dts])
        acs, tot = dts[:, 2, 0:6], dts[:, 3, 0:6]
        for hb in range(2):
            pav = pa[hb][:, 0:384].rearrange("p (a b) -> p a b", b=128)
            self.tt('dve', self.Dm[:, hb * 3:hb * 3 + 3, :], pav, bc(dts[:, 2, hb * 3:hb * 3 + 3].unsqueeze(2), [128, 3, 128]), ALU.subtract, [pa[hb], dts], [self.Dm])
            self.act(self.Eac[:, hb * 3:hb * 3 + 3, :], pav, AF.Exp, [pa[hb]], [self.Eac])
        yield
        self.tt('pool', self.Dm[:], self.Dm[:], bc(neg.unsqueeze(1), [128, 6, 128]), ALU.add, [self.Dm, sc], [self.Dm])
        self.act(self.Dm[:], self.Dm[:], AF.Exp, [self.Dm], [self.Dm])
        pc = self.gps()
        pcv = pc[:, 0:256].rearrange("p (a b) -> p a b", b=128)
        for g in range(2):
            self.mm(pcv[:, g, :], self.BCb[:, g, :], self.BCb[:, 2 + g, :], True, True, [self.BCb], [pc])
        for g in range(2):
            self.tt('dve', self.GT[:, g * 3:g * 3 + 3, :], self.Dm[:, g * 3:g * 3 + 3, :], bc(pcv[:, g:g + 1, :], [128, 3, 128]), ALU.mult, [self.Dm, pc], [self.GT])
            self.tt('pool', self.CE[:, g * 3:g * 3 + 3, :], self.Eac[:, g * 3:g * 3 + 3, :], bc(self.cBC[:, 2 + g:3 + g, :], [128, 3, 128]), ALU.mult, [self.Eac, self.cBC], [self.CE])
        self.tt('dve', dts[:, 3, 0:6], tot, acs, ALU.subtract, [dts], [dts])
        self.act(dts[:, 3, 0:6], dts[:, 3, 0:6], AF.Exp, [dts], [dts])
        self.tt('dve', self.xrd[:], self.xr[:], bc(dts[:, 3, 0:6].unsqueeze(2), [128, 6, 64]), ALU.mult, [self.xr, dts], [self.xrd])
        R2 = self.R2[:, 0:nseg, :]
        if nseg > 1:
            self.tt('dve', R2, bc(dtA.unsqueeze(1), [128, nseg, 6]), bc(self.segm[:, 0:nseg].unsqueeze(2), [128, nseg, 6]), ALU.mult, [dts, self.segm], [self.R2])
        else:
            self.cp('dve', R2, dtA.unsqueeze(1), [dts], [self.R2])
        pe_ = self.gps()
        self.mm(pe_[:, 0:nseg * 6], self.onesf[:], R2.rearrange("p a b -> p (a b)"), True, True, [self.onesf, self.R2], [pe_])
        et = self.etot[:, 0:nseg, :]
        self.act(et, pe_[:, 0:nseg * 6].rearrange("p (a b) -> p a b", b=6), AF.Exp, [pe_], [self.etot])
        xrdf = self.xrd[:].rearrange("p h d -> p (h d)")
        yield
        py = self.plong
        for h in range(6):
            o = py[h // 3][0:64, (h % 3) * 128:(h % 3 + 1) * 128]
            self.mm(o, self.xr[:, h, :], self.GT[:, h, :], True, True, [self.xr, self.GT], [py[h // 3]])
        pyo = self.tbanksf
        for b in range(nseg):
            if isp:
                st = self.st_p
            else:
                st = self.stb[b % 2]
                S.dma('sp', st[:], self.din["st_ssd"][l, :, b, :], writes=[st], sembuf=st)
            for h in range(6):
                o = pyo[h // 3][0:64, (h % 3) * 128:(h % 3 + 1) * 128]
                self.mm(o[:, b * Ls:(b + 1) * Ls], st[:, h * 64:(h + 1) * 64], self.CE[:, h, b * Ls:(b + 1) * Ls], True, True, [st, self.CE], [pyo[h // 3]])
            if nseg > 1:
                self.tsc('dve', self.xm[:], xrdf, self.segm[:, b:b + 1], None, ALU.mult, None, [self.xrd, self.segm], [self.xm])
                xm, xm_t = self.xm[:], self.xm
            else:
                xm, xm_t = xrdf, self.xrd
            pst = self.gps()
            for g in range(2):
                self.mm(pst[:, g * 192:(g + 1) * 192], self.Btok[:, g * 128:(g + 1) * 128], xm[:, g * 192:(g + 1) * 192], True, True, [self.Btok, xm_t], [pst])
            sb3 = st[:].rearrange("p (h d) -> p h d", d=64)
            self.tt('pool', sb3, sb3, bc(self.etot[:, b, :].unsqueeze(2), [128, 6, 64]), ALU.mult, [st, self.etot], [st])
            self.tt('dve', st[:], st[:], pst[:, 0:384], ALU.add, [st, pst], [st])
            if not isp:
                self.out_toks.append(S.dma('sp', self.dout["o_ssd_s"][l, :, b, :], st[:], reads=[st], sembuf=st))
        if isp and last:
            self.out_toks.append(S.dma('sp', self.dout["o_ssd_p"][l, :, 0, :], self.st_p[:], reads=[self.st_p], sembuf=self.st_p))
        yy, yt = self.yy, self.yt
        self.tt('pool', yt[:], self.cX[:], bc(hp[0:64, 12:18].unsqueeze(2), [64, 6, 128]), ALU.mult, [self.cX, hp], [yt])
        for hb in range(2):
            self.tt('dve', yy[:, hb * 3:hb * 3 + 3, :], py[hb][0:64, 0:384].rearrange("p (a b) -> p a b", b=128), yt[:, hb * 3:hb * 3 + 3, :], ALU.add, [py[hb], yt], [yy])
            self.tt('dve', yy[:, hb * 3:hb * 3 + 3, :], pyo[hb][0:64, 0:384].rearrange("p (a b) -> p a b", b=128), yy[:, hb * 3:hb * 3 + 3, :], ALU.add, [pyo[hb], yy], [yy])
        self.tt('dve', yy[:], yy[:], self.sz[:], ALU.mult, [yy, self.sz], [yy])
        self.tt('pool', yt[:], yy[:], yy[:], ALU.mult, [yy], [yt])
        pss = self.gps()
        for h in range(6):
            self.mm(pss[0:64, 0:128], self.onesf[0:64, 0:64], yt[:, h, :], h == 0, h == 5, [self.onesf, yt], [pss])
        rs = self.rs
        self.tsc('dve', rs[:], pss[0:64, 0:128], 1.0 / 384.0, EPS, ALU.mult, ALU.add, [pss], [rs])
        self.act(rs[:], rs[:], AF.Sqrt, [rs], [rs])
        self.recip(rs[:], rs[:], [rs], [rs])
        self.tt('dve', yy[:], yy[:], bc(rs[:].unsqueeze(1), [64, 6, 128]), ALU.mult, [yy, rs], [yy])
        yy2 = yy[:].rearrange("p (c two) l -> p c two l", two=2)
        sg2 = p["ssm_g"][:].rearrange("p (c two) -> p c two", two=2)
        for hh in range(2):
            self.tt('dve', self.mixC[hh * 64:(hh + 1) * 64, :, :], yy2[:, :, hh, :], bc(sg2[:, :, hh].unsqueeze(2), [64, 3, 128]), ALU.mult, [yy, p["ssm_g"]], [self.mixC])

    def ffn_phase(self, l):
        S, NT = self.S, self.NT
        din = self.din
        S.barrier()
        A = self.arena
        A.reset()
        w_gu = A.carve("w_gu", [128, 8, 2 * D_FF], BF16)
        w_dn = A.carve("w_dn", [128, 22, 1024], BF16)
        for c in range(8):
            S.dma('pool', w_gu[:, c, :], din["w_gu"][l, :, c, :], writes=[w_gu], sembuf=w_gu)
        for c0 in range(0, 22, 6):
            c1 = min(22, c0 + 6)
            S.dma('pool', w_dn[:, c0:c1, :], din["w_dn"][l, :, c0:c1, :], writes=[w_dn], sembuf=w_dn)
        S.dma('sp', self.gA[:], din["g4"][l, 2].partition_broadcast(128), writes=[self.gA], sembuf=self.gA)
        S.dma('sp', self.gB[:], din["g4"][l, 3].partition_broadcast(128), writes=[self.gB], sembuf=self.gB)
        actb = A.carve("actb", [128, D_FF], BF16)
        self.otmp = Tile(actb[:, 0:2048].bitcast(F32), "otmp", actb.b)
        actT = A.carve("actT", [128, 22, 128], BF16)
        sg = [A.carve("sg%d" % i, [128, 512], F32) for i in range(2)]
        widths = [512] * 5 + [256]
        S.mark('ffn start')
        xbs = [self.xb[0], A.carve("xb2", [128, 1024], F32)]
        hns = [self.hn, A.carve("hn2", [128, 1024], BF16)]
        hTs = [self.hT, A.carve("hT2", [128, 8, 128], BF16)]
        self.load_x(l, 1, 0, xbs[0])
        self.norm_A(xbs[0], self.gA, hns[0])
        self.norm_B(hns[0], hTs[0])
        for i in range(NT + 1):
            xt = xbs[i % 2]
            if i + 1 <= NT:
                self.load_x(l, 1, i + 1, xbs[(i + 1) % 2])
            hT = hTs[i % 2]
            off = 0
            for j, w in enumerate(widths):
                pg, pu = self.gps(), self.gps()
                for c in range(8):
                    self.mm(pg[:, 0:w], hT[:, c, :], w_gu[:, c, off:off + w], c == 0, c == 7, [hT, w_gu], [pg])
                for c in range(8):
                    self.mm(pu[:, 0:w], hT[:, c, :], w_gu[:, c, D_FF + off:D_FF + off + w], c == 0, c == 7, [hT, w_gu], [pu])
                s = sg[j % 2]
                self.act(s[:, 0:w], pg[:, 0:w], AF.Silu, [pg], [s])
                self.tt('dve', actb[:, off:off + w], s[:, 0:w], pu[:, 0:w], ALU.mult, [s, pu], [actb])
                off += w
            for k0 in range(0, 22, 8):
                k1 = min(22, k0 + 8)
                pT = self.tps()
                pv = pT[:].rearrange("p (a b) -> p a b", b=128)
                for k in range(k0, k1):
                    self.tr(pv[:, k - k0, :], actb[:, k * 128:(k + 1) * 128], self.identb[:], [actb, self.identb], [pT])
                self.cp('act' if (k0 // 8) % 2 == 0 else 'dve', actT[:, k0:k1, :], pv[:, 0:k1 - k0, :], [pT], [actT])
            if i + 1 <= NT:
                self.norm_A(xbs[(i + 1) % 2], self.gA, hns[(i + 1) % 2])
            po = [self.gps(), self.gps()]
            for hf in range(2):
                for k in range(22):
                    self.mm(po[hf][:], actT[:, k, :], w_dn[:, k, hf * 512:(hf + 1) * 512], k == 0, k == 21, [actT, w_dn], [po[hf]])
            if i + 1 <= NT:
                self.norm_B(hns[(i + 1) % 2], hTs[(i + 1) % 2])
            self.junk = hns[i % 2]
            self.out_norm_residual(xt, po, self.gB)
            self.junk = self.hn
            self.store_x(l, 1, i, xt)


def _mult(dist):
    dist = np.asarray(dist)
    m = ((dist >= 0) & (dist <= 128)).astype(np.float32)
    m += ((dist >= 0) & (dist <= 512) & (dist % 4 == 0))
    m += ((dist >= 0) & (dist <= 2048) & (dist % 16 == 0))
    return m.astype(np.float32)


def _consts(CB):
    LB = CB * 128
    kl = np.arange(128)[:, None, None]
    o = np.arange(17)[None, :, None]
    ql = np.arange(128)[None, None, :]
    maskP = _mult(ql + 128 * o - kl)
    blk = np.arange(CB + 1)[None, :, None]
    t = np.arange(8)[None, None, :]
    r = blk * 128 + kl
    maskS = _mult(LB + t - r)
    maskS[8:, CB, :] = 0.0
    maskS = maskS.astype(np.float32)

    def ssdc(Ls):
        k = np.arange(128)
        seg = k // Ls
        same = (seg[:, None] == seg[None, :])
        tri = same & (k[:, None] <= k[None, :])
        neg = np.where(tri, 0.0, -30000.0)
        return np.stack([tri.astype(np.float32), neg.astype(np.float32), same.astype(np.float32)], 1)
    segm = (np.arange(128)[:, None] // TS == np.arange(NB)[None, :]).astype(np.float32)
    return dict(ident=np.eye(128, dtype=np.float32), maskP=np.ascontiguousarray(maskP), maskS=maskS,
                ssdc_p=np.ascontiguousarray(ssdc(128)), ssdc_s=np.ascontiguousarray(ssdc(TS)), segm_s=segm)


def _ct(a, P):
    sh = a.shape
    n = sh[-1] // P
    return np.moveaxis(a.reshape(sh[:-1] + (n, P)), -1, -2)


_CACHE = {}


def _RUN(nc, in_maps, core_ids):
    return run_bass_kernel_spmd(nc, in_maps, core_ids=core_ids)


def kernel(x_prompt, x_sample, state_lru_h, state_lru_conv, cache_swa_k, cache_swa_v, state_ssd, state_ssd_conv,
           norm_mix_in, norm_mix_out, w_in, conv_a_w, conv_a_b, lru_wa, lru_ba, lru_wx, lru_bx, lru_lambda,
           conv_c_w, conv_c_b, dt_bias, a_log, d_skip, ssm_norm, w_out, norm_ffn_in, norm_ffn_out,
           w_gate_up, w_down):
    f = lambda a: np.ascontiguousarray(np.asarray(a, dtype=np.float32))
    x_prompt, x_sample = f(x_prompt), f(x_sample)
    BATCH, SEQ, _ = x_prompt.shape
    L = w_in.shape[0]
    DB = x_sample.shape[0]
    LB = cache_swa_k.shape[2]
    NT, CB = SEQ // 128, LB // 128
    assert DB == NB * NCORES and x_sample.shape[1] == TS and BATCH * 4 == NCORES
    key = (NT, CB, L)
    if key not in _CACHE:
        bld = Builder(NT, CB, L)
        bld.build()
        _CACHE[key] = bld
    bld = _CACHE[key]
    WT = bld.WT

    sh = {}
    sh["w_in"] = f(np.asarray(w_in).reshape(L, 8, 128, N_IN).transpose(0, 2, 1, 3))
    wo = np.asarray(w_out)
    sh["w_outA"] = f(wo[:, 0:640].reshape(L, 5, 128, 1024).transpose(0, 2, 1, 3))
    sh["w_outC"] = f(wo[:, 640:1024].reshape(L, 3, 128, 1024).transpose(0, 2, 1, 3))
    sh["w_gu"] = f(np.asarray(w_gate_up).reshape(L, 8, 128, 2 * D_FF).transpose(0, 2, 1, 3))
    sh["w_dn"] = f(np.asarray(w_down).reshape(L, 22, 128, 1024).transpose(0, 2, 1, 3))
    sh["g4"] = f(np.stack([norm_mix_in, norm_mix_out, norm_ffn_in, norm_ffn_out], 1))
    sh["cva_w"] = f(_ct(np.asarray(conv_a_w), 128).transpose(0, 2, 3, 1))
    sh["cva_b"] = f(_ct(np.asarray(conv_a_b), 128))
    ccw, ccb = np.asarray(conv_c_w), np.asarray(conv_c_b)
    sh["cvx_w"] = f(_ct(ccw[:, :, 0:384], 64).transpose(0, 2, 3, 1))
    sh["cvx_b"] = f(_ct(ccb[:, 0:384], 64))
    sh["cvbc_w"] = f(_ct(ccw[:, :, 384:896], 128).transpose(0, 2, 3, 1))
    sh["cvbc_b"] = f(_ct(ccb[:, 384:896], 128))
    def bd(w):
        w = np.asarray(w)
        o = np.zeros((L, 128, 2, 128), np.float32)
        for t in range(2):
            for q in range(2):
                o[:, q * 64:(q + 1) * 64, t, q * 64:(q + 1) * 64] = w[:, t * 2 + q]
        return o
    sh["wa_bd"], sh["wx_bd"] = bd(lru_wa), bd(lru_wx)
    sh["lru_vec"] = f(np.stack([_ct(np.asarray(lru_ba), 128), _ct(np.asarray(lru_bx), 128), _ct(np.asarray(lru_lambda), 128)], -1))
    sh["ssd_h"] = f(np.concatenate([dt_bias, a_log, d_skip], 1))
    sh["ssm_g"] = f(_ct(np.asarray(ssm_norm), 64))
    sh.update(_consts(CB))

    slh, slc = np.asarray(state_lru_h), np.asarray(state_lru_conv)
    ssc, sss = np.asarray(state_ssd_conv), np.asarray(state_ssd)
    ck, cv = np.asarray(cache_swa_k), np.asarray(cache_swa_v)
    in_maps = []
    for c in range(NCORES):
        bs = slice(c * NB, (c + 1) * NB)
        m = dict(sh)
        m["xp"] = x_prompt[c // 4].reshape(NT, 128, 1024)
        m["xs"] = x_sample[bs].reshape(128, 1024)
        m["st_lru_h"] = f(_ct(slh[:, bs], 128).transpose(0, 2, 3, 1))
        m["st_lru_cv"] = f(_ct(slc[:, bs], 128).transpose(0, 3, 4, 1, 2))
        m["st_cvx"] = f(_ct(ssc[:, bs, :, 0:384], 64).transpose(0, 3, 4, 1, 2))
        m["st_cvbc"] = f(_ct(ssc[:, bs, :, 384:896], 128).transpose(0, 3, 4, 1, 2))
        m["st_ssd"] = f(sss[:, bs].transpose(0, 4, 1, 2, 3).reshape(L, 128, NB, 384))
        m["kT_c"] = f(ck[:, bs].reshape(L, NB, LB, 3, 128).transpose(0, 1, 3, 4, 2))
        m["v_c"] = f(cv[:, bs].reshape(L, NB, LB, 384))
        in_maps.append(m)

    res = _RUN(bld.nc, in_maps, core_ids=list(range(NCORES)))
    R = res.results
    g = lambda c, n: np.asarray(R[min(c, len(R) - 1)][n], dtype=np.float32)

    def ct_inv(a, caxis, taxis):
        a = np.moveaxis(a, (taxis, caxis), (-2, -1))
        return a.reshape(a.shape[:-2] + (-1,))
    pc = [0, 4]
    y_p = np.stack([g(c, "y_p").reshape(SEQ, 1024) for c in pc], 0)
    y_s = np.concatenate([g(c, "y_s").reshape(NB, TS, 1024) for c in range(NCORES)], 0)

    def lru_h(n, cores):
        return np.concatenate([ct_inv(g(c, n), 1, 2) for c in cores], 1)

    def cv_out(n, cores):
        return np.concatenate([ct_inv(g(c, n), 1, 2) for c in cores], 1)
    p_lru_h = lru_h("o_lru_h_p", pc)
    s_lru_h = lru_h("o_lru_h_s", range(NCORES))
    p_lru_conv = cv_out("o_lru_cv_p", pc)
    s_lru_conv = cv_out("o_lru_cv_s", range(NCORES))
    keep = WT * 128
    p_k = np.stack([g(c, "o_k_p").reshape(L, keep, 6, 64) for c in pc], 1)
    p_v = np.stack([g(c, "o_v_p").reshape(L, keep, 6, 64) for c in pc], 1)
    s_k = np.concatenate([g(c, "o_k_s").reshape(L, NB, TS, 6, 64) for c in range(NCORES)], 1)
    s_v = np.concatenate([g(c, "o_v_s").reshape(L, NB, TS, 6, 64) for c in range(NCORES)], 1)

    def ssd_out(n, cores):
        return np.concatenate([g(c, n).reshape(L, 128, -1, 6, 64).transpose(0, 2, 3, 4, 1) for c in cores], 1)
    p_ssd = ssd_out("o_ssd_p", pc)
    s_ssd = ssd_out("o_ssd_s", range(NCORES))

    def scv(sfx, cores):
        return np.concatenate([np.concatenate([ct_inv(g(c, "o_cvx" + sfx), 1, 2), ct_inv(g(c, "o_cvbc" + sfx), 1, 2)], -1) for c in cores], 1)
    p_ssd_conv = scv("_p", pc)
    s_ssd_conv = scv("_s", range(NCORES))
    outs = (y_p, y_s, p_lru_h, p_lru_conv, p_k, p_v, p_ssd, p_ssd_conv,
            s_lru_h, s_lru_conv, s_k, s_v, s_ssd, s_ssd_conv)
    return tuple(np.ascontiguousarray(o, dtype=np.float32) for o in outs)
```
